# Optimizing a Trainium2 kernel written in Bass

```python
import math
import jax, jax.numpy as jnp
from jax import lax
import numpy as np

D_MODEL = 1024
BATCH = 1
SEQ = 16384
DEPTH = 1

NSA_HEADS = 8
NSA_KV_GROUPS = 2
NSA_HPG = NSA_HEADS // NSA_KV_GROUPS
NSA_HEAD_DIM = 64
CMP_BLOCK = 32
CMP_STRIDE = 16
CMP_HIDDEN = 256
SLC_BLOCK = 64
SLC_COUNT = 16
WINDOW = 512
Q_BLOCK = 128
ROPE_THETA = 500000.0
ROPE_DIM = NSA_HEAD_DIM // 4
RET_HEADS = 4
RET_HEAD_DIM = 128
RET_CHUNK = 128
RET_ROT_BASE = 10000.0
MLP_HIDDEN = 4 * D_MODEL
NORM_EPS = 1e-6
NEG = -1e30
BIG = 1e30

NSA_Q_W = NSA_HEADS * NSA_HEAD_DIM
NSA_KV_W = NSA_KV_GROUPS * NSA_HEAD_DIM
RET_W = RET_HEADS * RET_HEAD_DIM
IN_SIZES = (NSA_Q_W, NSA_KV_W, NSA_KV_W, NSA_KV_W, NSA_KV_W, NSA_KV_W, NSA_KV_W, 3 * NSA_HEADS,
            RET_W, RET_W, RET_W, RET_W, D_MODEL, D_MODEL)
IN_WIDTH = sum(IN_SIZES)

kernel_name = "hybrid_nsa_retention_gated_block"


def rmsnorm(x, g):
    xf = x.astype(jnp.float32)
    y = xf * lax.rsqrt(jnp.mean(xf * xf, axis=-1, keepdims=True) + NORM_EPS) * g.astype(jnp.float32)
    return y.astype(x.dtype)


def rotary(x, pos, rot_dim, base):
    half = rot_dim // 2
    inv = jnp.exp(-math.log(base) * jnp.arange(half, dtype=jnp.float32) * 2.0 / rot_dim)
    ang = pos.astype(jnp.float32)[..., None] * inv
    cos, sin = jnp.cos(ang)[:, :, None, :], jnp.sin(ang)[:, :, None, :]
    xf = x.astype(jnp.float32)
    x1, x2 = xf[..., :half], xf[..., half:rot_dim]
    out = jnp.concatenate([x1 * cos - x2 * sin, x1 * sin + x2 * cos, xf[..., rot_dim:]], axis=-1)
    return out.astype(x.dtype)


def nsa_mixer(q, kc, vc, ks, vs, kw, vw, gates, pe_k, pe_v, ck_w1, ck_w2, cv_w1, cv_w2):
    S = q.shape[0]
    dt = q.dtype
    G, HPG, dh = NSA_KV_GROUPS, NSA_HPG, NSA_HEAD_DIM
    n_cmp = (S - CMP_BLOCK) // CMP_STRIDE + 1
    n_slc = S // SLC_BLOCK
    n_sel = min(SLC_COUNT, n_slc)
    n_qb = S // Q_BLOCK
    scale = dh ** -0.5

    cmp_idx = jnp.arange(n_cmp)[:, None] * CMP_STRIDE + jnp.arange(CMP_BLOCK)[None, :]

    def compress(k, pe, w1, w2):
        blk = k[cmp_idx] + pe[None, :, None, :]
        blk = blk.transpose(0, 2, 1, 3).reshape(n_cmp, G, CMP_BLOCK * dh)
        return jax.nn.silu(blk @ w1) @ w2

    k_cmp = compress(kc, pe_k, ck_w1, ck_w2)
    v_cmp = compress(vc, pe_v, cv_w1, cv_w2)
    cmp_start = cmp_idx[:, 0]
    cmp_end = cmp_idx[:, -1]
    slc_start = jnp.arange(n_slc) * SLC_BLOCK
    overlap = jnp.clip(jnp.minimum(cmp_start[:, None] + CMP_BLOCK, slc_start[None, :] + SLC_BLOCK)
                       - jnp.maximum(cmp_start[:, None], slc_start[None, :]), 0, None)
    overlap = overlap.astype(jnp.float32) / CMP_BLOCK

    ks_b = ks.reshape(n_slc, SLC_BLOCK, G, dh).transpose(2, 0, 1, 3)
    vs_b = vs.reshape(n_slc, SLC_BLOCK, G, dh).transpose(2, 0, 1, 3)
    kw_pad = jnp.pad(kw, ((WINDOW, 0), (0, 0), (0, 0)))
    vw_pad = jnp.pad(vw, ((WINDOW, 0), (0, 0), (0, 0)))

    qg = q.reshape(S, G, HPG, dh)
    gg = jax.nn.sigmoid(gates.astype(jnp.float32)).reshape(S, G, HPG, 3)
    g_idx = jnp.arange(G)[:, None, None]
    blk_j = jnp.arange(n_slc)

    def block(b):
        s0 = b * Q_BLOCK
        qi = lax.dynamic_slice_in_dim(qg, s0, Q_BLOCK, 0)
        gi = lax.dynamic_slice_in_dim(gg, s0, Q_BLOCK, 0)
        t = s0 + jnp.arange(Q_BLOCK)

        sc = jnp.einsum('qghd,cgd->gqhc', qi, k_cmp).astype(jnp.float32) * scale
        valid = cmp_end[None, :] <= t[:, None]
        p_c = jax.nn.softmax(jnp.where(valid[None, :, None, :], sc, NEG), axis=-1)
        p_c = p_c * jnp.any(valid, axis=-1)[None, :, None, None].astype(jnp.float32)
        o_c = jnp.einsum('gqhc,cgd->qghd', p_c.astype(dt), v_cmp)

        imp = jnp.einsum('gqhc,cj->gqj', p_c, overlap)
        cur = t // SLC_BLOCK
        forced = (blk_j[None] == 0) | (blk_j[None] == cur[:, None]) | (blk_j[None] == cur[:, None] - 1)
        future = blk_j[None] > cur[:, None]
        imp = jnp.where(forced[None], BIG, jnp.where(future[None], NEG, imp))
        _, sel = lax.top_k(imp, n_sel)

        k_sel = ks_b[g_idx, sel].reshape(G, Q_BLOCK, n_sel * SLC_BLOCK, dh)
        v_sel = vs_b[g_idx, sel].reshape(G, Q_BLOCK, n_sel * SLC_BLOCK, dh)
        k_pos = (sel[..., None] * SLC_BLOCK + jnp.arange(SLC_BLOCK)).reshape(G, Q_BLOCK, n_sel * SLC_BLOCK)
        ss = jnp.einsum('qghd,gqkd->gqhk', qi, k_sel).astype(jnp.float32) * scale
        smask = (k_pos <= t[None, :, None])[:, :, None, :]
        p_s = jax.nn.softmax(jnp.where(smask, ss, NEG), axis=-1)
        o_s = jnp.einsum('gqhk,gqkd->qghd', p_s.astype(dt), v_sel)

        k_win = lax.dynamic_slice_in_dim(kw_pad, s0, WINDOW + Q_BLOCK, 0)
        v_win = lax.dynamic_slice_in_dim(vw_pad, s0, WINDOW + Q_BLOCK, 0)
        w_pos = s0 - WINDOW + jnp.arange(WINDOW + Q_BLOCK)
        wmask = (w_pos[None] <= t[:, None]) & (w_pos[None] > t[:, None] - WINDOW) & (w_pos[None] >= 0)
        sw = jnp.einsum('qghd,kgd->gqhk', qi, k_win).astype(jnp.float32) * scale
        p_w = jax.nn.softmax(jnp.where(wmask[None, :, None, :], sw, NEG), axis=-1)
        o_w = jnp.einsum('gqhk,kgd->qghd', p_w.astype(dt), v_win)

        o = gi[..., 0:1] * o_c + gi[..., 1:2] * o_s + gi[..., 2:3] * o_w
        return o.reshape(Q_BLOCK, NSA_HEADS * dh).astype(dt)

    out = lax.map(block, jnp.arange(n_qb))
    return out.reshape(S, NSA_HEADS * dh)


def retention(q, k, v, pos):
    B, S = q.shape[0], q.shape[1]
    H, d, C = RET_HEADS, RET_HEAD_DIM, RET_CHUNK
    q = rotary(q, pos, d, RET_ROT_BASE).astype(jnp.float32)
    k = rotary(k, pos, d, RET_ROT_BASE).astype(jnp.float32) * (d ** -0.5)
    v = v.astype(jnp.float32)
    n_c = S // C

    def chunks(a):
        return a.reshape(B, n_c, C, H, d).transpose(1, 0, 3, 2, 4)

    log_g = jnp.log(1.0 - jnp.exp2(-5.0 - jnp.arange(H, dtype=jnp.float32)))
    n = jnp.arange(C, dtype=jnp.float32)
    rel = n[:, None] - n[None, :]
    dmat = jnp.where(rel >= 0, jnp.exp(log_g[:, None, None] * jnp.maximum(rel, 0.0)), 0.0)
    xi = jnp.exp(log_g[:, None] * (n + 1.0))
    zeta = jnp.exp(log_g[:, None] * (C - 1.0 - n))
    chunk_decay = jnp.exp(log_g * C)

    def step(state, inp):
        qc, kc, vc = inp
        inner = jnp.einsum('bhnm,bhme->bhne', jnp.einsum('bhnd,bhmd->bhnm', qc, kc) * dmat, vc)
        cross = jnp.einsum('bhnd,bhde->bhne', qc, state) * xi[None, :, :, None]
        state = chunk_decay[None, :, None, None] * state + jnp.einsum(
            'bhmd,bhme->bhde', kc * zeta[None, :, :, None], vc)
        return state, inner + cross

    state0 = jnp.zeros((B, H, d, d), jnp.float32)
    _, o = lax.scan(step, state0, (chunks(q), chunks(k), chunks(v)))
    o = o.transpose(1, 0, 3, 2, 4).reshape(B, S, H, d)
    return o * lax.rsqrt(jnp.mean(o * o, axis=-1, keepdims=True) + NORM_EPS)


def setup_inputs(seed: int = 0) -> dict:
    key = jax.random.key(seed)
    ks = jax.random.split(key, 20)
    f32 = jnp.float32
    nrm = lambda k, shape, s: jax.random.normal(k, shape, f32) * s
    L = DEPTH
    x = jax.random.normal(ks[0], (BATCH, SEQ, D_MODEL), f32)
    offset = jax.random.randint(ks[1], (BATCH, 1), 0, 4096, dtype=jnp.int32)
    positions = (offset + jnp.arange(SEQ, dtype=jnp.int32)[None, :]).astype(jnp.int32)
    return {
        "x": x,
        "positions": positions,
        "norm_mix": 1.0 + nrm(ks[2], (L, D_MODEL), 0.02),
        "w_in": nrm(ks[3], (L, D_MODEL, IN_WIDTH), D_MODEL ** -0.5),
        "cmp_pos_k": nrm(ks[4], (L, CMP_BLOCK, NSA_HEAD_DIM), 0.1),
        "cmp_pos_v": nrm(ks[5], (L, CMP_BLOCK, NSA_HEAD_DIM), 0.1),
        "cmp_k_w1": nrm(ks[6], (L, CMP_BLOCK * NSA_HEAD_DIM, CMP_HIDDEN), (CMP_BLOCK * NSA_HEAD_DIM) ** -0.5),
        "cmp_k_w2": nrm(ks[7], (L, CMP_HIDDEN, NSA_HEAD_DIM), CMP_HIDDEN ** -0.5),
        "cmp_v_w1": nrm(ks[8], (L, CMP_BLOCK * NSA_HEAD_DIM, CMP_HIDDEN), (CMP_BLOCK * NSA_HEAD_DIM) ** -0.5),
        "cmp_v_w2": nrm(ks[9], (L, CMP_HIDDEN, NSA_HEAD_DIM), CMP_HIDDEN ** -0.5),
        "w_proj_a": nrm(ks[10], (L, NSA_Q_W, D_MODEL), NSA_Q_W ** -0.5),
        "w_proj_b": nrm(ks[11], (L, RET_W, D_MODEL), RET_W ** -0.5),
        "w_out": nrm(ks[12], (L, D_MODEL, D_MODEL), D_MODEL ** -0.5),
        "norm_mlp": 1.0 + nrm(ks[13], (L, D_MODEL), 0.02),
        "w_up": nrm(ks[14], (L, D_MODEL, MLP_HIDDEN), D_MODEL ** -0.5),
        "w_down": nrm(ks[15], (L, MLP_HIDDEN, D_MODEL), MLP_HIDDEN ** -0.5),
        "norm_final": 1.0 + nrm(ks[16], (D_MODEL,), 0.02),
    }


def reference(x, positions, norm_mix, w_in, cmp_pos_k, cmp_pos_v, cmp_k_w1, cmp_k_w2, cmp_v_w1, cmp_v_w2,
              w_proj_a, w_proj_b, w_out, norm_mlp, w_up, w_down, norm_final):
    B, S, _ = x.shape
    offsets = np.cumsum(IN_SIZES)[:-1].tolist()
    nsa_batched = jax.vmap(nsa_mixer, in_axes=(0,) * 8 + (None,) * 6)
    for layer in range(DEPTH):
        h = rmsnorm(x, norm_mix[layer])
        (q, kc, vc, ks, vs, kw, vw, nsa_g, rq, rk, rv, rg, gate_a, gate_b) = jnp.split(
            h @ w_in[layer], offsets, axis=-1)
        heads = lambda a, n_h, d: a.reshape(B, S, n_h, d)
        q = rotary(heads(q, NSA_HEADS, NSA_HEAD_DIM), positions, ROPE_DIM, ROPE_THETA)
        kc = rotary(heads(kc, NSA_KV_GROUPS, NSA_HEAD_DIM), positions, ROPE_DIM, ROPE_THETA)
        ks = rotary(heads(ks, NSA_KV_GROUPS, NSA_HEAD_DIM), positions, ROPE_DIM, ROPE_THETA)
        kw = rotary(heads(kw, NSA_KV_GROUPS, NSA_HEAD_DIM), positions, ROPE_DIM, ROPE_THETA)
        vc = heads(vc, NSA_KV_GROUPS, NSA_HEAD_DIM)
        vs = heads(vs, NSA_KV_GROUPS, NSA_HEAD_DIM)
        vw = heads(vw, NSA_KV_GROUPS, NSA_HEAD_DIM)
        nsa_out = nsa_batched(q, kc, vc, ks, vs, kw, vw, nsa_g.reshape(B, S, NSA_HEADS, 3),
                              cmp_pos_k[layer], cmp_pos_v[layer], cmp_k_w1[layer], cmp_k_w2[layer],
                              cmp_v_w1[layer], cmp_v_w2[layer])
        ret = retention(heads(rq, RET_HEADS, RET_HEAD_DIM), heads(rk, RET_HEADS, RET_HEAD_DIM),
                        heads(rv, RET_HEADS, RET_HEAD_DIM), positions)
        ret_out = (jax.nn.silu(rg.astype(jnp.float32)) * ret.reshape(B, S, RET_W)).astype(x.dtype)
        mix = (jax.nn.sigmoid(gate_a) * (nsa_out @ w_proj_a[layer])
               + jax.nn.sigmoid(gate_b) * (ret_out @ w_proj_b[layer]))
        x = x + mix @ w_out[layer]
        h = rmsnorm(x, norm_mlp[layer])
        x = x + jnp.square(jax.nn.relu(h @ w_up[layer])) @ w_down[layer]
    return rmsnorm(x, norm_final)
```

```python
import math
from contextlib import ExitStack

import numpy as np
import concourse.bass as bass
import concourse.mybir as mybir
from concourse.bass_utils import run_bass_kernel_spmd

F32 = mybir.dt.float32
BF16 = mybir.dt.bfloat16
I32 = mybir.dt.int32
AF = mybir.ActivationFunctionType
ALU = mybir.AluOpType

ENGS = ("pe", "dve", "act", "pool", "sp")
EPOCH = 24000
DMA_RING = {"sp": 8, "pool": 2, "act": 4}
NAMES = {}
CFG = dict(stop=99, p1_tiles=80, p2_tiles=128, p3_i=16, dbg=())

SEQ = 16384
D = 1024
NCORE = 8
NQB = 16
NTG = 128
TWO_PI = float(2 * np.pi)
MAGIC = 12582912.0
NEGBIG = 30000.0
GAM = [1.0 - 2.0 ** (-5.0 - h) for h in range(4)]
LOGG = [math.log(g) for g in GAM]


class _Stop(Exception):
    pass


class Sched:
    def __init__(self, nc):
        self.nc = nc
        self.ops = []
        self.phase = 0

    def next_phase(self):
        self.phase += 1

    def op(self, eng, fn, reads=(), writes=(), dma=False, waw=True):
        self.ops.append(dict(eng=eng, fn=fn, reads=tuple(reads), writes=tuple(writes),
                             dma=dma, waw=waw, deps=set(), signal=False, phase=self.phase))

    def dma(self, eng, out, in_, reads=(), writes=(), waw=True):
        self.op(eng, lambda e: e.dma_start(out=out, in_=in_), reads, writes, dma=True, waw=waw)

    def analyze(self):
        writers, readers = {}, {}
        ops = self.ops
        qcnt, last_on_sem = {}, {}
        for i, o in enumerate(ops):
            if not o["dma"]:
                continue
            n = qcnt.get(o["eng"], 0)
            qcnt[o["eng"]] = n + 1
            ring = DMA_RING[o["eng"]]
            key = ("dmaq", o["eng"], n % ring)
            o["sem"], o["val"] = key, 16 * (n // ring + 1)
            o["signal"] = True
            if key in last_on_sem:
                o["deps"].add(last_on_sem[key])
            last_on_sem[key] = i
        for o in ops:
            if o["fn"] is None:
                o["deps"].update(last_on_sem.values())
        last_eng, last_dma, barrier, cur = {}, {}, set(), 0
        for i, o in enumerate(ops):
            if o["phase"] != cur:
                cur = o["phase"]
                barrier = set(last_eng.values()) | set(last_dma.values())
            o["deps"].update(barrier)
            if o["dma"]:
                last_dma[o["sem"]] = i
            elif o["fn"] is not None:
                last_eng[o["eng"]] = i
        for i, o in enumerate(ops):
            deps = o["deps"]
            for k in o["reads"]:
                deps.update(writers.get(k, ()))
            for k in o["writes"]:
                deps.update(readers.get(k, ()))
                if o["waw"]:
                    deps.update(writers.get(k, ()))
            for k in o["reads"]:
                lst = readers.setdefault(k, [])
                if not o["dma"]:
                    lst[:] = [j for j in lst if ops[j]["dma"] or ops[j]["eng"] != o["eng"]]
                lst.append(i)
            for k in o["writes"]:
                if o["waw"]:
                    writers[k] = [i]
                    readers[k] = []
                else:
                    lst = writers.setdefault(k, [])
                    if not o["dma"]:
                        lst[:] = [j for j in lst if ops[j]["dma"] or ops[j]["eng"] != o["eng"]]
                    lst.append(i)
            deps.discard(i)
            if o["eng"] == "pe" and not o["dma"]:
                for j in [j for j in deps if ops[j]["eng"] == "pe" and not ops[j]["dma"]]:
                    deps.discard(j)
            for j in deps:
                ops[j]["signal"] = True
        cnt = {e: 0 for e in ENGS}
        dcnt = {}
        for o in ops:
            if not o["signal"]:
                continue
            if o["dma"]:
                continue
            else:
                e = o["eng"]
                o["sem"], o["val"] = ("eng", e, cnt[e] // EPOCH), cnt[e] % EPOCH + 1
                cnt[e] += 1
        self.sem_keys = sorted({o["sem"] for o in ops if o["signal"]}, key=str)
        print("sched: ops=%d signals=%s dma=%s nsem=%d" % (len(ops), cnt, qcnt, len(self.sem_keys)), flush=True)

    def emit(self):
        nc = self.nc
        self.analyze()
        ops = self.ops
        with ExitStack() as es:
            sems = {k: es.enter_context(nc.semaphore("s%d" % n)) for n, k in enumerate(self.sem_keys)}
            block = es.enter_context(nc.Block())

            def run_engine(engname, e):
                waited = {}
                for o in ops:
                    if o["eng"] != engname:
                        continue
                    need = {}
                    for j in o["deps"]:
                        k, v = ops[j]["sem"], ops[j]["val"]
                        if waited.get(k, 0) < v and need.get(k, 0) < v:
                            need[k] = v
                    for k, v in need.items():
                        e.wait_ge(sems[k], v)
                        waited[k] = v
                    if o["fn"] is None:
                        continue
                    inst = o["fn"](e)
                    if o["signal"]:
                        inst.then_inc(sems[o["sem"]], 16 if o["dma"] else 1)

            @block.sync
            def _(e):
                run_engine("sp", e)

            @block.scalar
            def _(e):
                run_engine("act", e)

            @block.vector
            def _(e):
                run_engine("dve", e)

            @block.gpsimd
            def _(e):
                run_engine("pool", e)

            @block.tensor
            def _(e):
                run_engine("pe", e)


C_Q, C_KC, C_VC, C_KS, C_VS, C_KW, C_VW, C_NG, C_RQ, C_RK, C_RV, C_RG, C_GA, C_GB = (
    0, 512, 640, 768, 896, 1024, 1152, 1280, 1304, 1816, 2328, 2840, 3352, 4376)


def build_nc():
    nc = bass.Bass("TRN2", target_bir_lowering=False)

    def din(name, shape, dt=F32):
        return nc.dram_tensor(name, list(shape), dt, kind="ExternalInput").ap()

    def dscr(name, shape, dt):
        kind = "ExternalOutput" if name in CFG["dbg"] else "Internal"
        return nc.dram_tensor(name, list(shape), dt, kind=kind).ap()

    xg = din("xg", [SEQ, D])
    xo = din("xo", [80 * 128, D])
    posall = din("posall", [128, 208], I32)
    inv = din("inv", [128, 72])
    w_in = din("w_in", [D, 5400])
    ckw1 = din("ckw1", [2048, 256])
    ckw2 = din("ckw2", [256, 64])
    cvw1 = din("cvw1", [2048, 256])
    cvw2 = din("cvw2", [256, 64])
    pekT = din("pekT", [64, 32])
    pevT = din("pevT", [64, 32])
    wpa = din("wpa", [512, D])
    wpb = din("wpb", [512, D])
    wout = din("wout", [D, D])
    wup = din("wup", [D, 4096])
    wdown = din("wdown", [4096, D])
    gmix = din("gmix", [128, D])
    gmlp = din("gmlp", [128, D])
    gfin = din("gfin", [128, D])
    biasimp = din("biasimp", [16, 128, 256])
    cmpmask = din("cmpmask", [16, 128, 256])
    diagmask = din("diagmask", [128, 8 * 128])
    winmask0 = din("winmask0", [128, 5 * 128])
    winmaskS = din("winmaskS", [128, 5 * 128])
    ohd = din("oh", [128, 8])
    ovd = din("ov", [128, 8 * 256])
    apat = din("apat", [64, 4096])
    identd = din("ident", [128, 128])
    zetad = din("zeta", [128, 4])
    xiTd = din("xiT", [128, 512])
    dmatTd = din("dmatT", [128, 512])
    y = nc.dram_tensor("y", [16, 128, D], F32, kind="ExternalOutput").ap()

    ROT = dscr("ROT", [128, 208, 144], F32)
    QT = dscr("QT", [16, 64, 1024], BF16)
    GN = dscr("GN", [16, 128, 24], F32)
    RQT = dscr("RQT", [16, 128, 512], BF16)
    RKT = dscr("RKT", [16, 128, 512], BF16)
    RV = dscr("RV", [16, 128, 512], BF16)
    SG = dscr("SG", [16, 128, 512], F32)
    KWT = dscr("KWT", [16, 64, 1280], BF16)
    VW = dscr("VW", [16, 128, 640], BF16)
    HT = dscr("HT", [16, 128, 1024], BF16)
    KCT = dscr("KCT", [128, 16512], BF16)
    VCT = dscr("VCT", [128, 16512], BF16)
    NSAT = dscr("NSAT", [16, 128, 512], BF16)
    RETT = dscr("RETT", [16, 128, 512], BF16)
    XMID = dscr("XMID", [16, 128, 1024], F32)
    H2T = dscr("H2T", [16, 128, 1024], BF16)

    s = Sched(nc)

    def ACT(out, in_, func, R, W, waw=True, **kw):
        s.op("act", lambda e: e.activation(out=out, in_=in_, func=func, **kw), R, W, waw=waw)

    def TT(eng, out, a, b, op, R, W, waw=True):
        s.op(eng, lambda e: e.tensor_tensor(out=out, in0=a, in1=b, op=op), R, W, waw=waw)

    def TS(eng, out, a, s1, s2, op0, op1, R, W, waw=True):
        if op1 is None:
            s.op(eng, lambda e: e.tensor_scalar(out=out, in0=a, scalar1=s1, scalar2=None, op0=op0), R, W, waw=waw)
        else:
            s.op(eng, lambda e: e.tensor_scalar(out=out, in0=a, scalar1=s1, scalar2=s2, op0=op0, op1=op1), R, W, waw=waw)

    def STT(eng, out, in0, scalar, in1, op0, op1, R, W, waw=True):
        s.op(eng, lambda e: e.scalar_tensor_tensor(out=out, in0=in0, scalar=scalar, in1=in1, op0=op0, op1=op1),
             R, W, waw=waw)

    def CP(eng, out, in_, R, W, waw=True):
        if eng == "act":
            ACT(out, in_, AF.Copy, R, W, waw=waw)
        else:
            s.op(eng, lambda e: e.tensor_copy(out=out, in_=in_), R, W, waw=waw)

    def RECIP(out, in_, R, W):
        s.op("dve", lambda e: e.reciprocal(out=out, in_=in_), R, W)

    def MM(out, lhsT, rhs, start, stop, R, W, skip=False):
        s.op("pe", lambda e: e.matmul(out, lhsT=lhsT, rhs=rhs, start=start, stop=stop, skip_group_check=skip),
             R, W, waw=False)

    def TR(out, in_, ident, R, W):
        s.op("pe", lambda e: e.transpose(out=out, in_=in_, identity=ident), R, W, waw=False)

    def MEMSET(eng, ap, val, W, waw=True):
        s.op(eng, lambda e: e.memset(ap, val), (), W, waw=waw)

    def wload(dst, src2d, key):
        K = dst.shape[1]
        for k in range(K):
            s.dma("pool", dst[:, k, :], src2d[k * 128:(k + 1) * 128, :], writes=[key], waw=False)

    def bc(ap, shape):
        return ap.to_broadcast(list(shape))

    top = ExitStack()
    try:
        uniq = [0]

        def T(es, name, shape, dt):
            uniq[0] += 1
            NAMES[name] = "sb%d_%s" % (uniq[0], name)
            return es.enter_context(nc.sbuf_tensor("sb%d_%s" % (uniq[0], name), list(shape), dt))

        def P(es, name, shape, dt):
            uniq[0] += 1
            return es.enter_context(nc.psum_tensor("ps%d_%s" % (uniq[0], name), list(shape), dt))

        identf = T(top, "identf", [128, 128], F32)
        identb = T(top, "identb", [128, 128], BF16)
        epsb = T(top, "epsb", [128, 1], F32)
        s.dma("sp", identf[:], identd, writes=["identf"])
        CP("dve", identb[:], identf[:], ["identf"], ["identb"])
        MEMSET("pool", epsb[:], 1e-6, ["epsb"])

        def rmsnorm(es_tmp, xt, kx, grep, kg, hb, khb, tag):
            junk, ss, rstd = es_tmp
            ACT(junk[:], xt, AF.Square, [kx], ["junk" + tag, "ss" + tag], accum_out=ss[:])
            ACT(rstd[:], ss[:], AF.Ln, ["ss" + tag, "epsb"], ["rstd" + tag], scale=1.0 / D, bias=epsb[:])
            ACT(rstd[:], rstd[:], AF.Exp, ["rstd" + tag], ["rstd" + tag], scale=-0.5)
            STT("dve", hb, xt, rstd[:], grep, ALU.mult, ALU.mult, [kx, "rstd" + tag, kg], [khb])

        def rotary(eng, x1, x2, cos, sin, o1, o2, tmp, R, W, tag):
            t1, t2, t3, t4 = tmp
            kt = ["rt%d%s" % (n, tag) for n in range(4)]
            TT(eng, t1, x1, cos, ALU.mult, R, [kt[0]])
            TT(eng, t2, x2, sin, ALU.mult, R, [kt[1]])
            TT(eng, t3, x1, sin, ALU.mult, R, [kt[2]])
            TT(eng, t4, x2, cos, ALU.mult, R, [kt[3]])
            TT(eng, o1, t1, t2, ALU.subtract, [kt[0], kt[1]], W, waw=False)
            TT(eng, o2, t4, t3, ALU.add, [kt[2], kt[3]], W, waw=False)

        with ExitStack() as es:
            posi = T(es, "posi", [128, 208], I32)
            posf = T(es, "posf", [128, 208], F32)
            invs = T(es, "invs", [128, 72], F32)
            s.dma("sp", posi[:], posall, writes=["posi"])
            s.dma("sp", invs[:], inv, writes=["invs"])
            CP("dve", posf[:], posi[:], ["posi"], ["posf"])
            CH = 16
            ang = [T(es, "ang%d" % n, [128, CH, 72], F32) for n in range(2)]
            a2 = T(es, "a2", [128, CH, 72], F32)
            kf = T(es, "kf", [128, CH, 72], F32)
            rr = T(es, "rr", [128, CH, 72], F32)
            rot = [T(es, "rotc%d" % n, [128, CH, 144], F32) for n in range(2)]
            for ci in range(208 // CH):
                pr = ci % 2
                c0 = ci * CH
                TT("dve", ang[pr][:], bc(invs[:].unsqueeze(1), [128, CH, 72]),
                   bc(posf[:, c0:c0 + CH].unsqueeze(2), [128, CH, 72]), ALU.mult, ["invs", "posf"], ["ang%d" % pr])
                for which in range(2):
                    if which == 0:
                        src, ksrc = ang[pr], "ang%d" % pr
                    else:
                        TS("dve", a2[:], ang[pr][:], float(np.pi / 2), None, ALU.add, None, ["ang%d" % pr], ["a2"])
                        src, ksrc = a2, "a2"
                    TS("dve", kf[:], src[:], 1.0 / TWO_PI, MAGIC, ALU.mult, ALU.add, [ksrc], ["kf"])
                    TS("dve", kf[:], kf[:], -MAGIC, None, ALU.add, None, ["kf"], ["kf"])
                    STT("dve", rr[:], kf[:], -TWO_PI, src[:], ALU.mult, ALU.add, ["kf", ksrc], ["rr"])
                    TS("dve", rr[:], rr[:], 3.14159, -3.14159, ALU.min, ALU.max, ["rr"], ["rr"])
                    ACT(rot[pr][:, :, which * 72:(which + 1) * 72], rr[:], AF.Sin, ["rr"], ["rot%d" % pr], waw=False)
                s.dma("sp", ROT[:, c0:c0 + CH, :], rot[pr][:], reads=["rot%d" % pr], writes=["ROT"], waw=False)

        if CFG['stop'] < 1:
            raise _Stop()
        s.next_phase()
        with ExitStack() as es:
            Wq = T(es, "Wq", [128, 8, 512], BF16)
            Wng = T(es, "Wng", [128, 8, 24], BF16)
            Wr = T(es, "Wr", [128, 8, 2048], BF16)
            Ww = T(es, "Ww", [128, 8, 256], BF16)
            grep = T(es, "grep1", [128, D], F32)
            s.dma("sp", grep[:], gmix, writes=["grep"])
            wload(Ww, w_in[:, C_KW:C_KW + 256], "Ww")
            wload(Wq, w_in[:, C_Q:C_Q + 512], "Wq")
            wload(Wng, w_in[:, C_NG:C_NG + 24], "Wng")
            wload(Wr, w_in[:, C_RQ:C_RQ + 2048], "Wr")
            xt = [T(es, "xt%d" % n, [128, D], F32) for n in range(2)]
            rot = [T(es, "rot%d" % n, [128, 144], F32) for n in range(2)]
            hb = [T(es, "hb%d" % n, [128, D], BF16) for n in range(2)]
            hT = [T(es, "hT%d" % n, [128, 8, 128], BF16) for n in range(2)]
            junk = T(es, "junk", [128, D], F32)
            ss = T(es, "ss", [128, 1], F32)
            rstd = T(es, "rstd", [128, 1], F32)
            kwf = T(es, "kwf", [128, 128], F32)
            kwb = T(es, "kwb", [128, 128], BF16)
            vwb = [T(es, "vwb%d" % n, [128, 128], BF16) for n in range(2)]
            kwT = [T(es, "kwT%d" % n, [64, 2, 128], BF16) for n in range(2)]
            rtmp_s = [T(es, "rts%d" % n, [128, 8, 8], F32) for n in range(4)]
            rtmp_b = [T(es, "rtb%d" % n, [128, 4, 64], F32) for n in range(4)]
            qf = T(es, "qf", [128, 512], F32)
            qb = T(es, "qb", [128, 512], BF16)
            qT = T(es, "qT", [64, 1024], BF16)
            ge = T(es, "ge", [128, 24], F32)
            rf = T(es, "rf", [128, 512], F32)
            rqb = T(es, "rqb", [128, 512], BF16)
            rkb = T(es, "rkb", [128, 512], BF16)
            rvb = T(es, "rvb", [128, 512], BF16)
            rTs = [T(es, "rTs%d" % n, [128, 512], BF16) for n in range(2)]
            sge = T(es, "sge", [128, 512], F32)
            sgo = T(es, "sgo", [128, 512], F32)
            pT = [P(es, "pT%d" % n, [128, 1024], BF16) for n in range(2)]
            pT2 = P(es, "pT2", [128, 1024], BF16)
            pW = P(es, "pW", [128, 512], F32)
            pQ = P(es, "pQ", [128, 512], F32)
            pR = P(es, "pR", [128, 2, 512], F32)

            def p1_load(t):
                pr = t % 2
                s.dma("sp", xt[pr][:], xo[t * 128:(t + 1) * 128, :], writes=["xt%d" % pr])
                s.dma("sp", rot[pr][:], ROT[:, 128 + t, :], reads=["ROT"], writes=["rot%d" % pr])

            p1_load(0)
            for t in range(CFG['p1_tiles']):
                i, j = divmod(t, 5)
                pr = t % 2
                if t + 1 < CFG['p1_tiles']:
                    p1_load(t + 1)
                kx, kr, khb, khT, kpT = "xt%d" % pr, "rot%d" % pr, "hb%d" % pr, "hT%d" % pr, "pT%d" % pr
                rmsnorm((junk, ss, rstd), xt[pr][:], kx, grep[:], "grep", hb[pr][:], khb, "1")
                for k in range(8):
                    TR(pT[pr][:, k * 128:(k + 1) * 128], hb[pr][:, k * 128:(k + 1) * 128], identb[:], [khb, "identb"], [kpT])
                CP("act", hT[pr][:].rearrange("p k t -> p (k t)"), pT[pr][:], [kpT], [khT])
                for k in range(8):
                    MM(pW[:, 0:256], hT[pr][:, k, :], Ww[:, k, :], k == 0, k == 7, [khT, "Ww"], ["pW"])
                CP("act", vwb[pr][:], pW[:, 128:256], ["pW"], ["vwb%d" % pr])
                s.dma("sp", VW[i][:, j * 128:(j + 1) * 128], vwb[pr][:], reads=["vwb%d" % pr], writes=["VW%d" % i], waw=False)
                CP("act", kwf[:], pW[:, 0:128], ["pW"], ["kwf"])
                CP("pool", kwb[:], kwf[:], ["kwf"], ["kwb"])
                kv = kwf[:].rearrange("p (g d) -> p g d", g=2)
                ko = kwb[:].rearrange("p (g d) -> p g d", g=2)
                rotary("dve", kv[:, :, 0:8], kv[:, :, 8:16], bc(rot[pr][:, 72:80].unsqueeze(1), [128, 2, 8]),
                       bc(rot[pr][:, 0:8].unsqueeze(1), [128, 2, 8]), ko[:, :, 0:8], ko[:, :, 8:16],
                       [r[:, 0:2, :] for r in rtmp_s], ["kwf", kr, "kwb"], ["kwb"], "s")
                for g in range(2):
                    TR(pT2[0:64, g * 128:(g + 1) * 128], kwb[:, g * 64:(g + 1) * 64], identb[:], ["kwb", "identb"], ["pT2"])
                CP("dve", kwT[pr][:].rearrange("p g t -> p (g t)"), pT2[0:64, 0:256], ["pT2"], ["kwT%d" % pr])
                s.dma("sp", KWT[i].rearrange("p (g t) -> p g t", g=2)[:, :, j * 128:(j + 1) * 128], kwT[pr][:],
                      reads=["kwT%d" % pr], writes=["KWT%d" % i], waw=False)
                if j != 4:
                    continue
                s.dma("sp", HT[i], hT[pr][:].rearrange("p k t -> p (k t)"), reads=[khT], writes=["HT%d" % i])
                for k in range(8):
                    MM(pQ[:], hT[pr][:, k, :], Wq[:, k, :], k == 0, k == 7, [khT, "Wq"], ["pQ"])
                for k in range(8):
                    MM(pW[:, 256:280], hT[pr][:, k, :], Wng[:, k, :], k == 0, k == 7, [khT, "Wng"], ["pW"])
                ACT(qf[:], pQ[:], AF.Copy, ["pQ"], ["qf"], scale=0.125)
                CP("pool", qb[:], qf[:], ["qf"], ["qb"])
                qv = qf[:].rearrange("p (h d) -> p h d", h=8)
                qo = qb[:].rearrange("p (h d) -> p h d", h=8)
                rotary("dve", qv[:, :, 0:8], qv[:, :, 8:16], bc(rot[pr][:, 72:80].unsqueeze(1), [128, 8, 8]),
                       bc(rot[pr][:, 0:8].unsqueeze(1), [128, 8, 8]), qo[:, :, 0:8], qo[:, :, 8:16],
                       [r[:] for r in rtmp_s], ["qf", kr, "qb"], ["qb"], "s")
                for h in range(8):
                    TR(pT2[0:64, h * 128:(h + 1) * 128], qb[:, h * 64:(h + 1) * 64], identb[:], ["qb", "identb"], ["pT2"])
                CP("dve", qT[:], pT2[0:64, :], ["pT2"], ["qT"])
                s.dma("sp", QT[i], qT[:], reads=["qT"], writes=["QT%d" % i])
                ACT(ge[:], pW[:, 256:280], AF.Exp, ["pW"], ["ge"], scale=-1.0)
                TS("dve", ge[:], ge[:], 1.0, None, ALU.add, None, ["ge"], ["ge"])
                RECIP(ge[:], ge[:], ["ge"], ["ge"])
                s.dma("sp", GN[i], ge[:], reads=["ge"], writes=["GN%d" % i])
                for half in range(2):
                    for b in range(2):
                        c0 = (half * 2 + b) * 512
                        for k in range(8):
                            MM(pR[:, b, :], hT[pr][:, k, :], Wr[:, k, c0:c0 + 512], k == 0, k == 7, [khT, "Wr"], ["pR%d" % b])
                    if half == 0:
                        for b, (dst, dsc, kd) in enumerate(((rqb, RQT, "RQT"), (rkb, RKT, "RKT"))):
                            CP("act", rf[:], pR[:, b, :], ["pR%d" % b], ["rf"])
                            rv_ = rf[:].rearrange("p (h d) -> p h d", h=4)
                            ro = dst[:].rearrange("p (h d) -> p h d", h=4)
                            rotary("pool", rv_[:, :, 0:64], rv_[:, :, 64:128], bc(rot[pr][:, 80:144].unsqueeze(1), [128, 4, 64]),
                                   bc(rot[pr][:, 8:72].unsqueeze(1), [128, 4, 64]), ro[:, :, 0:64], ro[:, :, 64:128],
                                   [r[:] for r in rtmp_b], ["rf", kr, "rdst%d" % b], ["rdst%d" % b], "b")
                            for h in range(4):
                                TR(pT2[:, h * 128:(h + 1) * 128], dst[:, h * 128:(h + 1) * 128], identb[:], ["rdst%d" % b, "identb"], ["pT2"])
                            CP("dve", rTs[b][:], pT2[:, 0:512], ["pT2"], ["rTs%d" % b])
                            s.dma("sp", dsc[i], rTs[b][:], reads=["rTs%d" % b], writes=["%s%d" % (kd, i)])
                    else:
                        CP("act", rvb[:], pR[:, 0, :], ["pR0"], ["rvb"])
                        s.dma("sp", RV[i], rvb[:], reads=["rvb"], writes=["RV%d" % i])
                        ACT(sge[:], pR[:, 1, :], AF.Exp, ["pR1"], ["sge"], scale=-1.0)
                        TS("dve", sge[:], sge[:], 1.0, None, ALU.add, None, ["sge"], ["sge"])
                        RECIP(sge[:], sge[:], ["sge"], ["sge"])
                        TT("dve", sgo[:], pR[:, 1, :], sge[:], ALU.mult, ["pR1", "sge"], ["sgo"])
                        s.dma("sp", SG[i], sgo[:], reads=["sgo"], writes=["SG%d" % i])

        if CFG['stop'] < 2:
            raise _Stop()
        s.next_phase()
        with ExitStack() as mid:
            KsTa = [T(mid, "KsTa%d" % g, [128, SEQ], BF16) for g in range(2)]
            Vsa = T(mid, "Vsa", [128, 128, 2, 65], BF16)
            Sown = T(mid, "Sown", [128, 16, 512], BF16)
            kcmpT = T(mid, "kcmpT", [64, 2, 1024], BF16)
            vcmpa = T(mid, "vcmpa", [128, 8, 2, 65], BF16)
            for g in range(2):
                for r8 in range(8):
                    s.dma("pool", KsTa[g][64:128, r8 * 2048:(r8 + 1) * 2048], apat[:, (r8 % 2) * 2048:(r8 % 2 + 1) * 2048],
                          writes=["KsTa%d" % g], waw=False)
            MEMSET("pool", Vsa[:, :, :, 64:65], 1.0, ["Vsa"], waw=False)
            MEMSET("pool", vcmpa[:, :, :, 64:65], 1.0, ["vcmpa"], waw=False)

            with ExitStack() as es:
                W1g = T(es, "W1g", [128, 8, 512], BF16)
                W2g = T(es, "W2g", [128, 8, 1024], BF16)
                grep = T(es, "grep2", [128, D], F32)
                zeta = T(es, "zeta", [128, 4], F32)
                oh = T(es, "oh", [128, 8], F32)
                s.dma("sp", grep[:], gmix, writes=["grep2"])
                s.dma("sp", zeta[:], zetad, writes=["zeta"])
                s.dma("sp", oh[:], ohd, writes=["oh"])
                wload(W1g, w_in[:, C_KC:C_KC + 512], "W1g")
                wload(W2g, w_in[:, C_RK:C_RK + 1024], "W2g")
                xt = [T(es, "gxt%d" % n, [128, D], F32) for n in range(2)]
                rot = [T(es, "grot%d" % n, [128, 144], F32) for n in range(2)]
                hb = [T(es, "ghb%d" % n, [128, D], BF16) for n in range(2)]
                hT = [T(es, "ghT%d" % n, [128, 8, 128], BF16) for n in range(2)]
                junk = T(es, "gjunk", [128, D], F32)
                ss = T(es, "gss", [128, 1], F32)
                rstd = T(es, "grstd", [128, 1], F32)
                abf = T(es, "abf", [128, 384], BF16)
                af = T(es, "af", [128, 384], F32)
                kvst = [T(es, "kvst%d" % n, [128, 256], BF16) for n in range(2)]
                rkf = T(es, "rkf", [128, 512], F32)
                rkb = T(es, "grkb", [128, 512], BF16)
                rvz = T(es, "rvz", [128, 512], BF16)
                rtmp_s = [T(es, "grts%d" % n, [128, 2, 2, 8], F32) for n in range(4)]
                rtmp_b = [T(es, "grtb%d" % n, [128, 4, 64], F32) for n in range(4)]
                Sst = T(es, "Sst", [128, 512], F32)
                acc = T(es, "acc", [128, 512], F32)
                zpad = T(es, "zpad", [128, 16], BF16)
                pT = [P(es, "gpT%d" % n, [128, 1024], BF16) for n in range(2)]
                pT2 = P(es, "gpT2", [128, 1024], BF16)
                pA = P(es, "pA", [128, 512], F32)
                pB = P(es, "pB", [128, 2, 512], F32)
                pL = P(es, "pL", [128, 512], F32)
                MEMSET("pool", Sst[:], 0.0, ["Sst"])
                MEMSET("pool", zpad[:], 0.0, ["zpad"])
                s.dma("sp", KCT[:, SEQ:SEQ + 16], zpad[:], reads=["zpad"], writes=["KCT"], waw=False)
                s.dma("sp", VCT[:, SEQ:SEQ + 16], zpad[:], reads=["zpad"], writes=["VCT"], waw=False)

                def p2_load(t):
                    pr = t % 2
                    s.dma("sp", xt[pr][:], xg[t * 128:(t + 1) * 128, :], writes=["gxt%d" % pr])
                    s.dma("sp", rot[pr][:], ROT[:, t, :], reads=["ROT"], writes=["grot%d" % pr])

                p2_load(0)
                for t in range(CFG['p2_tiles']):
                    pr = t % 2
                    if t + 1 < CFG['p2_tiles']:
                        p2_load(t + 1)
                    kx, kr, khb, khT, kpT = "gxt%d" % pr, "grot%d" % pr, "ghb%d" % pr, "ghT%d" % pr, "gpT%d" % pr
                    rmsnorm((junk, ss, rstd), xt[pr][:], kx, grep[:], "grep2", hb[pr][:], khb, "2")
                    for k in range(8):
                        TR(pT[pr][:, k * 128:(k + 1) * 128], hb[pr][:, k * 128:(k + 1) * 128], identb[:], [khb, "identb"], [kpT])
                    CP("act", hT[pr][:].rearrange("p k t -> p (k t)"), pT[pr][:], [kpT], [khT])
                    if CFG.get('p2_cut', 99) < 2:
                        continue
                    for k in range(8):
                        MM(pA[:], hT[pr][:, k, :], W1g[:, k, :], k == 0, k == 7, [khT, "W1g"], ["pA"])
                    for b in range(2):
                        for k in range(8):
                            MM(pB[:, b, :], hT[pr][:, k, :], W2g[:, k, b * 512:(b + 1) * 512], k == 0, k == 7, [khT, "W2g"], ["pB%d" % b])
                    if CFG.get('p2_cut', 99) < 3:
                        continue
                    CP("act", abf[:], pA[:, 0:384], ["pA"], ["abf"])
                    CP("act", Vsa[:, t, :, 0:64], pA[:, 384:512].rearrange("p (g d) -> p g d", g=2), ["pA"], ["Vsa"], waw=False)
                    if CFG.get('p2_cut', 99) < 4:
                        continue
                    CP("act", af[:], pA[:, 0:384], ["pA"], ["af"])
                    for a_ in (0, 2):
                        av = af[:, a_ * 128:(a_ + 1) * 128].rearrange("p (g d) -> p g d", g=2)
                        ao = abf[:, a_ * 128:(a_ + 1) * 128].rearrange("p (g d) -> p g d", g=2)
                        rotary("dve", av[:, :, 0:8], av[:, :, 8:16], bc(rot[pr][:, 72:80].unsqueeze(1), [128, 2, 8]),
                               bc(rot[pr][:, 0:8].unsqueeze(1), [128, 2, 8]), ao[:, :, 0:8], ao[:, :, 8:16],
                               [r[:, 0, :, :] for r in rtmp_s], ["af", kr, "abf"], ["abf"], "gs")
                    if CFG.get('p2_cut', 99) < 5:
                        continue
                    TR(pT2[:, 0:128], abf[:, 0:128], identb[:], ["abf", "identb"], ["gpT2"])
                    TR(pT2[:, 128:256], abf[:, 128:256], identb[:], ["abf", "identb"], ["gpT2"])
                    TR(pT2[0:64, 256:384], abf[:, 256:320], identb[:], ["abf", "identb"], ["gpT2"])
                    TR(pT2[0:64, 384:512], abf[:, 320:384], identb[:], ["abf", "identb"], ["gpT2"])
                    if CFG.get('p2_sub', 99) < 1:
                        continue
                    CP("act", kvst[pr][:], pT2[:, 0:256], ["gpT2"], ["kvst%d" % pr])
                    if CFG.get('p2_sub', 99) < 2:
                        continue
                    s.dma("sp", KCT[:, t * 128:(t + 1) * 128], kvst[pr][:, 0:128], reads=["kvst%d" % pr], writes=["KCT"], waw=False)
                    s.dma("sp", VCT[:, t * 128:(t + 1) * 128], kvst[pr][:, 128:256], reads=["kvst%d" % pr], writes=["VCT"], waw=False)
                    if CFG.get('p2_sub', 99) < 3:
                        continue
                    CP("act", KsTa[0][0:64, t * 128:(t + 1) * 128], pT2[0:64, 256:384], ["gpT2"], ["KsTa0"], waw=False)
                    CP("act", KsTa[1][0:64, t * 128:(t + 1) * 128], pT2[0:64, 384:512], ["gpT2"], ["KsTa1"], waw=False)
                    if CFG.get('p2_cut', 99) < 6:
                        continue
                    CP("act", rkf[:], pB[:, 0, :], ["pB0"], ["rkf"])
                    rv_ = rkf[:].rearrange("p (h d) -> p h d", h=4)
                    ro = rkb[:].rearrange("p (h d) -> p h d", h=4)
                    rotary("pool", rv_[:, :, 0:64], rv_[:, :, 64:128], bc(rot[pr][:, 80:144].unsqueeze(1), [128, 4, 64]),
                           bc(rot[pr][:, 8:72].unsqueeze(1), [128, 4, 64]), ro[:, :, 0:64], ro[:, :, 64:128],
                           [r[:] for r in rtmp_b], ["rkf", kr, "grkb"], ["grkb"], "gb")
                    for h in range(4):
                        TS("dve", rvz[:, h * 128:(h + 1) * 128], pB[:, 1, h * 128:(h + 1) * 128], zeta[:, h:h + 1], None, ALU.mult, None,
                           ["pB1", "zeta"], ["rvz"], waw=(h == 0))
                    if CFG.get('p2_cut', 99) < 7:
                        continue
                    m = t % 8
                    if m == 0:
                        TS("dve", acc[:], Sst[:], oh[:, 0:1], None, ALU.mult, None, ["Sst", "oh"], ["acc"])
                    else:
                        STT("dve", acc[:], Sst[:], oh[:, m:m + 1], acc[:], ALU.mult, ALU.add, ["Sst", "oh", "acc"], ["acc"])
                    if m == 7:
                        CP("act", Sown[:, t // 8, :], acc[:], ["acc"], ["Sown"], waw=False)
                    if CFG.get('p2_cut', 99) < 8:
                        continue
                    for h in range(4):
                        MM(pL[:, h * 128:(h + 1) * 128], rkb[:, h * 128:(h + 1) * 128], rvz[:, h * 128:(h + 1) * 128],
                           True, True, ["grkb", "rvz"], ["pL"])
                    for h in range(4):
                        STT("dve", Sst[:, h * 128:(h + 1) * 128], Sst[:, h * 128:(h + 1) * 128], float(GAM[h] ** 128),
                            pL[:, h * 128:(h + 1) * 128], ALU.mult, ALU.add, ["Sst", "pL"], ["Sst"])

            if CFG['stop'] < 3:
                raise _Stop()
            s.next_phase()
            with ExitStack() as es:
                w1 = [T(es, "w1_%d" % n, [128, 32, 256], BF16) for n in range(2)]
                w2 = [T(es, "w2_%d" % n, [128, 2, 64], BF16) for n in range(2)]
                peT = [T(es, "peT%d" % n, [64, 32], BF16) for n in range(2)]
                bia = T(es, "bia", [128, 4], F32)
                nbia = T(es, "nbia", [128, 4], F32)
                src = [T(es, "csrc%d" % n, [128, 4112], BF16) for n in range(2)]
                hid = T(es, "hid", [128, 2, 2, 256], BF16)
                ex = T(es, "cex", [128, 256], F32)
                pC = [P(es, "pC%d" % n, [128, 512], F32) for n in range(2)]
                pK = P(es, "pK", [128, 512], F32)
                pBi = P(es, "pBi", [128, 512], F32)
                for kind, (wd1, wd2, ped) in enumerate(((ckw1, ckw2, pekT), (cvw1, cvw2, pevT))):
                    w1v = wd1.rearrange("(l d) h -> d l h", d=64)
                    for half in range(2):
                        for l4 in range(4):
                            s.dma("pool", w1[kind][half * 64:(half + 1) * 64, l4 * 8:(l4 + 1) * 8, :], w1v[:, l4 * 8:(l4 + 1) * 8, :],
                                  writes=["w1_%d" % kind], waw=False)
                    s.dma("pool", w2[kind][:], wd2.rearrange("(c p) d -> p c d", p=128), writes=["w2_%d" % kind])
                    s.dma("pool", peT[kind][:], ped, writes=["peT%d" % kind])
                    for hc in range(2):
                        for l in range(32):
                            MM(pBi[:, kind * 2 + hc:kind * 2 + hc + 1], w1[kind][0:64, l, hc * 128:(hc + 1) * 128], peT[kind][:, l:l + 1],
                               l == 0 and kind == 0 and hc == 0, l == 31, ["w1_%d" % kind, "peT%d" % kind], ["pBi"], skip=True)
                CP("dve", bia[:], pBi[:, 0:4], ["pBi"], ["bia"])
                TS("dve", nbia[:], bia[:], -1.0, None, ALU.mult, None, ["bia"], ["nbia"])
                cnt = 0
                for qd in range(CFG.get('p2b_q', 4)):
                    for kind, SRC in enumerate((KCT, VCT)):
                        s.dma("sp", src[kind][:], SRC[:, qd * 4096:qd * 4096 + 4112], reads=["KCT", "VCT"], writes=["csrc%d" % kind])
                        for g in range(2):
                            for hc in range(2):
                                pc = pC[cnt % 2]
                                kpc = "pC%d" % (cnt % 2)
                                cnt += 1
                                for l in range(32):
                                    MM(pc[:, 0:256], w1[kind][g * 64:(g + 1) * 64, l, hc * 128:(hc + 1) * 128],
                                       src[kind][g * 64:(g + 1) * 64, l:l + 4081:16], l == 0, l == 31,
                                       ["w1_%d" % kind, "csrc%d" % kind], [kpc])
                                col = kind * 2 + hc
                                ACT(ex[:], pc[:, 0:256], AF.Exp, [kpc, "nbia"], ["cex"], scale=-1.0, bias=nbia[:, col:col + 1])
                                TS("dve", ex[:], ex[:], 1.0, None, ALU.add, None, ["cex"], ["cex"])
                                RECIP(ex[:], ex[:], ["cex"], ["cex"])
                                STT("dve", hid[:, g, hc, :], pc[:, 0:256], bia[:, col:col + 1], ex[:], ALU.add, ALU.mult,
                                    [kpc, "bia", "cex"], ["hid"], waw=False)
                        if kind == 0:
                            for g in range(2):
                                for hc in range(2):
                                    MM(pK[0:64, g * 256:(g + 1) * 256], w2[0][:, hc, :], hid[:, g, hc, :], hc == 0 and g == 0, hc == 1,
                                       ["w2_0", "hid"], ["pK"], skip=True)
                            CP("act", kcmpT[:, :, qd * 256:(qd + 1) * 256], pK[0:64, :].rearrange("p (g b) -> p g b", g=2),
                               ["pK"], ["kcmpT"], waw=False)
                        else:
                            for g in range(2):
                                for bch in range(2):
                                    for hc in range(2):
                                        MM(pK[:, (g * 2 + bch) * 64:(g * 2 + bch + 1) * 64], hid[:, g, hc, bch * 128:(bch + 1) * 128],
                                           w2[1][:, hc, :], hc == 0 and g == 0 and bch == 0, hc == 1, ["w2_1", "hid"], ["pK"], skip=True)
                            for bch in range(2):
                                CP("act", vcmpa[:, qd * 2 + bch, :, 0:64],
                                   pK[:, 0:256].rearrange("p (g b d) -> p g b d", g=2, b=2)[:, :, bch, :], ["pK"], ["vcmpa"], waw=False)

            if CFG['stop'] < 4:
                raise _Stop()
            s.next_phase()
            with ExitStack() as es:
                ov = T(es, "ov", [128, 8, 256], BF16)
                diag = T(es, "diag", [128, 8, 128], F32)
                winS = T(es, "winS", [128, 5, 128], F32)
                win0 = T(es, "win0", [128, 5, 128], F32)
                xiT = T(es, "xiT", [128, 512], F32)
                dmatT = T(es, "dmatT", [128, 512], F32)
                s.dma("pool", ov[:].rearrange("p m j -> p (m j)"), ovd, writes=["ov"])
                s.dma("sp", diag[:].rearrange("p m j -> p (m j)"), diagmask, writes=["diag"])
                s.dma("sp", winS[:].rearrange("p m j -> p (m j)"), winmaskS, writes=["winS"])
                s.dma("sp", win0[:].rearrange("p m j -> p (m j)"), winmask0, writes=["win0"])
                s.dma("sp", xiT[:], xiTd, writes=["xiT"])
                s.dma("sp", dmatT[:], dmatTd, writes=["dmatT"])
                qt = [T(es, "qt%d" % n, [64, 1024], BF16) for n in range(2)]
                gn = [T(es, "gn%d" % n, [128, 24], F32) for n in range(2)]
                kwt = [T(es, "kwt%d" % n, [64, 2, 640], BF16) for n in range(2)]
                vwa = [T(es, "vwa%d" % n, [128, 5, 2, 65], BF16) for n in range(2)]
                bimp = [T(es, "bimp%d" % n, [128, 256], F32) for n in range(2)]
                cmk = [T(es, "cmk%d" % n, [128, 2, 128], F32) for n in range(2)]
                rqT = [T(es, "rqT%d" % n, [128, 512], BF16) for n in range(2)]
                rkT = [T(es, "rkT%d" % n, [128, 512], BF16) for n in range(2)]
                rvt = [T(es, "rvt%d" % n, [128, 512], BF16) for n in range(2)]
                sgt = [T(es, "sgt%d" % n, [128, 512], F32) for n in range(2)]
                QA = [T(es, "QA%d" % n, [128, 4, 512], BF16) for n in range(2)]
                PT = [T(es, "PT%d" % n, [128, 512], BF16) for n in range(2)]
                oT = T(es, "oT", [65, 512], F32)
                den = T(es, "den", [128, 4], F32)
                coef = T(es, "coef", [128, 4], F32)
                nsa = T(es, "nsa", [128, 512], F32)
                nsab = T(es, "nsab", [128, 512], BF16)
                nsaTs = T(es, "nsaTs", [128, 512], BF16)
                iacc = T(es, "iacc", [128, 256], F32)
                itmp = T(es, "itmp", [128, 256], F32)
                m8a = T(es, "m8a", [128, 8], F32)
                m8b = T(es, "m8b", [128, 8], F32)
                thr = T(es, "thr", [128, 1], F32)
                nsp = T(es, "nsp", [128, 320], BF16)
                PTr = T(es, "PTr", [128, 512], BF16)
                rqx = T(es, "rqx", [128, 512], BF16)
                ss4 = T(es, "ss4", [128, 4], F32)
                rjunk = T(es, "rjunk", [128, 512], F32)
                retb = T(es, "retb", [128, 512], BF16)
                retTs = T(es, "retTs", [128, 512], BF16)
                pS = [P(es, "pS%d" % n, [128, 512], F32) for n in range(2)]
                pO = P(es, "pO", [128, 512], F32)
                pImp = P(es, "pImp", [128, 4, 256], F32)
                pTr = P(es, "pTr", [128, 512], F32)
                pRS = P(es, "pRS", [128, 512], F32)
                pRO = P(es, "pRO", [128, 512], F32)
                pTrb = pTr.bitcast(BF16) if hasattr(pTr, "bitcast") else None
                for n in range(2):
                    MEMSET("pool", vwa[n][:, :, :, 64:65], 1.0, ["vwa%d" % n], waw=False)
                MEMSET("pool", nsp[:], 0.0, ["nsp"])

                def p3_load(i):
                    pr = i % 2
                    s.dma("sp", qt[pr][:], QT[i], reads=["QT%d" % i], writes=["qt%d" % pr])
                    s.dma("sp", gn[pr][:], GN[i], reads=["GN%d" % i], writes=["gn%d" % pr])
                    s.dma("sp", kwt[pr][:].rearrange("p g t -> p (g t)"), KWT[i], reads=["KWT%d" % i], writes=["kwt%d" % pr])
                    s.dma("sp", vwa[pr][:, :, :, 0:64], VW[i].rearrange("p (j g d) -> p j g d", j=5, g=2),
                          reads=["VW%d" % i], writes=["vwa%d" % pr], waw=False)
                    s.dma("sp", bimp[pr][:], biasimp[i], writes=["bimp%d" % pr])
                    s.dma("sp", cmk[pr][:].rearrange("p a r -> p (a r)"), cmpmask[i], writes=["cmk%d" % pr])
                    s.dma("sp", rqT[pr][:], RQT[i], reads=["RQT%d" % i], writes=["rqT%d" % pr])
                    s.dma("sp", rkT[pr][:], RKT[i], reads=["RKT%d" % i], writes=["rkT%d" % pr])
                    s.dma("sp", rvt[pr][:], RV[i], reads=["RV%d" % i], writes=["rvt%d" % pr])
                    s.dma("sp", sgt[pr][:], SG[i], reads=["SG%d" % i], writes=["sgt%d" % pr])

                step = [0]

                def score_chunk(lhsT, rhs, R, mask, maskR):
                    n = step[0] % 2
                    step[0] += 1
                    MM(pS[n][:], lhsT, rhs, True, True, R, ["pS%d" % n])
                    ACT(PT[n][:], pS[n][:], AF.Exp, ["pS%d" % n], ["PT%d" % n])
                    if mask is not None:
                        pv = PT[n][:].rearrange("p (h q) -> p h q", h=4)
                        TT("dve", pv, pv, bc(mask.unsqueeze(1), [128, 4, 128]), ALU.mult, ["PT%d" % n] + maskR, ["PT%d" % n])
                    return PT[n], "PT%d" % n

                def finish_branch(br, g, pr):
                    CP("act", oT[:], pO[0:65, :], ["pO"], ["oT"])
                    for h in range(4):
                        TR(pTr[:, h * 65:(h + 1) * 65], oT[0:65, h * 128:(h + 1) * 128], identf[0:65, 0:65], ["oT", "identf"], ["pTr"])
                    pv = pTr[:, 0:260].rearrange("p (h e) -> p h e", h=4)
                    TS("dve", den[:], pv[:, :, 64], 1e-30, None, ALU.max, None, ["pTr"], ["den"])
                    RECIP(den[:], den[:], ["den"], ["den"])
                    gv = gn[pr][:].rearrange("p (h b) -> p h b", h=8)
                    TT("dve", coef[:], den[:], gv[:, 4 * g:4 * g + 4, br], ALU.mult, ["den", "gn%d" % pr], ["coef"])
                    for h in range(4):
                        dst = nsa[:, (4 * g + h) * 64:(4 * g + h + 1) * 64]
                        if br == 0:
                            TS("dve", dst, pv[:, h, 0:64], coef[:, h:h + 1], None, ALU.mult, None, ["pTr", "coef"], ["nsa"], waw=False)
                        else:
                            STT("dve", dst, pv[:, h, 0:64], coef[:, h:h + 1], dst, ALU.mult, ALU.add, ["pTr", "coef", "nsa"], ["nsa"])

                p3_load(0)
                for i in range(CFG['p3_i']):
                    pr = i % 2
                    if i + 1 < CFG['p3_i']:
                        p3_load(i + 1)
                    nkc = 8 * i + 8
                    nv = (nkc - 1) // 32 + 1
                    ncm = i // 2 + 1
                    for g in range(2):
                        qa = QA[g]
                        kqa = "QA%d" % g
                        for v in range(nv):
                            CP("pool", qa[0:64, v, :], qt[pr][:, g * 512:(g + 1) * 512], ["qt%d" % pr], [kqa], waw=(v == 0))
                        for m in range(ncm):
                            mask, maskR = None, []
                            if m >= ncm - 2:
                                mask, maskR = cmk[pr][:, m - (ncm - 2), :], ["cmk%d" % pr]
                            ptile, kpt = score_chunk(kcmpT[:, g, m * 128:(m + 1) * 128], qa[0:64, 0, :], ["kcmpT", kqa], mask, maskR)
                            MM(pO[0:65, :], vcmpa[:, m, g, :], ptile[:], m == 0, m == ncm - 1, ["vcmpa", kpt], ["pO"])
                            for h in range(4):
                                MM(pImp[:, h, :], ptile[:, h * 128:(h + 1) * 128], ov[:, m, :], m == 0 and h % 2 == 0, m == ncm - 1,
                                   [kpt, "ov"], ["pImp"], skip=True)
                        finish_branch(0, g, pr)
                        for h in range(4):
                            STT("dve", iacc[:], pImp[:, h, :], den[:, h:h + 1], bimp[pr][:] if h == 0 else iacc[:], ALU.mult, ALU.add,
                                ["pImp", "den", "bimp%d" % pr, "iacc"], ["iacc"])
                        s.op("dve", lambda e: e.max(out=m8a[:], in_=iacc[:]), ["iacc"], ["m8a"])
                        s.op("dve", lambda e: e.match_replace(out=itmp[:], in_to_replace=m8a[:], in_values=iacc[:], imm_value=-1e30),
                             ["iacc", "m8a"], ["itmp"])
                        s.op("dve", lambda e: e.max(out=m8b[:], in_=itmp[:]), ["itmp"], ["m8b"])
                        TS("dve", thr[:], m8b[:, 7:8], -64.0, None, ALU.max, None, ["m8b"], ["thr"])
                        TS("dve", nsp[:, 64:320], iacc[:], thr[:, 0:1], 1.0, ALU.is_ge, ALU.subtract, ["iacc", "thr"], ["nsp"])
                        for v in range(nv):
                            TR(pTrb[:, v * 128:(v + 1) * 128], nsp[:, 64 * v:64 * v + 128], identb[:], ["nsp", "identb"], ["pTr"])
                        for v in range(nv):
                            CP("act", qa[64:128, v, :].rearrange("p (h q) -> p h q", h=4),
                               bc(pTrb[64:128, v * 128:(v + 1) * 128].unsqueeze(1), [64, 4, 128]), ["pTr"], [kqa], waw=False)
                        for kc in range(nkc):
                            mask, maskR = None, []
                            if kc >= nkc - 8:
                                mask, maskR = diag[:, kc - (nkc - 8), :], ["diag"]
                            ptile, kpt = score_chunk(KsTa[g][:, kc * 128:(kc + 1) * 128], qa[:, kc // 32, :], ["KsTa%d" % g, kqa], mask, maskR)
                            MM(pO[0:65, :], Vsa[:, kc, g, :], ptile[:], kc == 0, kc == nkc - 1, ["Vsa", kpt], ["pO"])
                        finish_branch(1, g, pr)
                        wm = win0 if i == 0 else winS
                        for j in range(5):
                            ptile, kpt = score_chunk(kwt[pr][:, g, j * 128:(j + 1) * 128], qa[0:64, 0, :], ["kwt%d" % pr, kqa],
                                                     wm[:, j, :], ["win0" if i == 0 else "winS"])
                            MM(pO[0:65, :], vwa[pr][:, j, g, :], ptile[:], j == 0, j == 4, ["vwa%d" % pr, kpt], ["pO"])
                        finish_branch(2, g, pr)
                    CP("pool", nsab[:], nsa[:], ["nsa"], ["nsab"])
                    for k in range(4):
                        TR(pTrb[:, k * 128:(k + 1) * 128], nsab[:, k * 128:(k + 1) * 128], identb[:], ["nsab", "identb"], ["pTr"])
                    CP("act", nsaTs[:], pTrb[:, 0:512], ["pTr"], ["nsaTs"])
                    s.dma("sp", NSAT[i], nsaTs[:], reads=["nsaTs"], writes=["NSAT%d" % i])
                    for h in range(4):
                        MM(pRS[:, h * 128:(h + 1) * 128], rkT[pr][:, h * 128:(h + 1) * 128], rqT[pr][:, h * 128:(h + 1) * 128],
                           True, True, ["rkT%d" % pr, "rqT%d" % pr], ["pRS"])
                    TT("dve", PTr[:], pRS[:], dmatT[:], ALU.mult, ["pRS", "dmatT"], ["PTr"])
                    TT("pool", rqx[:], rqT[pr][:], xiT[:], ALU.mult, ["rqT%d" % pr, "xiT"], ["rqx"])
                    for h in range(4):
                        hs = slice(h * 128, (h + 1) * 128)
                        MM(pRO[:, hs], PTr[:, hs], rvt[pr][:, hs], True, False, ["PTr", "rvt%d" % pr], ["pRO"], skip=True)
                        MM(pRO[:, hs], rqx[:, hs], Sown[:, i, hs], False, True, ["rqx", "Sown"], ["pRO"], skip=True)
                    for h in range(4):
                        ACT(rjunk[:, h * 128:(h + 1) * 128], pRO[:, h * 128:(h + 1) * 128], AF.Square, ["pRO"], ["rjunk", "ss4"], waw=(h == 0),
                            accum_out=ss4[:, h:h + 1])
                    ACT(ss4[:], ss4[:], AF.Ln, ["ss4", "epsb"], ["ss4"], scale=1.0 / 128, bias=epsb[:])
                    ACT(ss4[:], ss4[:], AF.Exp, ["ss4"], ["ss4"], scale=-0.5)
                    for h in range(4):
                        hs = slice(h * 128, (h + 1) * 128)
                        STT("dve", retb[:, hs], pRO[:, hs], ss4[:, h:h + 1], sgt[pr][:, hs], ALU.mult, ALU.mult,
                            ["pRO", "ss4", "sgt%d" % pr], ["retb"], waw=False)
                    for k in range(4):
                        TR(pTrb[:, k * 128:(k + 1) * 128], retb[:, k * 128:(k + 1) * 128], identb[:], ["retb", "identb"], ["pTr"])
                    CP("act", retTs[:], pTrb[:, 0:512], ["pTr"], ["retTs"])
                    s.dma("sp", RETT[i], retTs[:], reads=["retTs"], writes=["RETT%d" % i])

        if CFG['stop'] < 5:
            raise _Stop()
        s.next_phase()
        with ExitStack() as es:
            Wga = T(es, "Wga", [128, 8, 2048], BF16)
            Wpa = T(es, "Wpa", [128, 4, 1024], BF16)
            Wpb = T(es, "Wpb", [128, 4, 1024], BF16)
            Wo = T(es, "Wo", [128, 8, 1024], BF16)
            grep = T(es, "grep4", [128, D], F32)
            s.dma("sp", grep[:], gmlp, writes=["grep4"])
            wload(Wga, w_in[:, C_GA:C_GA + 2048], "Wga")
            wload(Wpa, wpa, "Wpa")
            wload(Wpb, wpb, "Wpb")
            wload(Wo, wout, "Wo")
            hTg = [T(es, "hTg%d" % n, [128, 8, 512], BF16) for n in range(2)]
            nsT = [T(es, "nsT%d" % n, [128, 4, 512], BF16) for n in range(2)]
            reT = [T(es, "reT%d" % n, [128, 4, 512], BF16) for n in range(2)]
            sig = [T(es, "sig%d" % n, [128, 512], F32) for n in range(2)]
            mxa = T(es, "mxa", [128, 512], F32)
            mixT = T(es, "mixT", [128, 8, 512], BF16)
            xo_t = [T(es, "xo_t%d" % n, [128, D], F32) for n in range(2)]
            xm = [T(es, "xm%d" % n, [128, D], F32) for n in range(2)]
            junk = T(es, "junk4", [128, D], F32)
            ss = T(es, "ss4_", [128, 1], F32)
            rstd = T(es, "rstd4", [128, 1], F32)
            h2b = T(es, "h2b", [128, D], BF16)
            h2T = [T(es, "h2Ts%d" % n, [128, 1024], BF16) for n in range(2)]
            pG = [P(es, "pG%d" % n, [128, 512], F32) for n in range(2)]
            pPr = [P(es, "pPr%d" % n, [128, 512], F32) for n in range(2)]
            pX = P(es, "pX", [128, 2, 512], F32)
            pT = P(es, "pT4", [128, 1024], BF16)

            def p4_load(G):
                pr = G % 2
                for u in range(4):
                    i = 4 * G + u
                    s.dma("sp", hTg[pr][:, :, u * 128:(u + 1) * 128], HT[i].rearrange("p (k t) -> p k t", k=8),
                          reads=["HT%d" % i], writes=["hTg%d" % pr], waw=False)
                    s.dma("sp", nsT[pr][:, :, u * 128:(u + 1) * 128], NSAT[i].rearrange("p (k t) -> p k t", k=4),
                          reads=["NSAT%d" % i], writes=["nsT%d" % pr], waw=False)
                    s.dma("sp", reT[pr][:, :, u * 128:(u + 1) * 128], RETT[i].rearrange("p (k t) -> p k t", k=4),
                          reads=["RETT%d" % i], writes=["reT%d" % pr], waw=False)

            p4_load(0)
            for G in range(CFG.get('p4_g', 4)):
                pr = G % 2
                if G + 1 < CFG.get('p4_g', 4):
                    p4_load(G + 1)
                for oc in range(8):
                    for ab, (Wp, srcT, ksrc) in enumerate(((Wpa, nsT, "nsT"), (Wpb, reT, "reT"))):
                        c0 = ab * 1024 + oc * 128
                        for k in range(8):
                            MM(pG[ab][:], Wga[:, k, c0:c0 + 128], hTg[pr][:, k, :], k == 0, k == 7, ["Wga", "hTg%d" % pr], ["pG%d" % ab])
                        for k in range(4):
                            MM(pPr[ab][:], Wp[:, k, oc * 128:(oc + 1) * 128], srcT[pr][:, k, :], k == 0, k == 3,
                               ["Wpa", "Wpb", "%s%d" % (ksrc, pr)], ["pPr%d" % ab])
                        ACT(sig[ab][:], pG[ab][:], AF.Exp, ["pG%d" % ab], ["sig%d" % ab], scale=-1.0)
                        TS("dve", sig[ab][:], sig[ab][:], 1.0, None, ALU.add, None, ["sig%d" % ab], ["sig%d" % ab])
                        RECIP(sig[ab][:], sig[ab][:], ["sig%d" % ab], ["sig%d" % ab])
                    TT("dve", mxa[:], pPr[0][:], sig[0][:], ALU.mult, ["pPr0", "sig0"], ["mxa"])
                    TT("dve", sig[1][:], pPr[1][:], sig[1][:], ALU.mult, ["pPr1", "sig1"], ["sig1"])
                    TT("pool", mixT[:, oc, :], mxa[:], sig[1][:], ALU.add, ["mxa", "sig1"], ["mixT"], waw=False)
                for u in range(4):
                    i = 4 * G + u
                    p2_ = i % 2
                    s.dma("sp", xo_t[p2_][:], xo[(5 * i + 4) * 128:(5 * i + 5) * 128, :], writes=["xo_t%d" % p2_])
                    for hf in range(2):
                        for oc in range(8):
                            MM(pX[:, hf, :], mixT[:, oc, u * 128:(u + 1) * 128], Wo[:, oc, hf * 512:(hf + 1) * 512], oc == 0, oc == 7,
                               ["mixT", "Wo"], ["pX%d" % hf])
                    TT("dve", xm[p2_][:], pX[:].rearrange("p a c -> p (a c)"), xo_t[p2_][:], ALU.add, ["pX0", "pX1", "xo_t%d" % p2_], ["xm%d" % p2_])
                    s.dma("sp", XMID[i], xm[p2_][:], reads=["xm%d" % p2_], writes=["XMID%d" % i])
                    rmsnorm((junk, ss, rstd), xm[p2_][:], "xm%d" % p2_, grep[:], "grep4", h2b[:], "h2b", "4")
                    for k in range(8):
                        TR(pT[:, k * 128:(k + 1) * 128], h2b[:, k * 128:(k + 1) * 128], identb[:], ["h2b", "identb"], ["pT4"])
                    CP("act", h2T[p2_][:], pT[:], ["pT4"], ["h2Ts%d" % p2_])
                    s.dma("sp", H2T[i], h2T[p2_][:], reads=["h2Ts%d" % p2_], writes=["H2T%d" % i])

        if CFG['stop'] < 6:
            raise _Stop()
        s.next_phase()
        with ExitStack() as es:
            Wu = T(es, "Wu", [128, 8, 4096], BF16)
            Wd = T(es, "Wd", [128, 32, 1024], BF16)
            grep = T(es, "grep5", [128, D], F32)
            s.dma("sp", grep[:], gfin, writes=["grep5"])
            for q4 in range(4):
                for k in range(8):
                    s.dma("pool", Wu[:, k, q4 * 1024:(q4 + 1) * 1024], wup[k * 128:(k + 1) * 128, q4 * 1024:(q4 + 1) * 1024],
                          writes=["Wu"], waw=False)
                for hc in range(q4 * 8, q4 * 8 + 8):
                    s.dma("pool", Wd[:, hc, :], wdown[hc * 128:(hc + 1) * 128, :], writes=["Wd"], waw=False)
            h2g = [T(es, "h2g%d" % n, [128, 8, 256], BF16) for n in range(2)]
            ur = [T(es, "ur%d" % n, [128, 256], F32) for n in range(2)]
            uT = [T(es, "uT%d" % n, [128, 256], BF16) for n in range(2)]
            xmr = [T(es, "xmr%d" % n, [128, D], F32) for n in range(2)]
            xf = [T(es, "xf%d" % n, [128, D], F32) for n in range(2)]
            junk = T(es, "junk5", [128, D], F32)
            ss = T(es, "ss5", [128, 1], F32)
            rstd = T(es, "rstd5", [128, 1], F32)
            pU = [P(es, "pU%d" % n, [128, 512], F32) for n in range(2)]
            pD = P(es, "pD", [128, 4, 512], F32)

            def p5_load(G):
                pr = G % 2
                for u in range(2):
                    i = 2 * G + u
                    s.dma("sp", h2g[pr][:, :, u * 128:(u + 1) * 128], H2T[i].rearrange("p (k t) -> p k t", k=8),
                          reads=["H2T%d" % i], writes=["h2g%d" % pr], waw=False)

            p5_load(0)
            for G in range(CFG.get('p5_g', 8)):
                pr = G % 2
                if G + 1 < CFG.get('p5_g', 8):
                    p5_load(G + 1)
                for hc in range(32):
                    n = hc % 2
                    for k in range(8):
                        MM(pU[n][:, 0:256], Wu[:, k, hc * 128:(hc + 1) * 128], h2g[pr][:, k, :], k == 0, k == 7, ["Wu", "h2g%d" % pr], ["pU%d" % n])
                    ACT(ur[n][:], pU[n][:, 0:256], AF.Relu, ["pU%d" % n], ["ur%d" % n])
                    TT("dve", uT[n][:], ur[n][:], ur[n][:], ALU.mult, ["ur%d" % n], ["uT%d" % n])
                    for u in range(2):
                        for hf in range(2):
                            MM(pD[:, u * 2 + hf, :], uT[n][:, u * 128:(u + 1) * 128], Wd[:, hc, hf * 512:(hf + 1) * 512], hc == 0, hc == 31,
                               ["uT%d" % n, "Wd"], ["pD%d" % (u * 2 + hf)])
                for u in range(2):
                    i = 2 * G + u
                    s.dma("sp", xmr[u][:], XMID[i], reads=["XMID%d" % i], writes=["xmr%d" % u])
                    TT("dve", xf[u][:], pD[:, 2 * u:2 * u + 2, :].rearrange("p a c -> p (a c)"), xmr[u][:], ALU.add,
                       ["pD%d" % (2 * u), "pD%d" % (2 * u + 1), "xmr%d" % u], ["xf%d" % u])
                    rmsnorm((junk, ss, rstd), xf[u][:], "xf%d" % u, grep[:], "grep5", xmr[u][:], "xmr%d" % u, "5")
                    s.dma("sp", y[i], xmr[u][:], reads=["xmr%d" % u], writes=["Y"], waw=False)
            s.op("sp", None, reads=["Y"])

    except _Stop:
        pass
    s.op("sp", None, reads=["Y"])
    s.emit()
    try:
        top.close()
    except AssertionError:
        pass
    return nc


def _const_tables():
    n = np.arange(128, dtype=np.float64)
    zeta = np.stack([np.exp(LOGG[h] * (127.0 - n)) for h in range(4)], axis=1) * (128.0 ** -0.5)
    xi = np.stack([np.exp(LOGG[h] * (n + 1.0)) for h in range(4)], axis=0)
    xiT = np.broadcast_to(xi.reshape(1, 512), (128, 512))
    rel = n[None, :] - n[:, None]
    dmatT = np.stack([np.where(rel >= 0, np.exp(LOGG[h] * np.maximum(rel, 0.0)), 0.0) for h in range(4)], axis=1)
    dmatT = dmatT.reshape(128, 512) * (128.0 ** -0.5)
    c = np.arange(1024)
    j = np.arange(256)
    ovl = np.clip(np.minimum(c[:, None] * 16 + 32, j[None, :] * 64 + 64) - np.maximum(c[:, None] * 16, j[None, :] * 64), 0, None) / 32.0
    ov = ovl.reshape(8, 128, 256).transpose(1, 0, 2).reshape(128, 8 * 256)
    key = np.arange(4096)
    apat = ((key[None, :] // 64) % 64 == np.arange(64)[:, None]).astype(np.float64) * NEGBIG
    half_n = np.exp((-math.log(500000.0) * np.arange(8, dtype=np.float32) * 2.0 / 16).astype(np.float32))
    half_r = np.exp((-math.log(10000.0) * np.arange(64, dtype=np.float32) * 2.0 / 128).astype(np.float32))
    inv = np.broadcast_to(np.concatenate([half_n, half_r])[None, :], (128, 72))
    f = lambda a: np.ascontiguousarray(a, dtype=np.float32)
    kk = np.arange(128)[:, None]
    r = np.arange(128)[None, :]
    tri = (kk <= r).astype(np.float32)
    anti = (kk > r).astype(np.float32)
    ones = np.ones((128, 128), np.float32)
    winS = np.stack([anti, ones, ones, ones, tri], axis=1).reshape(128, 640)
    return dict(zeta=f(zeta), xiT=f(xiT), dmatT=f(dmatT), ov=f(ov), apat=f(apat), inv=f(inv),
                ident=f(np.eye(128)), winmaskS=f(winS)), tri, anti, ones


def _core_tables(c, tri, anti, ones):
    zeros = np.zeros((128, 128), np.float32)
    q = np.arange(128)
    jj = np.arange(256)
    biasimp = np.zeros((16, 128, 256), np.float32)
    cmpmask = np.zeros((16, 128, 256), np.float32)
    for i in range(16):
        B = 8 * i + c
        cur = 2 * B + (q >= 64).astype(np.int64)
        b = np.zeros((128, 256), np.float32)
        b += 128.0 * ((jj[None, :] >= cur[:, None] - 1) & (jj[None, :] <= cur[:, None]))
        b -= 256.0 * (jj[None, :] > cur[:, None])
        b[:, 0] += 128.0
        biasimp[i] = b
        ncm = i // 2 + 1
        for a in range(2):
            m = ncm - 2 + a
            if m < 0:
                continue
            ci = 128 * m + np.arange(128)
            valid = ((16 * ci[:, None] + 31) <= (128 * B + q[None, :])) & (ci[:, None] <= 1022)
            cmpmask[i, :, a * 128:(a + 1) * 128] = valid
    diag = np.stack([ones if m < c else (tri if m == c else zeros) for m in range(8)], axis=1).reshape(128, 1024)
    w0 = []
    for j in range(5):
        if c - 4 + j < 0:
            w0.append(zeros)
        else:
            w0.append(anti if j == 0 else (tri if j == 4 else ones))
    win0 = np.stack(w0, axis=1).reshape(128, 640)
    oh = np.zeros((128, 8), np.float32)
    oh[:, c] = 1.0
    return dict(biasimp=biasimp, cmpmask=cmpmask, diagmask=np.ascontiguousarray(diag, dtype=np.float32),
                winmask0=np.ascontiguousarray(win0, dtype=np.float32), oh=oh)


_NC_CACHE = {}


def _prep(x, positions, norm_mix, w_in, cmp_pos_k, cmp_pos_v, cmp_k_w1, cmp_k_w2, cmp_v_w1, cmp_v_w2,
          w_proj_a, w_proj_b, w_out, norm_mlp, w_up, w_down, norm_final, cores=range(NCORE)):
    f = lambda a: np.ascontiguousarray(np.asarray(a), dtype=np.float32)
    x2 = f(x).reshape(SEQ, D)
    pos = np.asarray(positions).reshape(SEQ).astype(np.int32)
    consts, tri, anti, ones = _const_tables()
    shared = dict(
        xg=x2, w_in=f(w_in)[0], ckw1=f(cmp_k_w1)[0], ckw2=f(cmp_k_w2)[0], cvw1=f(cmp_v_w1)[0], cvw2=f(cmp_v_w2)[0],
        pekT=np.ascontiguousarray(f(cmp_pos_k)[0].T), pevT=np.ascontiguousarray(f(cmp_pos_v)[0].T),
        wpa=f(w_proj_a)[0], wpb=f(w_proj_b)[0], wout=f(w_out)[0], wup=f(w_up)[0], wdown=f(w_down)[0],
        gmix=np.ascontiguousarray(np.broadcast_to(f(norm_mix)[0][None, :], (128, D))),
        gmlp=np.ascontiguousarray(np.broadcast_to(f(norm_mlp)[0][None, :], (128, D))),
        gfin=np.ascontiguousarray(np.broadcast_to(f(norm_final)[None, :], (128, D))),
        **consts)
    posg = pos.reshape(128, 128).T
    xpad = np.concatenate([np.zeros((512, D), np.float32), x2], axis=0)
    ppad = np.concatenate([np.zeros((512,), np.int32), pos], axis=0)
    in_maps = []
    for c in cores:
        rows = np.concatenate([np.arange(128 * (8 * i + c), 128 * (8 * i + c) + 640) for i in range(16)])
        xo = np.ascontiguousarray(xpad[rows])
        poso = ppad[rows].reshape(80, 128).T
        m = dict(shared)
        m["xo"] = xo
        m["posall"] = np.ascontiguousarray(np.concatenate([posg, poso], axis=1), dtype=np.int32)
        m.update(_core_tables(c, tri, anti, ones))
        in_maps.append(m)
    return in_maps


def kernel(**inputs):
    in_maps = _prep(**inputs)
    if "nc" not in _NC_CACHE:
        _NC_CACHE["nc"] = build_nc()
    res = run_bass_kernel_spmd(_NC_CACHE["nc"], in_maps, core_ids=list(range(NCORE)))
    out = np.zeros((SEQ, D), np.float32)
    for c in range(NCORE):
        yc = np.asarray(res.results[c]["y"]).reshape(16, 128, D)
        for i in range(16):
            B = 8 * i + c
            out[128 * B:128 * B + 128] = yc[i]
    return out.reshape(1, SEQ, D)
```

```python
import math
from contextlib import ExitStack

import numpy as np
import concourse.bass as bass
import concourse.mybir as mybir
from concourse.bass_utils import run_bass_kernel_spmd

F32 = mybir.dt.float32
BF16 = mybir.dt.bfloat16
I32 = mybir.dt.int32
AF = mybir.ActivationFunctionType
ALU = mybir.AluOpType

ENGS = ("pe", "dve", "act", "pool", "sp")
EPOCH = 24000
DMA_RING = {"sp": 8, "pool": 2, "act": 4}
NAMES = {}
CFG = dict(stop=99, p1_tiles=80, p2_tiles=128, p3_i=16, dbg=())

SEQ = 16384
D = 1024
NCORE = 8
NQB = 16
NTG = 128
TWO_PI = float(2 * np.pi)
MAGIC = 12582912.0
NEGBIG = 30000.0
GAM = [1.0 - 2.0 ** (-5.0 - h) for h in range(4)]
LOGG = [math.log(g) for g in GAM]


class _Stop(Exception):
    pass


class Sched:
    def __init__(self, nc):
        self.nc = nc
        self.ops = []
        self.phase = 0

    def next_phase(self):
        self.phase += 1

    def op(self, eng, fn, reads=(), writes=(), dma=False, waw=True):
        self.ops.append(dict(eng=eng, fn=fn, reads=tuple(reads), writes=tuple(writes),
                             dma=dma, waw=waw, deps=set(), signal=False, phase=self.phase))

    def dma(self, eng, out, in_, reads=(), writes=(), waw=True):
        self.op(eng, lambda e: e.dma_start(out=out, in_=in_), reads, writes, dma=True, waw=waw)

    def analyze(self):
        writers, readers = {}, {}
        ops = self.ops
        qcnt, last_on_sem = {}, {}
        for i, o in enumerate(ops):
            if not o["dma"]:
                continue
            n = qcnt.get(o["eng"], 0)
            qcnt[o["eng"]] = n + 1
            ring = DMA_RING[o["eng"]]
            key = ("dmaq", o["eng"], n % ring)
            o["sem"], o["val"] = key, 16 * (n // ring + 1)
            o["signal"] = True
            if key in last_on_sem:
                o["deps"].add(last_on_sem[key])
            last_on_sem[key] = i
        for o in ops:
            if o["fn"] is None:
                o["deps"].update(last_on_sem.values())
        last_eng, last_dma, barrier, cur = {}, {}, set(), 0
        for i, o in enumerate(ops):
            if o["phase"] != cur:
                cur = o["phase"]
                barrier = set(last_eng.values()) | set(last_dma.values())
            o["deps"].update(barrier)
            if o["dma"]:
                last_dma[o["sem"]] = i
            elif o["fn"] is not None:
                last_eng[o["eng"]] = i
        for i, o in enumerate(ops):
            deps = o["deps"]
            for k in o["reads"]:
                deps.update(writers.get(k, ()))
            for k in o["writes"]:
                deps.update(readers.get(k, ()))
                if o["waw"]:
                    deps.update(writers.get(k, ()))
            for k in o["reads"]:
                lst = readers.setdefault(k, [])
                if not o["dma"]:
                    lst[:] = [j for j in lst if ops[j]["dma"] or ops[j]["eng"] != o["eng"]]
                lst.append(i)
            for k in o["writes"]:
                if o["waw"]:
                    writers[k] = [i]
                    readers[k] = []
                else:
                    lst = writers.setdefault(k, [])
                    if not o["dma"]:
                        lst[:] = [j for j in lst if ops[j]["dma"] or ops[j]["eng"] != o["eng"]]
                    lst.append(i)
            deps.discard(i)
            if o["eng"] == "pe" and not o["dma"]:
                for j in [j for j in deps if ops[j]["eng"] == "pe" and not ops[j]["dma"]]:
                    deps.discard(j)
            for j in deps:
                ops[j]["signal"] = True
        cnt = {e: 0 for e in ENGS}
        dcnt = {}
        for o in ops:
            if not o["signal"]:
                continue
            if o["dma"]:
                continue
            else:
                e = o["eng"]
                o["sem"], o["val"] = ("eng", e, cnt[e] // EPOCH), cnt[e] % EPOCH + 1
                cnt[e] += 1
        self.sem_keys = sorted({o["sem"] for o in ops if o["signal"]}, key=str)
        print("sched: ops=%d signals=%s dma=%s nsem=%d" % (len(ops), cnt, qcnt, len(self.sem_keys)), flush=True)

    def emit(self):
        nc = self.nc
        self.analyze()
        ops = self.ops
        with ExitStack() as es:
            sems = {k: es.enter_context(nc.semaphore("s%d" % n)) for n, k in enumerate(self.sem_keys)}
            block = es.enter_context(nc.Block())

            def run_engine(engname, e):
                waited = {}
                for o in ops:
                    if o["eng"] != engname:
                        continue
                    need = {}
                    for j in o["deps"]:
                        k, v = ops[j]["sem"], ops[j]["val"]
                        if waited.get(k, 0) < v and need.get(k, 0) < v:
                            need[k] = v
                    for k, v in need.items():
                        e.wait_ge(sems[k], v)
                        waited[k] = v
                    if o["fn"] is None:
                        continue
                    inst = o["fn"](e)
                    if o["signal"]:
                        inst.then_inc(sems[o["sem"]], 16 if o["dma"] else 1)

            @block.sync
            def _(e):
                run_engine("sp", e)

            @block.scalar
            def _(e):
                run_engine("act", e)

            @block.vector
            def _(e):
                run_engine("dve", e)

            @block.gpsimd
            def _(e):
                run_engine("pool", e)

            @block.tensor
            def _(e):
                run_engine("pe", e)


C_Q, C_KC, C_VC, C_KS, C_VS, C_KW, C_VW, C_NG, C_RQ, C_RK, C_RV, C_RG, C_GA, C_GB = (
    0, 512, 640, 768, 896, 1024, 1152, 1280, 1304, 1816, 2328, 2840, 3352, 4376)


def build_nc():
    nc = bass.Bass("TRN2", target_bir_lowering=False)

    def din(name, shape, dt=F32):
        return nc.dram_tensor(name, list(shape), dt, kind="ExternalInput").ap()

    def dscr(name, shape, dt):
        kind = "ExternalOutput" if name in CFG["dbg"] else "Internal"
        return nc.dram_tensor(name, list(shape), dt, kind=kind).ap()

    xg = din("xg", [SEQ, D])
    xo = din("xo", [80 * 128, D])
    posall = din("posall", [128, 208], I32)
    inv = din("inv", [128, 72])
    w_in = din("w_in", [D, 5400])
    ckw1 = din("ckw1", [2048, 256])
    ckw2 = din("ckw2", [256, 64])
    cvw1 = din("cvw1", [2048, 256])
    cvw2 = din("cvw2", [256, 64])
    pekT = din("pekT", [64, 32])
    pevT = din("pevT", [64, 32])
    wpa = din("wpa", [512, D])
    wpb = din("wpb", [512, D])
    wout = din("wout", [D, D])
    wup = din("wup", [D, 4096])
    wdown = din("wdown", [4096, D])
    gmix = din("gmix", [128, D])
    gmlp = din("gmlp", [128, D])
    gfin = din("gfin", [128, D])
    biasimp = din("biasimp", [16, 128, 256])
    cmpmask = din("cmpmask", [16, 128, 256])
    diagmask = din("diagmask", [128, 8 * 128])
    winmask0 = din("winmask0", [128, 5 * 128])
    winmaskS = din("winmaskS", [128, 5 * 128])
    ohd = din("oh", [128, 8])
    ovd = din("ov", [128, 8 * 256])
    apat = din("apat", [64, 4096])
    identd = din("ident", [128, 128])
    zetad = din("zeta", [128, 4])
    xiTd = din("xiT", [128, 512])
    dmatTd = din("dmatT", [128, 512])
    y = nc.dram_tensor("y", [16, 128, D], F32, kind="ExternalOutput").ap()

    ROT = dscr("ROT", [128, 208, 144], F32)
    QT = dscr("QT", [16, 64, 1024], BF16)
    GN = dscr("GN", [16, 128, 24], F32)
    RQT = dscr("RQT", [16, 128, 512], BF16)
    RKT = dscr("RKT", [16, 128, 512], BF16)
    RV = dscr("RV", [16, 128, 512], BF16)
    SG = dscr("SG", [16, 128, 512], F32)
    KWT = dscr("KWT", [16, 64, 1280], BF16)
    VW = dscr("VW", [16, 128, 640], BF16)
    HT = dscr("HT", [16, 128, 1024], BF16)
    KCT = dscr("KCT", [128, 16512], BF16)
    VCT = dscr("VCT", [128, 16512], BF16)
    NSAT = dscr("NSAT", [16, 128, 512], BF16)
    RETT = dscr("RETT", [16, 128, 512], BF16)
    XMID = dscr("XMID", [16, 128, 1024], F32)
    H2T = dscr("H2T", [16, 128, 1024], BF16)

    s = Sched(nc)

    def ACT(out, in_, func, R, W, waw=True, **kw):
        s.op("act", lambda e: e.activation(out=out, in_=in_, func=func, **kw), R, W, waw=waw)

    def TT(eng, out, a, b, op, R, W, waw=True):
        s.op(eng, lambda e: e.tensor_tensor(out=out, in0=a, in1=b, op=op), R, W, waw=waw)

    def TS(eng, out, a, s1, s2, op0, op1, R, W, waw=True):
        if op1 is None:
            s.op(eng, lambda e: e.tensor_scalar(out=out, in0=a, scalar1=s1, scalar2=None, op0=op0), R, W, waw=waw)
        else:
            s.op(eng, lambda e: e.tensor_scalar(out=out, in0=a, scalar1=s1, scalar2=s2, op0=op0, op1=op1), R, W, waw=waw)

    def STT(eng, out, in0, scalar, in1, op0, op1, R, W, waw=True):
        s.op(eng, lambda e: e.scalar_tensor_tensor(out=out, in0=in0, scalar=scalar, in1=in1, op0=op0, op1=op1),
             R, W, waw=waw)

    def CP(eng, out, in_, R, W, waw=True):
        if eng == "act":
            ACT(out, in_, AF.Copy, R, W, waw=waw)
        else:
            s.op(eng, lambda e: e.tensor_copy(out=out, in_=in_), R, W, waw=waw)

    def RECIP(out, in_, R, W):
        s.op("dve", lambda e: e.reciprocal(out=out, in_=in_), R, W)

    def MM(out, lhsT, rhs, start, stop, R, W, skip=False):
        s.op("pe", lambda e: e.matmul(out, lhsT=lhsT, rhs=rhs, start=start, stop=stop, skip_group_check=skip),
             R, W, waw=False)

    def TR(out, in_, ident, R, W):
        s.op("pe", lambda e: e.transpose(out=out, in_=in_, identity=ident), R, W, waw=False)

    def MEMSET(eng, ap, val, W, waw=True):
        s.op(eng, lambda e: e.memset(ap, val), (), W, waw=waw)

    def wload(dst, src2d, key):
        K = dst.shape[1]
        for k in range(K):
            s.dma("pool", dst[:, k, :], src2d[k * 128:(k + 1) * 128, :], writes=[key], waw=False)

    def bc(ap, shape):
        return ap.to_broadcast(list(shape))

    top = ExitStack()
    try:
        uniq = [0]

        def T(es, name, shape, dt):
            uniq[0] += 1
            NAMES[name] = "sb%d_%s" % (uniq[0], name)
            return es.enter_context(nc.sbuf_tensor("sb%d_%s" % (uniq[0], name), list(shape), dt))

        def P(es, name, shape, dt):
            uniq[0] += 1
            return es.enter_context(nc.psum_tensor("ps%d_%s" % (uniq[0], name), list(shape), dt))

        identf = T(top, "identf", [128, 128], F32)
        identb = T(top, "identb", [128, 128], BF16)
        epsb = T(top, "epsb", [128, 1], F32)
        s.dma("sp", identf[:], identd, writes=["identf"])
        CP("dve", identb[:], identf[:], ["identf"], ["identb"])
        MEMSET("pool", epsb[:], 1e-6, ["epsb"])

        def rmsnorm(es_tmp, xt, kx, grep, kg, hb, khb, tag):
            junk, ss, rstd = es_tmp
            ACT(junk[:], xt, AF.Square, [kx], ["junk" + tag, "ss" + tag], accum_out=ss[:])
            ACT(rstd[:], ss[:], AF.Ln, ["ss" + tag, "epsb"], ["rstd" + tag], scale=1.0 / D, bias=epsb[:])
            ACT(rstd[:], rstd[:], AF.Exp, ["rstd" + tag], ["rstd" + tag], scale=-0.5)
            STT("dve", hb, xt, rstd[:], grep, ALU.mult, ALU.mult, [kx, "rstd" + tag, kg], [khb])

        def rotary(eng, x1, x2, cos, sin, o1, o2, tmp, R, W, tag):
            t1, t2, t3, t4 = tmp
            kt = ["rt%d%s" % (n, tag) for n in range(4)]
            TT(eng, t1, x1, cos, ALU.mult, R, [kt[0]])
            TT(eng, t2, x2, sin, ALU.mult, R, [kt[1]])
            TT(eng, t3, x1, sin, ALU.mult, R, [kt[2]])
            TT(eng, t4, x2, cos, ALU.mult, R, [kt[3]])
            TT(eng, o1, t1, t2, ALU.subtract, [kt[0], kt[1]], W, waw=False)
            TT(eng, o2, t4, t3, ALU.add, [kt[2], kt[3]], W, waw=False)

        with ExitStack() as es:
            posi = T(es, "posi", [128, 208], I32)
            posf = T(es, "posf", [128, 208], F32)
            invs = T(es, "invs", [128, 72], F32)
            s.dma("sp", posi[:], posall, writes=["posi"])
            s.dma("sp", invs[:], inv, writes=["invs"])
            CP("dve", posf[:], posi[:], ["posi"], ["posf"])
            CH = 16
            ang = [T(es, "ang%d" % n, [128, CH, 72], F32) for n in range(2)]
            a2 = T(es, "a2", [128, CH, 72], F32)
            kf = T(es, "kf", [128, CH, 72], F32)
            rr = T(es, "rr", [128, CH, 72], F32)
            rot = [T(es, "rotc%d" % n, [128, CH, 144], F32) for n in range(2)]
            for ci in range(208 // CH):
                pr = ci % 2
                c0 = ci * CH
                TT("dve", ang[pr][:], bc(invs[:].unsqueeze(1), [128, CH, 72]),
                   bc(posf[:, c0:c0 + CH].unsqueeze(2), [128, CH, 72]), ALU.mult, ["invs", "posf"], ["ang%d" % pr])
                for which in range(2):
                    if which == 0:
                        src, ksrc = ang[pr], "ang%d" % pr
                    else:
                        TS("dve", a2[:], ang[pr][:], float(np.pi / 2), None, ALU.add, None, ["ang%d" % pr], ["a2"])
                        src, ksrc = a2, "a2"
                    TS("dve", kf[:], src[:], 1.0 / TWO_PI, MAGIC, ALU.mult, ALU.add, [ksrc], ["kf"])
                    TS("dve", kf[:], kf[:], -MAGIC, None, ALU.add, None, ["kf"], ["kf"])
                    STT("dve", rr[:], kf[:], -TWO_PI, src[:], ALU.mult, ALU.add, ["kf", ksrc], ["rr"])
                    TS("dve", rr[:], rr[:], 3.14159, -3.14159, ALU.min, ALU.max, ["rr"], ["rr"])
                    ACT(rot[pr][:, :, which * 72:(which + 1) * 72], rr[:], AF.Sin, ["rr"], ["rot%d" % pr], waw=False)
                s.dma("sp", ROT[:, c0:c0 + CH, :], rot[pr][:], reads=["rot%d" % pr], writes=["ROT"], waw=False)

        if CFG['stop'] < 1:
            raise _Stop()
        s.next_phase()
        with ExitStack() as es:
            Wq = T(es, "Wq", [128, 8, 512], BF16)
            Wng = T(es, "Wng", [128, 8, 24], BF16)
            Wr = T(es, "Wr", [128, 8, 2048], BF16)
            Ww = T(es, "Ww", [128, 8, 256], BF16)
            grep = T(es, "grep1", [128, D], F32)
            s.dma("sp", grep[:], gmix, writes=["grep"])
            wload(Ww, w_in[:, C_KW:C_KW + 256], "Ww")
            wload(Wq, w_in[:, C_Q:C_Q + 512], "Wq")
            wload(Wng, w_in[:, C_NG:C_NG + 24], "Wng")
            wload(Wr, w_in[:, C_RQ:C_RQ + 2048], "Wr")
            xt = [T(es, "xt%d" % n, [128, D], F32) for n in range(2)]
            rot = [T(es, "rot%d" % n, [128, 144], F32) for n in range(2)]
            hb = [T(es, "hb%d" % n, [128, D], BF16) for n in range(2)]
            hT = [T(es, "hT%d" % n, [128, 8, 128], BF16) for n in range(2)]
            junk = T(es, "junk", [128, D], F32)
            ss = T(es, "ss", [128, 1], F32)
            rstd = T(es, "rstd", [128, 1], F32)
            kwf = T(es, "kwf", [128, 128], F32)
            kwb = T(es, "kwb", [128, 128], BF16)
            vwb = [T(es, "vwb%d" % n, [128, 128], BF16) for n in range(2)]
            kwT = [T(es, "kwT%d" % n, [64, 2, 128], BF16) for n in range(2)]
            rtmp_s = [T(es, "rts%d" % n, [128, 8, 8], F32) for n in range(4)]
            rtmp_b = [T(es, "rtb%d" % n, [128, 4, 64], F32) for n in range(4)]
            qf = T(es, "qf", [128, 512], F32)
            qb = T(es, "qb", [128, 512], BF16)
            qT = T(es, "qT", [64, 1024], BF16)
            ge = T(es, "ge", [128, 24], F32)
            rf = T(es, "rf", [128, 512], F32)
            rqb = T(es, "rqb", [128, 512], BF16)
            rkb = T(es, "rkb", [128, 512], BF16)
            rvb = T(es, "rvb", [128, 512], BF16)
            rTs = [T(es, "rTs%d" % n, [128, 512], BF16) for n in range(2)]
            sge = T(es, "sge", [128, 512], F32)
            sgo = T(es, "sgo", [128, 512], F32)
            pT = [P(es, "pT%d" % n, [128, 1024], BF16) for n in range(2)]
            pT2 = P(es, "pT2", [128, 1024], BF16)
            pW = P(es, "pW", [128, 512], F32)
            pQ = P(es, "pQ", [128, 512], F32)
            pR = P(es, "pR", [128, 2, 512], F32)

            def p1_load(t):
                pr = t % 2
                s.dma("sp", xt[pr][:], xo[t * 128:(t + 1) * 128, :], writes=["xt%d" % pr])
                s.dma("sp", rot[pr][:], ROT[:, 128 + t, :], reads=["ROT"], writes=["rot%d" % pr])

            p1_load(0)
            for t in range(CFG['p1_tiles']):
                i, j = divmod(t, 5)
                pr = t % 2
                if t + 1 < CFG['p1_tiles']:
                    p1_load(t + 1)
                kx, kr, khb, khT, kpT = "xt%d" % pr, "rot%d" % pr, "hb%d" % pr, "hT%d" % pr, "pT%d" % pr
                rmsnorm((junk, ss, rstd), xt[pr][:], kx, grep[:], "grep", hb[pr][:], khb, "1")
                for k in range(8):
                    TR(pT[pr][:, k * 128:(k + 1) * 128], hb[pr][:, k * 128:(k + 1) * 128], identb[:], [khb, "identb"], [kpT])
                CP("act", hT[pr][:].rearrange("p k t -> p (k t)"), pT[pr][:], [kpT], [khT])
                for k in range(8):
                    MM(pW[:, 0:256], hT[pr][:, k, :], Ww[:, k, :], k == 0, k == 7, [khT, "Ww"], ["pW"])
                CP("act", vwb[pr][:], pW[:, 128:256], ["pW"], ["vwb%d" % pr])
                s.dma("sp", VW[i][:, j * 128:(j + 1) * 128], vwb[pr][:], reads=["vwb%d" % pr], writes=["VW%d" % i], waw=False)
                CP("act", kwf[:], pW[:, 0:128], ["pW"], ["kwf"])
                CP("pool", kwb[:], kwf[:], ["kwf"], ["kwb"])
                kv = kwf[:].rearrange("p (g d) -> p g d", g=2)
                ko = kwb[:].rearrange("p (g d) -> p g d", g=2)
                rotary("dve", kv[:, :, 0:8], kv[:, :, 8:16], bc(rot[pr][:, 72:80].unsqueeze(1), [128, 2, 8]),
                       bc(rot[pr][:, 0:8].unsqueeze(1), [128, 2, 8]), ko[:, :, 0:8], ko[:, :, 8:16],
                       [r[:, 0:2, :] for r in rtmp_s], ["kwf", kr, "kwb"], ["kwb"], "s")
                for g in range(2):
                    TR(pT2[0:64, g * 128:(g + 1) * 128], kwb[:, g * 64:(g + 1) * 64], identb[:], ["kwb", "identb"], ["pT2"])
                CP("dve", kwT[pr][:].rearrange("p g t -> p (g t)"), pT2[0:64, 0:256], ["pT2"], ["kwT%d" % pr])
                s.dma("sp", KWT[i].rearrange("p (g t) -> p g t", g=2)[:, :, j * 128:(j + 1) * 128], kwT[pr][:],
                      reads=["kwT%d" % pr], writes=["KWT%d" % i], waw=False)
                if j != 4:
                    continue
                s.dma("sp", HT[i], hT[pr][:].rearrange("p k t -> p (k t)"), reads=[khT], writes=["HT%d" % i])
                for k in range(8):
                    MM(pQ[:], hT[pr][:, k, :], Wq[:, k, :], k == 0, k == 7, [khT, "Wq"], ["pQ"])
                for k in range(8):
                    MM(pW[:, 256:280], hT[pr][:, k, :], Wng[:, k, :], k == 0, k == 7, [khT, "Wng"], ["pW"])
                ACT(qf[:], pQ[:], AF.Copy, ["pQ"], ["qf"], scale=0.125)
                CP("pool", qb[:], qf[:], ["qf"], ["qb"])
                qv = qf[:].rearrange("p (h d) -> p h d", h=8)
                qo = qb[:].rearrange("p (h d) -> p h d", h=8)
                rotary("dve", qv[:, :, 0:8], qv[:, :, 8:16], bc(rot[pr][:, 72:80].unsqueeze(1), [128, 8, 8]),
                       bc(rot[pr][:, 0:8].unsqueeze(1), [128, 8, 8]), qo[:, :, 0:8], qo[:, :, 8:16],
                       [r[:] for r in rtmp_s], ["qf", kr, "qb"], ["qb"], "s")
                for h in range(8):
                    TR(pT2[0:64, h * 128:(h + 1) * 128], qb[:, h * 64:(h + 1) * 64], identb[:], ["qb", "identb"], ["pT2"])
                CP("dve", qT[:], pT2[0:64, :], ["pT2"], ["qT"])
                s.dma("sp", QT[i], qT[:], reads=["qT"], writes=["QT%d" % i])
                ACT(ge[:], pW[:, 256:280], AF.Exp, ["pW"], ["ge"], scale=-1.0)
                TS("dve", ge[:], ge[:], 1.0, None, ALU.add, None, ["ge"], ["ge"])
                RECIP(ge[:], ge[:], ["ge"], ["ge"])
                s.dma("sp", GN[i], ge[:], reads=["ge"], writes=["GN%d" % i])
                for half in range(2):
                    for b in range(2):
                        c0 = (half * 2 + b) * 512
                        for k in range(8):
                            MM(pR[:, b, :], hT[pr][:, k, :], Wr[:, k, c0:c0 + 512], k == 0, k == 7, [khT, "Wr"], ["pR%d" % b])
                    if half == 0:
                        for b, (dst, dsc, kd) in enumerate(((rqb, RQT, "RQT"), (rkb, RKT, "RKT"))):
                            CP("act", rf[:], pR[:, b, :], ["pR%d" % b], ["rf"])
                            rv_ = rf[:].rearrange("p (h d) -> p h d", h=4)
                            ro = dst[:].rearrange("p (h d) -> p h d", h=4)
                            rotary("pool", rv_[:, :, 0:64], rv_[:, :, 64:128], bc(rot[pr][:, 80:144].unsqueeze(1), [128, 4, 64]),
                                   bc(rot[pr][:, 8:72].unsqueeze(1), [128, 4, 64]), ro[:, :, 0:64], ro[:, :, 64:128],
                                   [r[:] for r in rtmp_b], ["rf", kr, "rdst%d" % b], ["rdst%d" % b], "b")
                            for h in range(4):
                                TR(pT2[:, h * 128:(h + 1) * 128], dst[:, h * 128:(h + 1) * 128], identb[:], ["rdst%d" % b, "identb"], ["pT2"])
                            CP("dve", rTs[b][:], pT2[:, 0:512], ["pT2"], ["rTs%d" % b])
                            s.dma("sp", dsc[i], rTs[b][:], reads=["rTs%d" % b], writes=["%s%d" % (kd, i)])
                    else:
                        CP("act", rvb[:], pR[:, 0, :], ["pR0"], ["rvb"])
                        s.dma("sp", RV[i], rvb[:], reads=["rvb"], writes=["RV%d" % i])
                        ACT(sge[:], pR[:, 1, :], AF.Exp, ["pR1"], ["sge"], scale=-1.0)
                        TS("dve", sge[:], sge[:], 1.0, None, ALU.add, None, ["sge"], ["sge"])
                        RECIP(sge[:], sge[:], ["sge"], ["sge"])
                        TT("dve", sgo[:], pR[:, 1, :], sge[:], ALU.mult, ["pR1", "sge"], ["sgo"])
                        s.dma("sp", SG[i], sgo[:], reads=["sgo"], writes=["SG%d" % i])

        if CFG['stop'] < 2:
            raise _Stop()
        s.next_phase()
        with ExitStack() as mid:
            KsTa = [T(mid, "KsTa%d" % g, [128, SEQ], BF16) for g in range(2)]
            Vsa = T(mid, "Vsa", [128, 128, 2, 65], BF16)
            Sown = T(mid, "Sown", [128, 16, 512], BF16)
            kcmpT = T(mid, "kcmpT", [64, 2, 1024], BF16)
            vcmpa = T(mid, "vcmpa", [128, 8, 2, 65], BF16)
            for g in range(2):
                for r8 in range(8):
                    s.dma("pool", KsTa[g][64:128, r8 * 2048:(r8 + 1) * 2048], apat[:, (r8 % 2) * 2048:(r8 % 2 + 1) * 2048],
                          writes=["KsTa%d" % g], waw=False)
            MEMSET("pool", Vsa[:, :, :, 64:65], 1.0, ["Vsa"], waw=False)
            MEMSET("pool", vcmpa[:, :, :, 64:65], 1.0, ["vcmpa"], waw=False)

            with ExitStack() as es:
                W1g = T(es, "W1g", [128, 8, 512], BF16)
                W2g = T(es, "W2g", [128, 8, 1024], BF16)
                grep = T(es, "grep2", [128, D], F32)
                zeta = T(es, "zeta", [128, 4], F32)
                oh = T(es, "oh", [128, 8], F32)
                s.dma("sp", grep[:], gmix, writes=["grep2"])
                s.dma("sp", zeta[:], zetad, writes=["zeta"])
                s.dma("sp", oh[:], ohd, writes=["oh"])
                wload(W1g, w_in[:, C_KC:C_KC + 512], "W1g")
                wload(W2g, w_in[:, C_RK:C_RK + 1024], "W2g")
                xt = [T(es, "gxt%d" % n, [128, D], F32) for n in range(2)]
                rot3 = [T(es, "grot%d" % n, [128, 144], F32) for n in range(3)]
                hb = [T(es, "ghb%d" % n, [128, D], BF16) for n in range(2)]
                hT = [T(es, "ghT%d" % n, [128, 8, 128], BF16) for n in range(2)]
                junk = T(es, "gjunk", [128, D], F32)
                ss = T(es, "gss", [128, 1], F32)
                rstd = T(es, "grstd", [128, 1], F32)
                abf = T(es, "abf", [128, 384], BF16)
                af = T(es, "af", [128, 384], F32)
                kvst = [T(es, "kvst%d" % n, [128, 256], BF16) for n in range(2)]
                rkf = T(es, "rkf", [128, 512], F32)
                rkb = T(es, "grkb", [128, 512], BF16)
                rvz = T(es, "rvz", [128, 512], BF16)
                rtmp_s = [T(es, "grts%d" % n, [128, 2, 2, 8], F32) for n in range(4)]
                rtmp_b = [T(es, "grtb%d" % n, [128, 4, 64], F32) for n in range(4)]
                Sst = T(es, "Sst", [128, 512], F32)
                acc = T(es, "acc", [128, 512], F32)
                zpad = T(es, "zpad", [128, 16], BF16)
                pT = [P(es, "gpT%d" % n, [128, 1024], BF16) for n in range(2)]
                pT2 = P(es, "gpT2", [128, 1024], BF16)
                pA = P(es, "pA", [128, 512], F32)
                pB = P(es, "pB", [128, 2, 512], F32)
                pL = P(es, "pL", [128, 512], F32)
                MEMSET("pool", Sst[:], 0.0, ["Sst"])
                MEMSET("pool", zpad[:], 0.0, ["zpad"])
                s.dma("sp", KCT[:, SEQ:SEQ + 16], zpad[:], reads=["zpad"], writes=["KCT"], waw=False)
                s.dma("sp", VCT[:, SEQ:SEQ + 16], zpad[:], reads=["zpad"], writes=["VCT"], waw=False)

                def p2_load(t):
                    pr = t % 2
                    s.dma("sp", xt[pr][:], xg[t * 128:(t + 1) * 128, :], writes=["gxt%d" % pr])
                    s.dma("sp", rot3[t % 3][:], ROT[:, t, :], reads=["ROT"], writes=["grot%d" % (t % 3)])

                def p2_keys(t):
                    pr = t % 2
                    return pr, "gxt%d" % pr, "grot%d" % (t % 3), "ghb%d" % pr, "ghT%d" % pr, "gpT%d" % pr

                def p2_A1(t):
                    pr, kx, kr, khb, khT, kpT = p2_keys(t)
                    rmsnorm((junk, ss, rstd), xt[pr][:], kx, grep[:], "grep2", hb[pr][:], khb, "2")

                def p2_A2(t):
                    pr, kx, kr, khb, khT, kpT = p2_keys(t)
                    for k in range(8):
                        TR(pT[pr][:, k * 128:(k + 1) * 128], hb[pr][:, k * 128:(k + 1) * 128], identb[:], [khb, "identb"], [kpT])
                    CP("act", hT[pr][:].rearrange("p k t -> p (k t)"), pT[pr][:], [kpT], [khT])

                def p2_B1(t):
                    pr, kx, kr, khb, khT, kpT = p2_keys(t)
                    for k in range(8):
                        MM(pA[:], hT[pr][:, k, :], W1g[:, k, :], k == 0, k == 7, [khT, "W1g"], ["pA"])
                    for b in range(2):
                        for k in range(8):
                            MM(pB[:, b, :], hT[pr][:, k, :], W2g[:, k, b * 512:(b + 1) * 512], k == 0, k == 7, [khT, "W2g"], ["pB%d" % b])

                def p2_B2(t):
                    pr, kx, kr, khb, khT, kpT = p2_keys(t)
                    if CFG.get('p2_cut', 99) < 3:
                        return
                    CP("act", abf[:], pA[:, 0:384], ["pA"], ["abf"])
                    CP("act", Vsa[:, t, :, 0:64], pA[:, 384:512].rearrange("p (g d) -> p g d", g=2), ["pA"], ["Vsa"], waw=False)
                    if CFG.get('p2_cut', 99) < 4:
                        return
                    CP("act", af[:], pA[:, 0:384], ["pA"], ["af"])
                    for a_ in (0, 2):
                        av = af[:, a_ * 128:(a_ + 1) * 128].rearrange("p (g d) -> p g d", g=2)
                        ao = abf[:, a_ * 128:(a_ + 1) * 128].rearrange("p (g d) -> p g d", g=2)
                        rotary("dve", av[:, :, 0:8], av[:, :, 8:16], bc(rot3[t % 3][:, 72:80].unsqueeze(1), [128, 2, 8]),
                               bc(rot3[t % 3][:, 0:8].unsqueeze(1), [128, 2, 8]), ao[:, :, 0:8], ao[:, :, 8:16],
                               [r[:, 0, :, :] for r in rtmp_s], ["af", kr, "abf"], ["abf"], "gs")
                    if CFG.get('p2_cut', 99) < 5:
                        return
                    TR(pT2[:, 0:128], abf[:, 0:128], identb[:], ["abf", "identb"], ["gpT2"])
                    TR(pT2[:, 128:256], abf[:, 128:256], identb[:], ["abf", "identb"], ["gpT2"])
                    TR(pT2[0:64, 256:384], abf[:, 256:320], identb[:], ["abf", "identb"], ["gpT2"])
                    TR(pT2[0:64, 384:512], abf[:, 320:384], identb[:], ["abf", "identb"], ["gpT2"])
                    if CFG.get('p2_sub', 99) < 1:
                        return
                    CP("act", kvst[pr][:], pT2[:, 0:256], ["gpT2"], ["kvst%d" % pr])
                    if CFG.get('p2_sub', 99) < 2:
                        return
                    s.dma("sp", KCT[:, t * 128:(t + 1) * 128], kvst[pr][:, 0:128], reads=["kvst%d" % pr], writes=["KCT"], waw=False)
                    s.dma("sp", VCT[:, t * 128:(t + 1) * 128], kvst[pr][:, 128:256], reads=["kvst%d" % pr], writes=["VCT"], waw=False)
                    if CFG.get('p2_sub', 99) < 3:
                        return
                    CP("act", KsTa[0][0:64, t * 128:(t + 1) * 128], pT2[0:64, 256:384], ["gpT2"], ["KsTa0"], waw=False)
                    CP("act", KsTa[1][0:64, t * 128:(t + 1) * 128], pT2[0:64, 384:512], ["gpT2"], ["KsTa1"], waw=False)
                    if CFG.get('p2_cut', 99) < 6:
                        return
                    CP("act", rkf[:], pB[:, 0, :], ["pB0"], ["rkf"])
                    rv_ = rkf[:].rearrange("p (h d) -> p h d", h=4)
                    ro = rkb[:].rearrange("p (h d) -> p h d", h=4)
                    rotary("pool", rv_[:, :, 0:64], rv_[:, :, 64:128], bc(rot3[t % 3][:, 80:144].unsqueeze(1), [128, 4, 64]),
                           bc(rot3[t % 3][:, 8:72].unsqueeze(1), [128, 4, 64]), ro[:, :, 0:64], ro[:, :, 64:128],
                           [r[:] for r in rtmp_b], ["rkf", kr, "grkb"], ["grkb"], "gb")
                    for h in range(4):
                        TS("dve", rvz[:, h * 128:(h + 1) * 128], pB[:, 1, h * 128:(h + 1) * 128], zeta[:, h:h + 1], None, ALU.mult, None,
                           ["pB1", "zeta"], ["rvz"], waw=(h == 0))
                    if CFG.get('p2_cut', 99) < 7:
                        return
                    m = t % 8
                    if m == 0:
                        TS("dve", acc[:], Sst[:], oh[:, 0:1], None, ALU.mult, None, ["Sst", "oh"], ["acc"])
                    else:
                        STT("dve", acc[:], Sst[:], oh[:, m:m + 1], acc[:], ALU.mult, ALU.add, ["Sst", "oh", "acc"], ["acc"])
                    if m == 7:
                        CP("act", Sown[:, t // 8, :], acc[:], ["acc"], ["Sown"], waw=False)
                    if CFG.get('p2_cut', 99) < 8:
                        return
                    for h in range(4):
                        MM(pL[:, h * 128:(h + 1) * 128], rkb[:, h * 128:(h + 1) * 128], rvz[:, h * 128:(h + 1) * 128],
                           True, True, ["grkb", "rvz"], ["pL"])
                    for h in range(4):
                        STT("dve", Sst[:, h * 128:(h + 1) * 128], Sst[:, h * 128:(h + 1) * 128], float(GAM[h] ** 128),
                            pL[:, h * 128:(h + 1) * 128], ALU.mult, ALU.add, ["Sst", "pL"], ["Sst"])


                NT2 = CFG['p2_tiles']
                for t0 in range(min(2, NT2)):
                    p2_load(t0)
                if NT2 > 0:
                    p2_A1(0)
                    p2_A2(0)
                for t in range(NT2):
                    if t + 2 < NT2:
                        p2_load(t + 2)
                    if t + 1 < NT2:
                        p2_A1(t + 1)
                    p2_B1(t)
                    if t + 1 < NT2:
                        p2_A2(t + 1)
                    p2_B2(t)
            if CFG['stop'] < 3:
                raise _Stop()
            s.next_phase()
            with ExitStack() as es:
                w1 = [T(es, "w1_%d" % n, [128, 32, 256], BF16) for n in range(2)]
                w2 = [T(es, "w2_%d" % n, [128, 2, 64], BF16) for n in range(2)]
                peT = [T(es, "peT%d" % n, [64, 32], BF16) for n in range(2)]
                bia = T(es, "bia", [128, 4], F32)
                nbia = T(es, "nbia", [128, 4], F32)
                src = [T(es, "csrc%d" % n, [128, 4112], BF16) for n in range(2)]
                hid = T(es, "hid", [128, 2, 2, 256], BF16)
                ex = T(es, "cex", [128, 256], F32)
                pC = [P(es, "pC%d" % n, [128, 512], F32) for n in range(2)]
                pK = P(es, "pK", [128, 512], F32)
                pBi = P(es, "pBi", [128, 512], F32)
                for kind, (wd1, wd2, ped) in enumerate(((ckw1, ckw2, pekT), (cvw1, cvw2, pevT))):
                    w1v = wd1.rearrange("(l d) h -> d l h", d=64)
                    for half in range(2):
                        for l4 in range(4):
                            s.dma("pool", w1[kind][half * 64:(half + 1) * 64, l4 * 8:(l4 + 1) * 8, :], w1v[:, l4 * 8:(l4 + 1) * 8, :],
                                  writes=["w1_%d" % kind], waw=False)
                    s.dma("pool", w2[kind][:], wd2.rearrange("(c p) d -> p c d", p=128), writes=["w2_%d" % kind])
                    s.dma("pool", peT[kind][:], ped, writes=["peT%d" % kind])
                    for hc in range(2):
                        for l in range(32):
                            MM(pBi[:, kind * 2 + hc:kind * 2 + hc + 1], w1[kind][0:64, l, hc * 128:(hc + 1) * 128], peT[kind][:, l:l + 1],
                               l == 0 and kind == 0 and hc == 0, l == 31, ["w1_%d" % kind, "peT%d" % kind], ["pBi"], skip=True)
                CP("dve", bia[:], pBi[:, 0:4], ["pBi"], ["bia"])
                TS("dve", nbia[:], bia[:], -1.0, None, ALU.mult, None, ["bia"], ["nbia"])
                cnt = 0
                for qd in range(CFG.get('p2b_q', 4)):
                    for kind, SRC in enumerate((KCT, VCT)):
                        s.dma("sp", src[kind][:], SRC[:, qd * 4096:qd * 4096 + 4112], reads=["KCT", "VCT"], writes=["csrc%d" % kind])
                        for g in range(2):
                            for hc in range(2):
                                pc = pC[cnt % 2]
                                kpc = "pC%d" % (cnt % 2)
                                cnt += 1
                                for l in range(32):
                                    MM(pc[:, 0:256], w1[kind][g * 64:(g + 1) * 64, l, hc * 128:(hc + 1) * 128],
                                       src[kind][g * 64:(g + 1) * 64, l:l + 4081:16], l == 0, l == 31,
                                       ["w1_%d" % kind, "csrc%d" % kind], [kpc])
                                col = kind * 2 + hc
                                ACT(ex[:], pc[:, 0:256], AF.Exp, [kpc, "nbia"], ["cex"], scale=-1.0, bias=nbia[:, col:col + 1])
                                TS("dve", ex[:], ex[:], 1.0, None, ALU.add, None, ["cex"], ["cex"])
                                RECIP(ex[:], ex[:], ["cex"], ["cex"])
                                STT("dve", hid[:, g, hc, :], pc[:, 0:256], bia[:, col:col + 1], ex[:], ALU.add, ALU.mult,
                                    [kpc, "bia", "cex"], ["hid"], waw=False)
                        if kind == 0:
                            for g in range(2):
                                for hc in range(2):
                                    MM(pK[0:64, g * 256:(g + 1) * 256], w2[0][:, hc, :], hid[:, g, hc, :], hc == 0 and g == 0, hc == 1,
                                       ["w2_0", "hid"], ["pK"], skip=True)
                            CP("act", kcmpT[:, :, qd * 256:(qd + 1) * 256], pK[0:64, :].rearrange("p (g b) -> p g b", g=2),
                               ["pK"], ["kcmpT"], waw=False)
                        else:
                            for g in range(2):
                                for bch in range(2):
                                    for hc in range(2):
                                        MM(pK[:, (g * 2 + bch) * 64:(g * 2 + bch + 1) * 64], hid[:, g, hc, bch * 128:(bch + 1) * 128],
                                           w2[1][:, hc, :], hc == 0 and g == 0 and bch == 0, hc == 1, ["w2_1", "hid"], ["pK"], skip=True)
                            for bch in range(2):
                                CP("act", vcmpa[:, qd * 2 + bch, :, 0:64],
                                   pK[:, 0:256].rearrange("p (g b d) -> p g b d", g=2, b=2)[:, :, bch, :], ["pK"], ["vcmpa"], waw=False)

            if CFG['stop'] < 4:
                raise _Stop()
            s.next_phase()
            with ExitStack() as es:
                ov = T(es, "ov", [128, 8, 256], BF16)
                diag = T(es, "diag", [128, 8, 128], F32)
                winS = T(es, "winS", [128, 5, 128], F32)
                win0 = T(es, "win0", [128, 5, 128], F32)
                xiT = T(es, "xiT", [128, 512], F32)
                dmatT = T(es, "dmatT", [128, 512], F32)
                s.dma("pool", ov[:].rearrange("p m j -> p (m j)"), ovd, writes=["ov"])
                s.dma("sp", diag[:].rearrange("p m j -> p (m j)"), diagmask, writes=["diag"])
                s.dma("sp", winS[:].rearrange("p m j -> p (m j)"), winmaskS, writes=["winS"])
                s.dma("sp", win0[:].rearrange("p m j -> p (m j)"), winmask0, writes=["win0"])
                s.dma("sp", xiT[:], xiTd, writes=["xiT"])
                s.dma("sp", dmatT[:], dmatTd, writes=["dmatT"])
                qt = [T(es, "qt%d" % n, [64, 1024], BF16) for n in range(2)]
                gn = [T(es, "gn%d" % n, [128, 24], F32) for n in range(2)]
                kwt = [T(es, "kwt%d" % n, [64, 2, 640], BF16) for n in range(2)]
                vwa = [T(es, "vwa%d" % n, [128, 5, 2, 65], BF16) for n in range(2)]
                bimp = [T(es, "bimp%d" % n, [128, 256], F32) for n in range(2)]
                cmk = [T(es, "cmk%d" % n, [128, 2, 128], F32) for n in range(2)]
                rqT = [T(es, "rqT%d" % n, [128, 512], BF16) for n in range(2)]
                rkT = [T(es, "rkT%d" % n, [128, 512], BF16) for n in range(2)]
                rvt = [T(es, "rvt%d" % n, [128, 512], BF16) for n in range(2)]
                sgt = [T(es, "sgt%d" % n, [128, 512], F32) for n in range(2)]
                QA = [T(es, "QA%d" % n, [128, 4, 512], BF16) for n in range(2)]
                PT = [T(es, "PT%d" % n, [128, 512], BF16) for n in range(2)]
                oT = T(es, "oT", [65, 512], F32)
                den = T(es, "den", [128, 4], F32)
                coef = T(es, "coef", [128, 4], F32)
                nsa = T(es, "nsa", [128, 512], F32)
                nsab = T(es, "nsab", [128, 512], BF16)
                nsaTs = T(es, "nsaTs", [128, 512], BF16)
                iacc = T(es, "iacc", [128, 256], F32)
                itmp = T(es, "itmp", [128, 256], F32)
                m8a = T(es, "m8a", [128, 8], F32)
                m8b = T(es, "m8b", [128, 8], F32)
                thr = T(es, "thr", [128, 1], F32)
                nsp = T(es, "nsp", [128, 320], BF16)
                PTr = T(es, "PTr", [128, 512], BF16)
                rqx = T(es, "rqx", [128, 512], BF16)
                ss4 = T(es, "ss4", [128, 4], F32)
                rjunk = T(es, "rjunk", [128, 512], F32)
                retb = T(es, "retb", [128, 512], BF16)
                retTs = T(es, "retTs", [128, 512], BF16)
                pS = [P(es, "pS%d" % n, [128, 512], F32) for n in range(2)]
                pO = P(es, "pO", [128, 512], F32)
                pImp = P(es, "pImp", [128, 4, 256], F32)
                pTr = P(es, "pTr", [128, 512], F32)
                pRS = P(es, "pRS", [128, 512], F32)
                pRO = P(es, "pRO", [128, 512], F32)
                pTrb = pTr.bitcast(BF16) if hasattr(pTr, "bitcast") else None
                for n in range(2):
                    MEMSET("pool", vwa[n][:, :, :, 64:65], 1.0, ["vwa%d" % n], waw=False)
                MEMSET("pool", nsp[:], 0.0, ["nsp"])

                def p3_load(i):
                    pr = i % 2
                    s.dma("sp", qt[pr][:], QT[i], reads=["QT%d" % i], writes=["qt%d" % pr])
                    s.dma("sp", gn[pr][:], GN[i], reads=["GN%d" % i], writes=["gn%d" % pr])
                    s.dma("sp", kwt[pr][:].rearrange("p g t -> p (g t)"), KWT[i], reads=["KWT%d" % i], writes=["kwt%d" % pr])
                    s.dma("sp", vwa[pr][:, :, :, 0:64], VW[i].rearrange("p (j g d) -> p j g d", j=5, g=2),
                          reads=["VW%d" % i], writes=["vwa%d" % pr], waw=False)
                    s.dma("sp", bimp[pr][:], biasimp[i], writes=["bimp%d" % pr])
                    s.dma("sp", cmk[pr][:].rearrange("p a r -> p (a r)"), cmpmask[i], writes=["cmk%d" % pr])
                    s.dma("sp", rqT[pr][:], RQT[i], reads=["RQT%d" % i], writes=["rqT%d" % pr])
                    s.dma("sp", rkT[pr][:], RKT[i], reads=["RKT%d" % i], writes=["rkT%d" % pr])
                    s.dma("sp", rvt[pr][:], RV[i], reads=["RV%d" % i], writes=["rvt%d" % pr])
                    s.dma("sp", sgt[pr][:], SG[i], reads=["SG%d" % i], writes=["sgt%d" % pr])

                step = [0]

                def score_chunk(lhsT, rhs, R, mask, maskR, meng="dve"):
                    n = step[0] % 2
                    step[0] += 1
                    MM(pS[n][:], lhsT, rhs, True, True, R, ["pS%d" % n])
                    ACT(PT[n][:], pS[n][:], AF.Exp, ["pS%d" % n], ["PT%d" % n])
                    if mask is not None:
                        pv = PT[n][:].rearrange("p (h q) -> p h q", h=4)
                        TT(meng, pv, pv, bc(mask.unsqueeze(1), [128, 4, 128]), ALU.mult, ["PT%d" % n] + maskR, ["PT%d" % n])
                    return PT[n], "PT%d" % n

                def finish_branch(br, g, pr):
                    CP("act", oT[:], pO[0:65, :], ["pO"], ["oT"])
                    for h in range(4):
                        TR(pTr[:, h * 65:(h + 1) * 65], oT[0:65, h * 128:(h + 1) * 128], identf[0:65, 0:65], ["oT", "identf"], ["pTr"])
                    pv = pTr[:, 0:260].rearrange("p (h e) -> p h e", h=4)
                    TS("dve", den[:], pv[:, :, 64], 1e-30, None, ALU.max, None, ["pTr"], ["den"])
                    RECIP(den[:], den[:], ["den"], ["den"])
                    gv = gn[pr][:].rearrange("p (h b) -> p h b", h=8)
                    TT("dve", coef[:], den[:], gv[:, 4 * g:4 * g + 4, br], ALU.mult, ["den", "gn%d" % pr], ["coef"])
                    for h in range(4):
                        dst = nsa[:, (4 * g + h) * 64:(4 * g + h + 1) * 64]
                        if br == 0:
                            TS("dve", dst, pv[:, h, 0:64], coef[:, h:h + 1], None, ALU.mult, None, ["pTr", "coef"], ["nsa"], waw=False)
                        else:
                            STT("dve", dst, pv[:, h, 0:64], coef[:, h:h + 1], dst, ALU.mult, ALU.add, ["pTr", "coef", "nsa"], ["nsa"])

                p3_load(0)
                for i in range(CFG['p3_i']):
                    pr = i % 2
                    if i + 1 < CFG['p3_i']:
                        p3_load(i + 1)
                    nkc = 8 * i + 8
                    nv = (nkc - 1) // 32 + 1
                    ncm = i // 2 + 1
                    for g in range(2):
                        qa = QA[g]
                        kqa = "QA%d" % g
                        for v in range(nv):
                            CP("pool", qa[0:64, v, :], qt[pr][:, g * 512:(g + 1) * 512], ["qt%d" % pr], [kqa], waw=(v == 0))
                        pend = None
                        for m in range(ncm):
                            mask, maskR = None, []
                            if m >= ncm - 2:
                                mask, maskR = cmk[pr][:, m - (ncm - 2), :], ["cmk%d" % pr]
                            ptile, kpt = score_chunk(kcmpT[:, g, m * 128:(m + 1) * 128], qa[0:64, 0, :], ["kcmpT", kqa], mask, maskR)
                            if pend is not None:
                                pend()

                            def pend(m=m, ptile=ptile, kpt=kpt):
                                MM(pO[0:65, :], vcmpa[:, m, g, :], ptile[:], m == 0, m == ncm - 1, ["vcmpa", kpt], ["pO"])
                                for h in range(4):
                                    MM(pImp[:, h, :], ptile[:, h * 128:(h + 1) * 128], ov[:, m, :], m == 0 and h % 2 == 0, m == ncm - 1,
                                       [kpt, "ov"], ["pImp"], skip=True)
                        pend()
                        finish_branch(0, g, pr)
                        for h in range(4):
                            STT("dve", iacc[:], pImp[:, h, :], den[:, h:h + 1], bimp[pr][:] if h == 0 else iacc[:], ALU.mult, ALU.add,
                                ["pImp", "den", "bimp%d" % pr, "iacc"], ["iacc"])
                        s.op("dve", lambda e: e.max(out=m8a[:], in_=iacc[:]), ["iacc"], ["m8a"])
                        s.op("dve", lambda e: e.match_replace(out=itmp[:], in_to_replace=m8a[:], in_values=iacc[:], imm_value=-1e30),
                             ["iacc", "m8a"], ["itmp"])
                        s.op("dve", lambda e: e.max(out=m8b[:], in_=itmp[:]), ["itmp"], ["m8b"])
                        TS("dve", thr[:], m8b[:, 7:8], -64.0, None, ALU.max, None, ["m8b"], ["thr"])
                        TS("dve", nsp[:, 64:320], iacc[:], thr[:, 0:1], 1.0, ALU.is_ge, ALU.subtract, ["iacc", "thr"], ["nsp"])
                        wm = win0 if i == 0 else winS
                        pend = None
                        for j in range(5):
                            ptile, kpt = score_chunk(kwt[pr][:, g, j * 128:(j + 1) * 128], qa[0:64, 0, :], ["kwt%d" % pr, kqa],
                                                     wm[:, j, :], ["win0" if i == 0 else "winS"], meng="pool")
                            if pend is not None:
                                pend()

                            def pend(j=j, ptile=ptile, kpt=kpt):
                                MM(pO[0:65, :], vwa[pr][:, j, g, :], ptile[:], j == 0, j == 4, ["vwa%d" % pr, kpt], ["pO"])
                        pend()
                        for v in range(nv):
                            TR(pTrb[:, v * 128:(v + 1) * 128], nsp[:, 64 * v:64 * v + 128], identb[:], ["nsp", "identb"], ["pTr"])
                        for v in range(nv):
                            CP("act", qa[64:128, v, :].rearrange("p (h q) -> p h q", h=4),
                               bc(pTrb[64:128, v * 128:(v + 1) * 128].unsqueeze(1), [64, 4, 128]), ["pTr"], [kqa], waw=False)
                        finish_branch(2, g, pr)
                        pend = None
                        for kc in range(nkc):
                            mask, maskR = None, []
                            if kc >= nkc - 8:
                                mask, maskR = diag[:, kc - (nkc - 8), :], ["diag"]
                            ptile, kpt = score_chunk(KsTa[g][:, kc * 128:(kc + 1) * 128], qa[:, kc // 32, :], ["KsTa%d" % g, kqa], mask, maskR)
                            if pend is not None:
                                pend()

                            def pend(kc=kc, ptile=ptile, kpt=kpt):
                                MM(pO[0:65, :], Vsa[:, kc, g, :], ptile[:], kc == 0, kc == nkc - 1, ["Vsa", kpt], ["pO"])
                        pend()
                        finish_branch(1, g, pr)
                    CP("pool", nsab[:], nsa[:], ["nsa"], ["nsab"])
                    for k in range(4):
                        TR(pTrb[:, k * 128:(k + 1) * 128], nsab[:, k * 128:(k + 1) * 128], identb[:], ["nsab", "identb"], ["pTr"])
                    CP("act", nsaTs[:], pTrb[:, 0:512], ["pTr"], ["nsaTs"])
                    s.dma("sp", NSAT[i], nsaTs[:], reads=["nsaTs"], writes=["NSAT%d" % i])
                    for h in range(4):
                        MM(pRS[:, h * 128:(h + 1) * 128], rkT[pr][:, h * 128:(h + 1) * 128], rqT[pr][:, h * 128:(h + 1) * 128],
                           True, True, ["rkT%d" % pr, "rqT%d" % pr], ["pRS"])
                    TT("dve", PTr[:], pRS[:], dmatT[:], ALU.mult, ["pRS", "dmatT"], ["PTr"])
                    TT("pool", rqx[:], rqT[pr][:], xiT[:], ALU.mult, ["rqT%d" % pr, "xiT"], ["rqx"])
                    for h in range(4):
                        hs = slice(h * 128, (h + 1) * 128)
                        MM(pRO[:, hs], PTr[:, hs], rvt[pr][:, hs], True, False, ["PTr", "rvt%d" % pr], ["pRO"], skip=True)
                        MM(pRO[:, hs], rqx[:, hs], Sown[:, i, hs], False, True, ["rqx", "Sown"], ["pRO"], skip=True)
                    for h in range(4):
                        ACT(rjunk[:, h * 128:(h + 1) * 128], pRO[:, h * 128:(h + 1) * 128], AF.Square, ["pRO"], ["rjunk", "ss4"], waw=(h == 0),
                            accum_out=ss4[:, h:h + 1])
                    ACT(ss4[:], ss4[:], AF.Ln, ["ss4", "epsb"], ["ss4"], scale=1.0 / 128, bias=epsb[:])
                    ACT(ss4[:], ss4[:], AF.Exp, ["ss4"], ["ss4"], scale=-0.5)
                    for h in range(4):
                        hs = slice(h * 128, (h + 1) * 128)
                        STT("dve", retb[:, hs], pRO[:, hs], ss4[:, h:h + 1], sgt[pr][:, hs], ALU.mult, ALU.mult,
                            ["pRO", "ss4", "sgt%d" % pr], ["retb"], waw=False)
                    for k in range(4):
                        TR(pTrb[:, k * 128:(k + 1) * 128], retb[:, k * 128:(k + 1) * 128], identb[:], ["retb", "identb"], ["pTr"])
                    CP("act", retTs[:], pTrb[:, 0:512], ["pTr"], ["retTs"])
                    s.dma("sp", RETT[i], retTs[:], reads=["retTs"], writes=["RETT%d" % i])

        if CFG['stop'] < 5:
            raise _Stop()
        s.next_phase()
        with ExitStack() as es:
            Wga = T(es, "Wga", [128, 8, 2048], BF16)
            Wpa = T(es, "Wpa", [128, 4, 1024], BF16)
            Wpb = T(es, "Wpb", [128, 4, 1024], BF16)
            Wo = T(es, "Wo", [128, 8, 1024], BF16)
            grep = T(es, "grep4", [128, D], F32)
            s.dma("sp", grep[:], gmlp, writes=["grep4"])
            wload(Wga, w_in[:, C_GA:C_GA + 2048], "Wga")
            wload(Wpa, wpa, "Wpa")
            wload(Wpb, wpb, "Wpb")
            wload(Wo, wout, "Wo")
            hTg = [T(es, "hTg%d" % n, [128, 8, 512], BF16) for n in range(2)]
            nsT = [T(es, "nsT%d" % n, [128, 4, 512], BF16) for n in range(2)]
            reT = [T(es, "reT%d" % n, [128, 4, 512], BF16) for n in range(2)]
            sig = [T(es, "sig%d" % n, [128, 512], F32) for n in range(2)]
            mxa = T(es, "mxa", [128, 512], F32)
            mixT = T(es, "mixT", [128, 8, 512], BF16)
            xo_t = [T(es, "xo_t%d" % n, [128, D], F32) for n in range(2)]
            xm = [T(es, "xm%d" % n, [128, D], F32) for n in range(2)]
            junk = T(es, "junk4", [128, D], F32)
            ss = T(es, "ss4_", [128, 1], F32)
            rstd = T(es, "rstd4", [128, 1], F32)
            h2b = T(es, "h2b", [128, D], BF16)
            h2T = [T(es, "h2Ts%d" % n, [128, 1024], BF16) for n in range(2)]
            pG = [P(es, "pG%d" % n, [128, 512], F32) for n in range(2)]
            pPr = [P(es, "pPr%d" % n, [128, 512], F32) for n in range(2)]
            pX = P(es, "pX", [128, 2, 512], F32)
            pT = P(es, "pT4", [128, 1024], BF16)

            def p4_load(G):
                pr = G % 2
                for u in range(4):
                    i = 4 * G + u
                    s.dma("sp", hTg[pr][:, :, u * 128:(u + 1) * 128], HT[i].rearrange("p (k t) -> p k t", k=8),
                          reads=["HT%d" % i], writes=["hTg%d" % pr], waw=False)
                    s.dma("sp", nsT[pr][:, :, u * 128:(u + 1) * 128], NSAT[i].rearrange("p (k t) -> p k t", k=4),
                          reads=["NSAT%d" % i], writes=["nsT%d" % pr], waw=False)
                    s.dma("sp", reT[pr][:, :, u * 128:(u + 1) * 128], RETT[i].rearrange("p (k t) -> p k t", k=4),
                          reads=["RETT%d" % i], writes=["reT%d" % pr], waw=False)

            p4_load(0)
            for G in range(CFG.get('p4_g', 4)):
                pr = G % 2
                if G + 1 < CFG.get('p4_g', 4):
                    p4_load(G + 1)
                for oc in range(8):
                    for ab, (Wp, srcT, ksrc) in enumerate(((Wpa, nsT, "nsT"), (Wpb, reT, "reT"))):
                        c0 = ab * 1024 + oc * 128
                        for k in range(8):
                            MM(pG[ab][:], Wga[:, k, c0:c0 + 128], hTg[pr][:, k, :], k == 0, k == 7, ["Wga", "hTg%d" % pr], ["pG%d" % ab])
                        for k in range(4):
                            MM(pPr[ab][:], Wp[:, k, oc * 128:(oc + 1) * 128], srcT[pr][:, k, :], k == 0, k == 3,
                               ["Wpa", "Wpb", "%s%d" % (ksrc, pr)], ["pPr%d" % ab])
                        ACT(sig[ab][:], pG[ab][:], AF.Exp, ["pG%d" % ab], ["sig%d" % ab], scale=-1.0)
                        TS("dve", sig[ab][:], sig[ab][:], 1.0, None, ALU.add, None, ["sig%d" % ab], ["sig%d" % ab])
                        RECIP(sig[ab][:], sig[ab][:], ["sig%d" % ab], ["sig%d" % ab])
                    TT("dve", mxa[:], pPr[0][:], sig[0][:], ALU.mult, ["pPr0", "sig0"], ["mxa"])
                    TT("dve", sig[1][:], pPr[1][:], sig[1][:], ALU.mult, ["pPr1", "sig1"], ["sig1"])
                    TT("pool", mixT[:, oc, :], mxa[:], sig[1][:], ALU.add, ["mxa", "sig1"], ["mixT"], waw=False)
                for u in range(4):
                    i = 4 * G + u
                    p2_ = i % 2
                    s.dma("sp", xo_t[p2_][:], xo[(5 * i + 4) * 128:(5 * i + 5) * 128, :], writes=["xo_t%d" % p2_])
                    for hf in range(2):
                        for oc in range(8):
                            MM(pX[:, hf, :], mixT[:, oc, u * 128:(u + 1) * 128], Wo[:, oc, hf * 512:(hf + 1) * 512], oc == 0, oc == 7,
                               ["mixT", "Wo"], ["pX%d" % hf])
                    TT("dve", xm[p2_][:], pX[:].rearrange("p a c -> p (a c)"), xo_t[p2_][:], ALU.add, ["pX0", "pX1", "xo_t%d" % p2_], ["xm%d" % p2_])
                    s.dma("sp", XMID[i], xm[p2_][:], reads=["xm%d" % p2_], writes=["XMID%d" % i])
                    rmsnorm((junk, ss, rstd), xm[p2_][:], "xm%d" % p2_, grep[:], "grep4", h2b[:], "h2b", "4")
                    for k in range(8):
                        TR(pT[:, k * 128:(k + 1) * 128], h2b[:, k * 128:(k + 1) * 128], identb[:], ["h2b", "identb"], ["pT4"])
                    CP("act", h2T[p2_][:], pT[:], ["pT4"], ["h2Ts%d" % p2_])
                    s.dma("sp", H2T[i], h2T[p2_][:], reads=["h2Ts%d" % p2_], writes=["H2T%d" % i])

        if CFG['stop'] < 6:
            raise _Stop()
        s.next_phase()
        with ExitStack() as es:
            Wu = T(es, "Wu", [128, 8, 4096], BF16)
            Wd = T(es, "Wd", [128, 32, 1024], BF16)
            grep = T(es, "grep5", [128, D], F32)
            s.dma("sp", grep[:], gfin, writes=["grep5"])
            for q4 in range(4):
                for k in range(8):
                    s.dma("pool", Wu[:, k, q4 * 1024:(q4 + 1) * 1024], wup[k * 128:(k + 1) * 128, q4 * 1024:(q4 + 1) * 1024],
                          writes=["Wu%d" % q4], waw=False)
                for hc in range(q4 * 8, q4 * 8 + 8):
                    s.dma("pool", Wd[:, hc, :], wdown[hc * 128:(hc + 1) * 128, :], writes=["Wd%d" % q4], waw=False)
            h2g = [T(es, "h2g%d" % n, [128, 8, 256], BF16) for n in range(2)]
            ur = [T(es, "ur%d" % n, [128, 256], F32) for n in range(2)]
            uT = [T(es, "uT%d" % n, [128, 256], BF16) for n in range(2)]
            xmr = [T(es, "xmr%d" % n, [128, D], F32) for n in range(2)]
            xf = [T(es, "xf%d" % n, [128, D], F32) for n in range(2)]
            junk = T(es, "junk5", [128, D], F32)
            ss = T(es, "ss5", [128, 1], F32)
            rstd = T(es, "rstd5", [128, 1], F32)
            pU = [P(es, "pU%d" % n, [128, 512], F32) for n in range(2)]
            pD = P(es, "pD", [128, 4, 512], F32)

            def p5_load(G):
                pr = G % 2
                for u in range(2):
                    i = 2 * G + u
                    s.dma("sp", h2g[pr][:, :, u * 128:(u + 1) * 128], H2T[i].rearrange("p (k t) -> p k t", k=8),
                          reads=["H2T%d" % i], writes=["h2g%d" % pr], waw=False)

            p5_load(0)
            for G in range(CFG.get('p5_g', 8)):
                pr = G % 2
                if G + 1 < CFG.get('p5_g', 8):
                    p5_load(G + 1)
                for hc in range(32):
                    n = hc % 2
                    for k in range(8):
                        MM(pU[n][:, 0:256], Wu[:, k, hc * 128:(hc + 1) * 128], h2g[pr][:, k, :], k == 0, k == 7, ["Wu%d" % (hc // 8), "h2g%d" % pr], ["pU%d" % n])
                    ACT(ur[n][:], pU[n][:, 0:256], AF.Relu, ["pU%d" % n], ["ur%d" % n])
                    TT("dve", uT[n][:], ur[n][:], ur[n][:], ALU.mult, ["ur%d" % n], ["uT%d" % n])
                    for u in range(2):
                        for hf in range(2):
                            MM(pD[:, u * 2 + hf, :], uT[n][:, u * 128:(u + 1) * 128], Wd[:, hc, hf * 512:(hf + 1) * 512], hc == 0, hc == 31,
                               ["uT%d" % n, "Wd%d" % (hc // 8)], ["pD%d" % (u * 2 + hf)])
                for u in range(2):
                    i = 2 * G + u
                    s.dma("sp", xmr[u][:], XMID[i], reads=["XMID%d" % i], writes=["xmr%d" % u])
                    TT("dve", xf[u][:], pD[:, 2 * u:2 * u + 2, :].rearrange("p a c -> p (a c)"), xmr[u][:], ALU.add,
                       ["pD%d" % (2 * u), "pD%d" % (2 * u + 1), "xmr%d" % u], ["xf%d" % u])
                    rmsnorm((junk, ss, rstd), xf[u][:], "xf%d" % u, grep[:], "grep5", xmr[u][:], "xmr%d" % u, "5")
                    s.dma("sp", y[i], xmr[u][:], reads=["xmr%d" % u], writes=["Y"], waw=False)
            s.op("sp", None, reads=["Y"])

    except _Stop:
        pass
    s.op("sp", None, reads=["Y"])
    s.emit()
    try:
        top.close()
    except AssertionError:
        pass
    return nc


def _const_tables():
    n = np.arange(128, dtype=np.float64)
    zeta = np.stack([np.exp(LOGG[h] * (127.0 - n)) for h in range(4)], axis=1) * (128.0 ** -0.5)
    xi = np.stack([np.exp(LOGG[h] * (n + 1.0)) for h in range(4)], axis=0)
    xiT = np.broadcast_to(xi.reshape(1, 512), (128, 512))
    rel = n[None, :] - n[:, None]
    dmatT = np.stack([np.where(rel >= 0, np.exp(LOGG[h] * np.maximum(rel, 0.0)), 0.0) for h in range(4)], axis=1)
    dmatT = dmatT.reshape(128, 512) * (128.0 ** -0.5)
    c = np.arange(1024)
    j = np.arange(256)
    ovl = np.clip(np.minimum(c[:, None] * 16 + 32, j[None, :] * 64 + 64) - np.maximum(c[:, None] * 16, j[None, :] * 64), 0, None) / 32.0
    ov = ovl.reshape(8, 128, 256).transpose(1, 0, 2).reshape(128, 8 * 256)
    key = np.arange(4096)
    apat = ((key[None, :] // 64) % 64 == np.arange(64)[:, None]).astype(np.float64) * NEGBIG
    half_n = np.exp((-math.log(500000.0) * np.arange(8, dtype=np.float32) * 2.0 / 16).astype(np.float32))
    half_r = np.exp((-math.log(10000.0) * np.arange(64, dtype=np.float32) * 2.0 / 128).astype(np.float32))
    inv = np.broadcast_to(np.concatenate([half_n, half_r])[None, :], (128, 72))
    f = lambda a: np.ascontiguousarray(a, dtype=np.float32)
    kk = np.arange(128)[:, None]
    r = np.arange(128)[None, :]
    tri = (kk <= r).astype(np.float32)
    anti = (kk > r).astype(np.float32)
    ones = np.ones((128, 128), np.float32)
    winS = np.stack([anti, ones, ones, ones, tri], axis=1).reshape(128, 640)
    return dict(zeta=f(zeta), xiT=f(xiT), dmatT=f(dmatT), ov=f(ov), apat=f(apat), inv=f(inv),
                ident=f(np.eye(128)), winmaskS=f(winS)), tri, anti, ones


def _core_tables(c, tri, anti, ones):
    zeros = np.zeros((128, 128), np.float32)
    q = np.arange(128)
    jj = np.arange(256)
    biasimp = np.zeros((16, 128, 256), np.float32)
    cmpmask = np.zeros((16, 128, 256), np.float32)
    for i in range(16):
        B = 8 * i + c
        cur = 2 * B + (q >= 64).astype(np.int64)
        b = np.zeros((128, 256), np.float32)
        b += 128.0 * ((jj[None, :] >= cur[:, None] - 1) & (jj[None, :] <= cur[:, None]))
        b -= 256.0 * (jj[None, :] > cur[:, None])
        b[:, 0] += 128.0
        biasimp[i] = b
        ncm = i // 2 + 1
        for a in range(2):
            m = ncm - 2 + a
            if m < 0:
                continue
            ci = 128 * m + np.arange(128)
            valid = ((16 * ci[:, None] + 31) <= (128 * B + q[None, :])) & (ci[:, None] <= 1022)
            cmpmask[i, :, a * 128:(a + 1) * 128] = valid
    diag = np.stack([ones if m < c else (tri if m == c else zeros) for m in range(8)], axis=1).reshape(128, 1024)
    w0 = []
    for j in range(5):
        if c - 4 + j < 0:
            w0.append(zeros)
        else:
            w0.append(anti if j == 0 else (tri if j == 4 else ones))
    win0 = np.stack(w0, axis=1).reshape(128, 640)
    oh = np.zeros((128, 8), np.float32)
    oh[:, c] = 1.0
    return dict(biasimp=biasimp, cmpmask=cmpmask, diagmask=np.ascontiguousarray(diag, dtype=np.float32),
                winmask0=np.ascontiguousarray(win0, dtype=np.float32), oh=oh)


_NC_CACHE = {}


def _prep(x, positions, norm_mix, w_in, cmp_pos_k, cmp_pos_v, cmp_k_w1, cmp_k_w2, cmp_v_w1, cmp_v_w2,
          w_proj_a, w_proj_b, w_out, norm_mlp, w_up, w_down, norm_final, cores=range(NCORE)):
    f = lambda a: np.ascontiguousarray(np.asarray(a), dtype=np.float32)
    x2 = f(x).reshape(SEQ, D)
    pos = np.asarray(positions).reshape(SEQ).astype(np.int32)
    consts, tri, anti, ones = _const_tables()
    shared = dict(
        xg=x2, w_in=f(w_in)[0], ckw1=f(cmp_k_w1)[0], ckw2=f(cmp_k_w2)[0], cvw1=f(cmp_v_w1)[0], cvw2=f(cmp_v_w2)[0],
        pekT=np.ascontiguousarray(f(cmp_pos_k)[0].T), pevT=np.ascontiguousarray(f(cmp_pos_v)[0].T),
        wpa=f(w_proj_a)[0], wpb=f(w_proj_b)[0], wout=f(w_out)[0], wup=f(w_up)[0], wdown=f(w_down)[0],
        gmix=np.ascontiguousarray(np.broadcast_to(f(norm_mix)[0][None, :], (128, D))),
        gmlp=np.ascontiguousarray(np.broadcast_to(f(norm_mlp)[0][None, :], (128, D))),
        gfin=np.ascontiguousarray(np.broadcast_to(f(norm_final)[None, :], (128, D))),
        **consts)
    posg = pos.reshape(128, 128).T
    xpad = np.concatenate([np.zeros((512, D), np.float32), x2], axis=0)
    ppad = np.concatenate([np.zeros((512,), np.int32), pos], axis=0)
    in_maps = []
    for c in cores:
        rows = np.concatenate([np.arange(128 * (8 * i + c), 128 * (8 * i + c) + 640) for i in range(16)])
        xo = np.ascontiguousarray(xpad[rows])
        poso = ppad[rows].reshape(80, 128).T
        m = dict(shared)
        m["xo"] = xo
        m["posall"] = np.ascontiguousarray(np.concatenate([posg, poso], axis=1), dtype=np.int32)
        m.update(_core_tables(c, tri, anti, ones))
        in_maps.append(m)
    return in_maps


def kernel(**inputs):
    in_maps = _prep(**inputs)
    if "nc" not in _NC_CACHE:
        _NC_CACHE["nc"] = build_nc()
    res = run_bass_kernel_spmd(_NC_CACHE["nc"], in_maps, core_ids=list(range(NCORE)))
    out = np.zeros((SEQ, D), np.float32)
    for c in range(NCORE):
        yc = np.asarray(res.results[c]["y"]).reshape(16, 128, D)
        for i in range(16):
            B = 8 * i + c
            out[128 * B:128 * B + 128] = yc[i]
    return out.reshape(1, SEQ, D)
```

```python
import math
from contextlib import ExitStack

import numpy as np
import concourse.bass as bass
import concourse.mybir as mybir
from concourse.bass_utils import run_bass_kernel_spmd

F32 = mybir.dt.float32
BF16 = mybir.dt.bfloat16
I32 = mybir.dt.int32
AF = mybir.ActivationFunctionType
ALU = mybir.AluOpType

ENGS = ("pe", "dve", "act", "pool", "sp")
EPOCH = 24000
DMA_RING = {"sp": 8, "pool": 2, "act": 4}
NAMES = {}
CFG = dict(stop=99, p1_tiles=80, p2_tiles=128, p3_i=16, dbg=())

SEQ = 16384
D = 1024
NCORE = 8
NQB = 16
NTG = 128
TWO_PI = float(2 * np.pi)
MAGIC = 12582912.0
NEGBIG = 30000.0
GAM = [1.0 - 2.0 ** (-5.0 - h) for h in range(4)]
LOGG = [math.log(g) for g in GAM]


class _Stop(Exception):
    pass


class Sched:
    def __init__(self, nc):
        self.nc = nc
        self.ops = []
        self.phase = 0

    def next_phase(self):
        self.phase += 1

    def op(self, eng, fn, reads=(), writes=(), dma=False, waw=True):
        self.ops.append(dict(eng=eng, fn=fn, reads=tuple(reads), writes=tuple(writes),
                             dma=dma, waw=waw, deps=set(), signal=False, phase=self.phase))

    def dma(self, eng, out, in_, reads=(), writes=(), waw=True):
        self.op(eng, lambda e: e.dma_start(out=out, in_=in_), reads, writes, dma=True, waw=waw)

    def analyze(self):
        writers, readers = {}, {}
        ops = self.ops
        qcnt, last_on_sem = {}, {}
        for i, o in enumerate(ops):
            if not o["dma"]:
                continue
            n = qcnt.get(o["eng"], 0)
            qcnt[o["eng"]] = n + 1
            ring = DMA_RING[o["eng"]]
            key = ("dmaq", o["eng"], n % ring)
            o["sem"], o["val"] = key, 16 * (n // ring + 1)
            o["signal"] = True
            if key in last_on_sem:
                o["deps"].add(last_on_sem[key])
            last_on_sem[key] = i
        for o in ops:
            if o["fn"] is None:
                o["deps"].update(last_on_sem.values())
        last_eng, last_dma, barrier, cur = {}, {}, set(), 0
        for i, o in enumerate(ops):
            if o["phase"] != cur:
                cur = o["phase"]
                barrier = set(last_eng.values()) | set(last_dma.values())
            o["deps"].update(barrier)
            if o["dma"]:
                last_dma[o["sem"]] = i
            elif o["fn"] is not None:
                last_eng[o["eng"]] = i
        for i, o in enumerate(ops):
            deps = o["deps"]
            for k in o["reads"]:
                deps.update(writers.get(k, ()))
            for k in o["writes"]:
                deps.update(readers.get(k, ()))
                if o["waw"]:
                    deps.update(writers.get(k, ()))
            for k in o["reads"]:
                lst = readers.setdefault(k, [])
                if not o["dma"]:
                    lst[:] = [j for j in lst if ops[j]["dma"] or ops[j]["eng"] != o["eng"]]
                lst.append(i)
            for k in o["writes"]:
                if o["waw"]:
                    writers[k] = [i]
                    readers[k] = []
                else:
                    lst = writers.setdefault(k, [])
                    if not o["dma"]:
                        lst[:] = [j for j in lst if ops[j]["dma"] or ops[j]["eng"] != o["eng"]]
                    lst.append(i)
            deps.discard(i)
            if o["eng"] == "pe" and not o["dma"]:
                for j in [j for j in deps if ops[j]["eng"] == "pe" and not ops[j]["dma"]]:
                    deps.discard(j)
            for j in deps:
                ops[j]["signal"] = True
        cnt = {e: 0 for e in ENGS}
        dcnt = {}
        for o in ops:
            if not o["signal"]:
                continue
            if o["dma"]:
                continue
            else:
                e = o["eng"]
                o["sem"], o["val"] = ("eng", e, cnt[e] // EPOCH), cnt[e] % EPOCH + 1
                cnt[e] += 1
        self.sem_keys = sorted({o["sem"] for o in ops if o["signal"]}, key=str)
        print("sched: ops=%d signals=%s dma=%s nsem=%d" % (len(ops), cnt, qcnt, len(self.sem_keys)), flush=True)

    def emit(self):
        nc = self.nc
        self.analyze()
        ops = self.ops
        with ExitStack() as es:
            sems = {k: es.enter_context(nc.semaphore("s%d" % n)) for n, k in enumerate(self.sem_keys)}
            block = es.enter_context(nc.Block())

            def run_engine(engname, e):
                waited = {}
                for o in ops:
                    if o["eng"] != engname:
                        continue
                    need = {}
                    for j in o["deps"]:
                        k, v = ops[j]["sem"], ops[j]["val"]
                        if waited.get(k, 0) < v and need.get(k, 0) < v:
                            need[k] = v
                    for k, v in need.items():
                        e.wait_ge(sems[k], v)
                        waited[k] = v
                    if o["fn"] is None:
                        continue
                    inst = o["fn"](e)
                    if o["signal"]:
                        inst.then_inc(sems[o["sem"]], 16 if o["dma"] else 1)

            @block.sync
            def _(e):
                run_engine("sp", e)

            @block.scalar
            def _(e):
                run_engine("act", e)

            @block.vector
            def _(e):
                run_engine("dve", e)

            @block.gpsimd
            def _(e):
                run_engine("pool", e)

            @block.tensor
            def _(e):
                run_engine("pe", e)


C_Q, C_KC, C_VC, C_KS, C_VS, C_KW, C_VW, C_NG, C_RQ, C_RK, C_RV, C_RG, C_GA, C_GB = (
    0, 512, 640, 768, 896, 1024, 1152, 1280, 1304, 1816, 2328, 2840, 3352, 4376)


def build_nc():
    nc = bass.Bass("TRN2", target_bir_lowering=False)

    def din(name, shape, dt=F32):
        return nc.dram_tensor(name, list(shape), dt, kind="ExternalInput").ap()

    def dscr(name, shape, dt):
        kind = "ExternalOutput" if name in CFG["dbg"] else "Internal"
        return nc.dram_tensor(name, list(shape), dt, kind=kind).ap()

    xg = din("xg", [SEQ, D])
    xo = din("xo", [80 * 128, D])
    posall = din("posall", [128, 208], I32)
    inv = din("inv", [128, 72])
    w_in = din("w_in", [D, 5400])
    ckw1 = din("ckw1", [2048, 256])
    ckw2 = din("ckw2", [256, 64])
    cvw1 = din("cvw1", [2048, 256])
    cvw2 = din("cvw2", [256, 64])
    pekT = din("pekT", [64, 32])
    pevT = din("pevT", [64, 32])
    wpa = din("wpa", [512, D])
    wpb = din("wpb", [512, D])
    wout = din("wout", [D, D])
    wup = din("wup", [D, 4096])
    wdown = din("wdown", [4096, D])
    gmix = din("gmix", [128, D])
    gmlp = din("gmlp", [128, D])
    gfin = din("gfin", [128, D])
    biasimp = din("biasimp", [16, 128, 256])
    cmpmask = din("cmpmask", [16, 128, 256])
    diagmask = din("diagmask", [128, 8 * 128])
    winmask0 = din("winmask0", [128, 5 * 128])
    winmaskS = din("winmaskS", [128, 5 * 128])
    ohd = din("oh", [128, 8])
    ovd = din("ov", [128, 8 * 256])
    apat = din("apat", [64, 4096])
    identd = din("ident", [128, 128])
    zetad = din("zeta", [128, 4])
    xiTd = din("xiT", [128, 512])
    dmatTd = din("dmatT", [128, 512])
    y = nc.dram_tensor("y", [16, 128, D], F32, kind="ExternalOutput").ap()

    ROT = dscr("ROT", [128, 208, 144], F32)
    QT = dscr("QT", [16, 64, 1024], BF16)
    GN = dscr("GN", [16, 128, 24], F32)
    RQT = dscr("RQT", [16, 128, 512], BF16)
    RKT = dscr("RKT", [16, 128, 512], BF16)
    RV = dscr("RV", [16, 128, 512], BF16)
    SG = dscr("SG", [16, 128, 512], F32)
    KWT = dscr("KWT", [16, 64, 1280], BF16)
    VW = dscr("VW", [16, 128, 640], BF16)
    HT = dscr("HT", [16, 128, 1024], BF16)
    KCT = dscr("KCT", [128, 16512], BF16)
    VCT = dscr("VCT", [128, 16512], BF16)
    NSAT = dscr("NSAT", [16, 128, 512], BF16)
    RETT = dscr("RETT", [16, 128, 512], BF16)
    XMID = dscr("XMID", [16, 128, 1024], F32)
    H2T = dscr("H2T", [16, 128, 1024], BF16)

    s = Sched(nc)

    def ACT(out, in_, func, R, W, waw=True, **kw):
        s.op("act", lambda e: e.activation(out=out, in_=in_, func=func, **kw), R, W, waw=waw)

    def TT(eng, out, a, b, op, R, W, waw=True):
        s.op(eng, lambda e: e.tensor_tensor(out=out, in0=a, in1=b, op=op), R, W, waw=waw)

    def TS(eng, out, a, s1, s2, op0, op1, R, W, waw=True):
        if op1 is None:
            s.op(eng, lambda e: e.tensor_scalar(out=out, in0=a, scalar1=s1, scalar2=None, op0=op0), R, W, waw=waw)
        else:
            s.op(eng, lambda e: e.tensor_scalar(out=out, in0=a, scalar1=s1, scalar2=s2, op0=op0, op1=op1), R, W, waw=waw)

    def STT(eng, out, in0, scalar, in1, op0, op1, R, W, waw=True):
        s.op(eng, lambda e: e.scalar_tensor_tensor(out=out, in0=in0, scalar=scalar, in1=in1, op0=op0, op1=op1),
             R, W, waw=waw)

    def CP(eng, out, in_, R, W, waw=True):
        if eng == "act":
            ACT(out, in_, AF.Copy, R, W, waw=waw)
        else:
            s.op(eng, lambda e: e.tensor_copy(out=out, in_=in_), R, W, waw=waw)

    def RECIP(out, in_, R, W):
        s.op("dve", lambda e: e.reciprocal(out=out, in_=in_), R, W)

    def MM(out, lhsT, rhs, start, stop, R, W, skip=False):
        s.op("pe", lambda e: e.matmul(out, lhsT=lhsT, rhs=rhs, start=start, stop=stop, skip_group_check=skip),
             R, W, waw=False)

    def TR(out, in_, ident, R, W):
        s.op("pe", lambda e: e.transpose(out=out, in_=in_, identity=ident), R, W, waw=False)

    def MEMSET(eng, ap, val, W, waw=True):
        s.op(eng, lambda e: e.memset(ap, val), (), W, waw=waw)

    def wload(dst, src2d, key):
        K = dst.shape[1]
        for k in range(K):
            s.dma("pool", dst[:, k, :], src2d[k * 128:(k + 1) * 128, :], writes=[key], waw=False)

    def bc(ap, shape):
        return ap.to_broadcast(list(shape))

    top = ExitStack()
    try:
        uniq = [0]

        def T(es, name, shape, dt):
            uniq[0] += 1
            NAMES[name] = "sb%d_%s" % (uniq[0], name)
            return es.enter_context(nc.sbuf_tensor("sb%d_%s" % (uniq[0], name), list(shape), dt))

        def P(es, name, shape, dt):
            uniq[0] += 1
            return es.enter_context(nc.psum_tensor("ps%d_%s" % (uniq[0], name), list(shape), dt))

        identf = T(top, "identf", [128, 128], F32)
        identb = T(top, "identb", [128, 128], BF16)
        epsb = T(top, "epsb", [128, 1], F32)
        s.dma("sp", identf[:], identd, writes=["identf"])
        CP("dve", identb[:], identf[:], ["identf"], ["identb"])
        MEMSET("pool", epsb[:], 1e-6, ["epsb"])

        def rmsnorm(es_tmp, xt, kx, grep, kg, hb, khb, tag):
            junk, ss, rstd = es_tmp
            ACT(junk[:], xt, AF.Square, [kx], ["junk" + tag, "ss" + tag], accum_out=ss[:])
            ACT(rstd[:], ss[:], AF.Ln, ["ss" + tag, "epsb"], ["rstd" + tag], scale=1.0 / D, bias=epsb[:])
            ACT(rstd[:], rstd[:], AF.Exp, ["rstd" + tag], ["rstd" + tag], scale=-0.5)
            STT("dve", hb, xt, rstd[:], grep, ALU.mult, ALU.mult, [kx, "rstd" + tag, kg], [khb])

        def rotary(eng, x1, x2, cos, sin, o1, o2, tmp, R, W, tag):
            t1, t2, t3, t4 = tmp
            kt = ["rt%d%s" % (n, tag) for n in range(4)]
            TT(eng, t1, x1, cos, ALU.mult, R, [kt[0]])
            TT(eng, t2, x2, sin, ALU.mult, R, [kt[1]])
            TT(eng, t3, x1, sin, ALU.mult, R, [kt[2]])
            TT(eng, t4, x2, cos, ALU.mult, R, [kt[3]])
            TT(eng, o1, t1, t2, ALU.subtract, [kt[0], kt[1]], W, waw=False)
            TT(eng, o2, t4, t3, ALU.add, [kt[2], kt[3]], W, waw=False)

        p0 = ExitStack()
        if True:
            es = p0
            posi = T(es, "posi", [128, 208], I32)
            posf = T(es, "posf", [128, 208], F32)
            invs = T(es, "invs", [128, 72], F32)
            s.dma("sp", posi[:], posall, writes=["posi"])
            s.dma("sp", invs[:], inv, writes=["invs"])
            CP("dve", posf[:], posi[:], ["posi"], ["posf"])
            CH = 16
            ang = [T(es, "ang%d" % n, [128, CH, 72], F32) for n in range(2)]
            a2 = T(es, "a2", [128, CH, 72], F32)
            kf = T(es, "kf", [128, CH, 72], F32)
            rr = T(es, "rr", [128, CH, 72], F32)
            rotc = [T(es, "rotc%d" % n, [128, CH, 144], F32) for n in range(2)]

            def p0_chunk(ci):
                pr = ci % 2
                c0 = ci * CH
                TT("dve", ang[pr][:], bc(invs[:].unsqueeze(1), [128, CH, 72]),
                   bc(posf[:, c0:c0 + CH].unsqueeze(2), [128, CH, 72]), ALU.mult, ["invs", "posf"], ["ang%d" % pr])
                for which in range(2):
                    if which == 0:
                        src, ksrc = ang[pr], "ang%d" % pr
                    else:
                        TS("dve", a2[:], ang[pr][:], float(np.pi / 2), None, ALU.add, None, ["ang%d" % pr], ["a2"])
                        src, ksrc = a2, "a2"
                    TS("dve", kf[:], src[:], 1.0 / TWO_PI, MAGIC, ALU.mult, ALU.add, [ksrc], ["kf"])
                    TS("dve", kf[:], kf[:], -MAGIC, None, ALU.add, None, ["kf"], ["kf"])
                    STT("dve", rr[:], kf[:], -TWO_PI, src[:], ALU.mult, ALU.add, ["kf", ksrc], ["rr"])
                    TS("dve", rr[:], rr[:], 3.14159, -3.14159, ALU.min, ALU.max, ["rr"], ["rr"])
                    ACT(rotc[pr][:, :, which * 72:(which + 1) * 72], rr[:], AF.Sin, ["rr"], ["rotc%d" % pr], waw=False)
                s.dma("sp", ROT[:, c0:c0 + CH, :], rotc[pr][:], reads=["rotc%d" % pr], writes=["ROT%d" % ci], waw=False)

            for ci in range(8, 13):
                p0_chunk(ci)

        if CFG['stop'] < 1:
            raise _Stop()
        with ExitStack() as es:
            Wq = T(es, "Wq", [128, 8, 512], BF16)
            Wng = T(es, "Wng", [128, 8, 24], BF16)
            Wr = T(es, "Wr", [128, 8, 2048], BF16)
            Ww = T(es, "Ww", [128, 8, 256], BF16)
            grep = T(es, "grep1", [128, D], F32)
            s.dma("sp", grep[:], gmix, writes=["grep"])
            wload(Ww, w_in[:, C_KW:C_KW + 256], "Ww")
            wload(Wq, w_in[:, C_Q:C_Q + 512], "Wq")
            wload(Wng, w_in[:, C_NG:C_NG + 24], "Wng")
            wload(Wr, w_in[:, C_RQ:C_RQ + 2048], "Wr")
            xt = [T(es, "xt%d" % n, [128, D], F32) for n in range(2)]
            rot = [T(es, "rot%d" % n, [128, 144], F32) for n in range(2)]
            hb = [T(es, "hb%d" % n, [128, D], BF16) for n in range(2)]
            hT = [T(es, "hT%d" % n, [128, 8, 128], BF16) for n in range(2)]
            junk = T(es, "junk", [128, D], F32)
            ss = T(es, "ss", [128, 1], F32)
            rstd = T(es, "rstd", [128, 1], F32)
            kwf = T(es, "kwf", [128, 128], F32)
            kwb = T(es, "kwb", [128, 128], BF16)
            vwb = [T(es, "vwb%d" % n, [128, 128], BF16) for n in range(2)]
            kwT = [T(es, "kwT%d" % n, [64, 2, 128], BF16) for n in range(2)]
            rtmp_s = [T(es, "rts%d" % n, [128, 8, 8], F32) for n in range(4)]
            rtmp_b = [T(es, "rtb%d" % n, [128, 4, 64], F32) for n in range(4)]
            qf = T(es, "qf", [128, 512], F32)
            qb = T(es, "qb", [128, 512], BF16)
            qT = T(es, "qT", [64, 1024], BF16)
            ge = T(es, "ge", [128, 24], F32)
            rf = T(es, "rf", [128, 512], F32)
            rqb = T(es, "rqb", [128, 512], BF16)
            rkb = T(es, "rkb", [128, 512], BF16)
            rvb = T(es, "rvb", [128, 512], BF16)
            rTs = [T(es, "rTs%d" % n, [128, 512], BF16) for n in range(2)]
            sge = T(es, "sge", [128, 512], F32)
            sgo = T(es, "sgo", [128, 512], F32)
            pT = [P(es, "pT%d" % n, [128, 1024], BF16) for n in range(2)]
            pT2 = P(es, "pT2", [128, 1024], BF16)
            pW = P(es, "pW", [128, 512], F32)
            pQ = P(es, "pQ", [128, 512], F32)
            pR = P(es, "pR", [128, 2, 512], F32)

            def p1_load(t):
                pr = t % 2
                s.dma("sp", xt[pr][:], xo[t * 128:(t + 1) * 128, :], writes=["xt%d" % pr])
                s.dma("sp", rot[pr][:], ROT[:, 128 + t, :], reads=["ROT%d" % ((128 + t) // 16)], writes=["rot%d" % pr])

            p1_load(0)
            for t in range(CFG['p1_tiles']):
                i, j = divmod(t, 5)
                pr = t % 2
                if t + 1 < CFG['p1_tiles']:
                    p1_load(t + 1)
                if t % 8 == 2 and t // 8 < 8:
                    p0_chunk(t // 8)
                kx, kr, khb, khT, kpT = "xt%d" % pr, "rot%d" % pr, "hb%d" % pr, "hT%d" % pr, "pT%d" % pr
                rmsnorm((junk, ss, rstd), xt[pr][:], kx, grep[:], "grep", hb[pr][:], khb, "1")
                for k in range(8):
                    TR(pT[pr][:, k * 128:(k + 1) * 128], hb[pr][:, k * 128:(k + 1) * 128], identb[:], [khb, "identb"], [kpT])
                CP("act", hT[pr][:].rearrange("p k t -> p (k t)"), pT[pr][:], [kpT], [khT])
                for k in range(8):
                    MM(pW[:, 0:256], hT[pr][:, k, :], Ww[:, k, :], k == 0, k == 7, [khT, "Ww"], ["pW"])
                CP("act", vwb[pr][:], pW[:, 128:256], ["pW"], ["vwb%d" % pr])
                s.dma("sp", VW[i][:, j * 128:(j + 1) * 128], vwb[pr][:], reads=["vwb%d" % pr], writes=["VW%d" % i], waw=False)
                CP("act", kwf[:], pW[:, 0:128], ["pW"], ["kwf"])
                CP("pool", kwb[:], kwf[:], ["kwf"], ["kwb"])
                kv = kwf[:].rearrange("p (g d) -> p g d", g=2)
                ko = kwb[:].rearrange("p (g d) -> p g d", g=2)
                rotary("dve", kv[:, :, 0:8], kv[:, :, 8:16], bc(rot[pr][:, 72:80].unsqueeze(1), [128, 2, 8]),
                       bc(rot[pr][:, 0:8].unsqueeze(1), [128, 2, 8]), ko[:, :, 0:8], ko[:, :, 8:16],
                       [r[:, 0:2, :] for r in rtmp_s], ["kwf", kr, "kwb"], ["kwb"], "s")
                for g in range(2):
                    TR(pT2[0:64, g * 128:(g + 1) * 128], kwb[:, g * 64:(g + 1) * 64], identb[:], ["kwb", "identb"], ["pT2"])
                CP("dve", kwT[pr][:].rearrange("p g t -> p (g t)"), pT2[0:64, 0:256], ["pT2"], ["kwT%d" % pr])
                s.dma("sp", KWT[i].rearrange("p (g t) -> p g t", g=2)[:, :, j * 128:(j + 1) * 128], kwT[pr][:],
                      reads=["kwT%d" % pr], writes=["KWT%d" % i], waw=False)
                if j != 4:
                    continue
                s.dma("sp", HT[i], hT[pr][:].rearrange("p k t -> p (k t)"), reads=[khT], writes=["HT%d" % i])
                for k in range(8):
                    MM(pQ[:], hT[pr][:, k, :], Wq[:, k, :], k == 0, k == 7, [khT, "Wq"], ["pQ"])
                for k in range(8):
                    MM(pW[:, 256:280], hT[pr][:, k, :], Wng[:, k, :], k == 0, k == 7, [khT, "Wng"], ["pW"])
                ACT(qf[:], pQ[:], AF.Copy, ["pQ"], ["qf"], scale=0.125)
                CP("pool", qb[:], qf[:], ["qf"], ["qb"])
                qv = qf[:].rearrange("p (h d) -> p h d", h=8)
                qo = qb[:].rearrange("p (h d) -> p h d", h=8)
                rotary("dve", qv[:, :, 0:8], qv[:, :, 8:16], bc(rot[pr][:, 72:80].unsqueeze(1), [128, 8, 8]),
                       bc(rot[pr][:, 0:8].unsqueeze(1), [128, 8, 8]), qo[:, :, 0:8], qo[:, :, 8:16],
                       [r[:] for r in rtmp_s], ["qf", kr, "qb"], ["qb"], "s")
                for h in range(8):
                    TR(pT2[0:64, h * 128:(h + 1) * 128], qb[:, h * 64:(h + 1) * 64], identb[:], ["qb", "identb"], ["pT2"])
                CP("dve", qT[:], pT2[0:64, :], ["pT2"], ["qT"])
                s.dma("sp", QT[i], qT[:], reads=["qT"], writes=["QT%d" % i])
                ACT(ge[:], pW[:, 256:280], AF.Exp, ["pW"], ["ge"], scale=-1.0)
                TS("dve", ge[:], ge[:], 1.0, None, ALU.add, None, ["ge"], ["ge"])
                RECIP(ge[:], ge[:], ["ge"], ["ge"])
                s.dma("sp", GN[i], ge[:], reads=["ge"], writes=["GN%d" % i])
                for half in range(2):
                    for b in range(2):
                        c0 = (half * 2 + b) * 512
                        for k in range(8):
                            MM(pR[:, b, :], hT[pr][:, k, :], Wr[:, k, c0:c0 + 512], k == 0, k == 7, [khT, "Wr"], ["pR%d" % b])
                    if half == 0:
                        for b, (dst, dsc, kd) in enumerate(((rqb, RQT, "RQT"), (rkb, RKT, "RKT"))):
                            CP("act", rf[:], pR[:, b, :], ["pR%d" % b], ["rf"])
                            rv_ = rf[:].rearrange("p (h d) -> p h d", h=4)
                            ro = dst[:].rearrange("p (h d) -> p h d", h=4)
                            rotary("pool", rv_[:, :, 0:64], rv_[:, :, 64:128], bc(rot[pr][:, 80:144].unsqueeze(1), [128, 4, 64]),
                                   bc(rot[pr][:, 8:72].unsqueeze(1), [128, 4, 64]), ro[:, :, 0:64], ro[:, :, 64:128],
                                   [r[:] for r in rtmp_b], ["rf", kr, "rdst%d" % b], ["rdst%d" % b], "b")
                            for h in range(4):
                                TR(pT2[:, h * 128:(h + 1) * 128], dst[:, h * 128:(h + 1) * 128], identb[:], ["rdst%d" % b, "identb"], ["pT2"])
                            CP("dve", rTs[b][:], pT2[:, 0:512], ["pT2"], ["rTs%d" % b])
                            s.dma("sp", dsc[i], rTs[b][:], reads=["rTs%d" % b], writes=["%s%d" % (kd, i)])
                    else:
                        CP("act", rvb[:], pR[:, 0, :], ["pR0"], ["rvb"])
                        s.dma("sp", RV[i], rvb[:], reads=["rvb"], writes=["RV%d" % i])
                        ACT(sge[:], pR[:, 1, :], AF.Exp, ["pR1"], ["sge"], scale=-1.0)
                        TS("dve", sge[:], sge[:], 1.0, None, ALU.add, None, ["sge"], ["sge"])
                        RECIP(sge[:], sge[:], ["sge"], ["sge"])
                        TT("dve", sgo[:], pR[:, 1, :], sge[:], ALU.mult, ["pR1", "sge"], ["sgo"])
                        s.dma("sp", SG[i], sgo[:], reads=["sgo"], writes=["SG%d" % i])

        if CFG['stop'] < 2:
            raise _Stop()
        p0.close()
        s.next_phase()
        with ExitStack() as mid:
            KsTa = [T(mid, "KsTa%d" % g, [128, SEQ], BF16) for g in range(2)]
            Vsa = T(mid, "Vsa", [128, 128, 2, 65], BF16)
            Sown = T(mid, "Sown", [128, 16, 512], BF16)
            kcmpT = T(mid, "kcmpT", [64, 2, 1024], BF16)
            vcmpa = T(mid, "vcmpa", [128, 8, 2, 65], BF16)
            for g in range(2):
                for r8 in range(8):
                    s.dma("pool", KsTa[g][64:128, r8 * 2048:(r8 + 1) * 2048], apat[:, (r8 % 2) * 2048:(r8 % 2 + 1) * 2048],
                          writes=["KsTa%d" % g], waw=False)
            MEMSET("pool", Vsa[:, :, :, 64:65], 1.0, ["Vsa"], waw=False)
            MEMSET("pool", vcmpa[:, :, :, 64:65], 1.0, ["vcmpa"], waw=False)

            with ExitStack() as es:
                W1g = T(es, "W1g", [128, 8, 512], BF16)
                W2g = T(es, "W2g", [128, 8, 1024], BF16)
                grep = T(es, "grep2", [128, D], F32)
                zeta = T(es, "zeta", [128, 4], F32)
                oh = T(es, "oh", [128, 8], F32)
                s.dma("sp", grep[:], gmix, writes=["grep2"])
                s.dma("sp", zeta[:], zetad, writes=["zeta"])
                s.dma("sp", oh[:], ohd, writes=["oh"])
                wload(W1g, w_in[:, C_KC:C_KC + 512], "W1g")
                wload(W2g, w_in[:, C_RK:C_RK + 1024], "W2g")
                xt = [T(es, "gxt%d" % n, [128, D], F32) for n in range(2)]
                rot3 = [T(es, "grot%d" % n, [128, 144], F32) for n in range(3)]
                hb = [T(es, "ghb%d" % n, [128, D], BF16) for n in range(2)]
                hT = [T(es, "ghT%d" % n, [128, 8, 128], BF16) for n in range(2)]
                junk = T(es, "gjunk", [128, D], F32)
                ss = T(es, "gss", [128, 1], F32)
                rstd = T(es, "grstd", [128, 1], F32)
                abf = T(es, "abf", [128, 384], BF16)
                af = T(es, "af", [128, 384], F32)
                kvst = [T(es, "kvst%d" % n, [128, 256], BF16) for n in range(2)]
                rkf = T(es, "rkf", [128, 512], F32)
                rkb = T(es, "grkb", [128, 512], BF16)
                rvz = T(es, "rvz", [128, 512], BF16)
                rtmp_s = [T(es, "grts%d" % n, [128, 2, 2, 8], F32) for n in range(4)]
                rtmp_b = [T(es, "grtb%d" % n, [128, 4, 64], F32) for n in range(4)]
                Sst = T(es, "Sst", [128, 512], F32)
                acc = T(es, "acc", [128, 512], F32)
                zpad = T(es, "zpad", [128, 16], BF16)
                pT = [P(es, "gpT%d" % n, [128, 1024], BF16) for n in range(2)]
                pT2 = P(es, "gpT2", [128, 1024], BF16)
                pA = P(es, "pA", [128, 512], F32)
                pB = P(es, "pB", [128, 2, 512], F32)
                pL = P(es, "pL", [128, 512], F32)
                MEMSET("pool", Sst[:], 0.0, ["Sst"])
                MEMSET("pool", zpad[:], 0.0, ["zpad"])
                s.dma("sp", KCT[:, SEQ:SEQ + 16], zpad[:], reads=["zpad"], writes=["KCT"], waw=False)
                s.dma("sp", VCT[:, SEQ:SEQ + 16], zpad[:], reads=["zpad"], writes=["VCT"], waw=False)

                def p2_load(t):
                    pr = t % 2
                    s.dma("sp", xt[pr][:], xg[t * 128:(t + 1) * 128, :], writes=["gxt%d" % pr])
                    s.dma("sp", rot3[t % 3][:], ROT[:, t, :], reads=["ROT%d" % (t // 16)], writes=["grot%d" % (t % 3)])

                def p2_keys(t):
                    pr = t % 2
                    return pr, "gxt%d" % pr, "grot%d" % (t % 3), "ghb%d" % pr, "ghT%d" % pr, "gpT%d" % pr

                def p2_A1(t):
                    pr, kx, kr, khb, khT, kpT = p2_keys(t)
                    rmsnorm((junk, ss, rstd), xt[pr][:], kx, grep[:], "grep2", hb[pr][:], khb, "2")

                def p2_A2(t):
                    pr, kx, kr, khb, khT, kpT = p2_keys(t)
                    for k in range(8):
                        TR(pT[pr][:, k * 128:(k + 1) * 128], hb[pr][:, k * 128:(k + 1) * 128], identb[:], [khb, "identb"], [kpT])
                    CP("act", hT[pr][:].rearrange("p k t -> p (k t)"), pT[pr][:], [kpT], [khT])

                def p2_B1(t):
                    pr, kx, kr, khb, khT, kpT = p2_keys(t)
                    for k in range(8):
                        MM(pA[:], hT[pr][:, k, :], W1g[:, k, :], k == 0, k == 7, [khT, "W1g"], ["pA"])
                    for b in range(2):
                        for k in range(8):
                            MM(pB[:, b, :], hT[pr][:, k, :], W2g[:, k, b * 512:(b + 1) * 512], k == 0, k == 7, [khT, "W2g"], ["pB%d" % b])

                def p2_B2a(t):
                    pr, kx, kr, khb, khT, kpT = p2_keys(t)
                    for h in range(4):
                        TS("dve", rvz[:, h * 128:(h + 1) * 128], pB[:, 1, h * 128:(h + 1) * 128], zeta[:, h:h + 1], None, ALU.mult, None,
                           ["pB1", "zeta"], ["rvz"], waw=(h == 0))
                    CP("act", abf[:], pA[:, 0:384], ["pA"], ["abf"])
                    CP("act", Vsa[:, t, :, 0:64], pA[:, 384:512].rearrange("p (g d) -> p g d", g=2), ["pA"], ["Vsa"], waw=False)
                    CP("act", af[:], pA[:, 0:384], ["pA"], ["af"])
                    CP("act", rkf[:], pB[:, 0, :], ["pB0"], ["rkf"])
                    for a_ in (0, 2):
                        av = af[:, a_ * 128:(a_ + 1) * 128].rearrange("p (g d) -> p g d", g=2)
                        ao = abf[:, a_ * 128:(a_ + 1) * 128].rearrange("p (g d) -> p g d", g=2)
                        rotary("dve", av[:, :, 0:8], av[:, :, 8:16], bc(rot3[t % 3][:, 72:80].unsqueeze(1), [128, 2, 8]),
                               bc(rot3[t % 3][:, 0:8].unsqueeze(1), [128, 2, 8]), ao[:, :, 0:8], ao[:, :, 8:16],
                               [r[:, 0, :, :] for r in rtmp_s], ["af", kr, "abf"], ["abf"], "gs")
                    rv_ = rkf[:].rearrange("p (h d) -> p h d", h=4)
                    ro = rkb[:].rearrange("p (h d) -> p h d", h=4)
                    rotary("pool", rv_[:, :, 0:64], rv_[:, :, 64:128], bc(rot3[t % 3][:, 80:144].unsqueeze(1), [128, 4, 64]),
                           bc(rot3[t % 3][:, 8:72].unsqueeze(1), [128, 4, 64]), ro[:, :, 0:64], ro[:, :, 64:128],
                           [r[:] for r in rtmp_b], ["rkf", kr, "grkb"], ["grkb"], "gb")

                def p2_B2b(t):
                    pr, kx, kr, khb, khT, kpT = p2_keys(t)
                    TR(pT2[:, 0:128], abf[:, 0:128], identb[:], ["abf", "identb"], ["gpT2"])
                    TR(pT2[:, 128:256], abf[:, 128:256], identb[:], ["abf", "identb"], ["gpT2"])
                    TR(pT2[0:64, 256:384], abf[:, 256:320], identb[:], ["abf", "identb"], ["gpT2"])
                    TR(pT2[0:64, 384:512], abf[:, 320:384], identb[:], ["abf", "identb"], ["gpT2"])
                    CP("act", kvst[pr][:], pT2[:, 0:256], ["gpT2"], ["kvst%d" % pr])
                    s.dma("sp", KCT[:, t * 128:(t + 1) * 128], kvst[pr][:, 0:128], reads=["kvst%d" % pr], writes=["KCT"], waw=False)
                    s.dma("sp", VCT[:, t * 128:(t + 1) * 128], kvst[pr][:, 128:256], reads=["kvst%d" % pr], writes=["VCT"], waw=False)
                    CP("act", KsTa[0][0:64, t * 128:(t + 1) * 128], pT2[0:64, 256:384], ["gpT2"], ["KsTa0"], waw=False)
                    CP("act", KsTa[1][0:64, t * 128:(t + 1) * 128], pT2[0:64, 384:512], ["gpT2"], ["KsTa1"], waw=False)
                    m = t % 8
                    if m == 0:
                        TS("dve", acc[:], Sst[:], oh[:, 0:1], None, ALU.mult, None, ["Sst", "oh"], ["acc"])
                    else:
                        STT("dve", acc[:], Sst[:], oh[:, m:m + 1], acc[:], ALU.mult, ALU.add, ["Sst", "oh", "acc"], ["acc"])
                    if m == 7:
                        CP("act", Sown[:, t // 8, :], acc[:], ["acc"], ["Sown"], waw=False)
                    for h in range(4):
                        MM(pL[:, h * 128:(h + 1) * 128], rkb[:, h * 128:(h + 1) * 128], rvz[:, h * 128:(h + 1) * 128],
                           True, True, ["grkb", "rvz"], ["pL"])
                    for h in range(4):
                        STT("dve", Sst[:, h * 128:(h + 1) * 128], Sst[:, h * 128:(h + 1) * 128], float(GAM[h] ** 128),
                            pL[:, h * 128:(h + 1) * 128], ALU.mult, ALU.add, ["Sst", "pL"], ["Sst"])

                NT2 = CFG['p2_tiles']
                for t0 in range(min(2, NT2)):
                    p2_load(t0)
                if NT2 > 0:
                    p2_A1(0)
                    p2_A2(0)
                for t in range(NT2):
                    if t + 2 < NT2:
                        p2_load(t + 2)
                    if t + 1 < NT2:
                        p2_A1(t + 1)
                    p2_B1(t)
                    if t + 1 < NT2:
                        p2_A2(t + 1)
                    if t >= 1:
                        p2_B2b(t - 1)
                    p2_B2a(t)
                if NT2 > 0:
                    p2_B2b(NT2 - 1)
            if CFG['stop'] < 3:
                raise _Stop()
            s.next_phase()
            with ExitStack() as es:
                w1 = [T(es, "w1_%d" % n, [128, 32, 256], BF16) for n in range(2)]
                w2 = [T(es, "w2_%d" % n, [128, 2, 64], BF16) for n in range(2)]
                peT = [T(es, "peT%d" % n, [64, 32], BF16) for n in range(2)]
                bia = T(es, "bia", [128, 4], F32)
                nbia = T(es, "nbia", [128, 4], F32)
                src = [T(es, "csrc%d" % n, [128, 4112], BF16) for n in range(2)]
                hid = T(es, "hid", [128, 2, 2, 256], BF16)
                ex = T(es, "cex", [128, 256], F32)
                pC = [P(es, "pC%d" % n, [128, 512], F32) for n in range(2)]
                pK = P(es, "pK", [128, 512], F32)
                pBi = P(es, "pBi", [128, 512], F32)
                for kind, (wd1, wd2, ped) in enumerate(((ckw1, ckw2, pekT), (cvw1, cvw2, pevT))):
                    w1v = wd1.rearrange("(l d) h -> d l h", d=64)
                    for half in range(2):
                        for l4 in range(4):
                            s.dma("pool", w1[kind][half * 64:(half + 1) * 64, l4 * 8:(l4 + 1) * 8, :], w1v[:, l4 * 8:(l4 + 1) * 8, :],
                                  writes=["w1_%d" % kind], waw=False)
                    s.dma("pool", w2[kind][:], wd2.rearrange("(c p) d -> p c d", p=128), writes=["w2_%d" % kind])
                    s.dma("pool", peT[kind][:], ped, writes=["peT%d" % kind])
                    for hc in range(2):
                        for l in range(32):
                            MM(pBi[:, kind * 2 + hc:kind * 2 + hc + 1], w1[kind][0:64, l, hc * 128:(hc + 1) * 128], peT[kind][:, l:l + 1],
                               l == 0 and kind == 0 and hc == 0, l == 31, ["w1_%d" % kind, "peT%d" % kind], ["pBi"], skip=True)
                CP("dve", bia[:], pBi[:, 0:4], ["pBi"], ["bia"])
                TS("dve", nbia[:], bia[:], -1.0, None, ALU.mult, None, ["bia"], ["nbia"])
                cnt = 0
                for qd in range(CFG.get('p2b_q', 4)):
                    for kind, SRC in enumerate((KCT, VCT)):
                        s.dma("sp", src[kind][:], SRC[:, qd * 4096:qd * 4096 + 4112], reads=["KCT", "VCT"], writes=["csrc%d" % kind])
                        for g in range(2):
                            for hc in range(2):
                                pc = pC[cnt % 2]
                                kpc = "pC%d" % (cnt % 2)
                                cnt += 1
                                for l in range(32):
                                    MM(pc[:, 0:256], w1[kind][g * 64:(g + 1) * 64, l, hc * 128:(hc + 1) * 128],
                                       src[kind][g * 64:(g + 1) * 64, l:l + 4081:16], l == 0, l == 31,
                                       ["w1_%d" % kind, "csrc%d" % kind], [kpc])
                                col = kind * 2 + hc
                                ACT(ex[:], pc[:, 0:256], AF.Exp, [kpc, "nbia"], ["cex"], scale=-1.0, bias=nbia[:, col:col + 1])
                                TS("dve", ex[:], ex[:], 1.0, None, ALU.add, None, ["cex"], ["cex"])
                                RECIP(ex[:], ex[:], ["cex"], ["cex"])
                                STT("dve", hid[:, g, hc, :], pc[:, 0:256], bia[:, col:col + 1], ex[:], ALU.add, ALU.mult,
                                    [kpc, "bia", "cex"], ["hid"], waw=False)
                        if kind == 0:
                            for g in range(2):
                                for hc in range(2):
                                    MM(pK[0:64, g * 256:(g + 1) * 256], w2[0][:, hc, :], hid[:, g, hc, :], hc == 0 and g == 0, hc == 1,
                                       ["w2_0", "hid"], ["pK"], skip=True)
                            CP("act", kcmpT[:, :, qd * 256:(qd + 1) * 256], pK[0:64, :].rearrange("p (g b) -> p g b", g=2),
                               ["pK"], ["kcmpT"], waw=False)
                        else:
                            for g in range(2):
                                for bch in range(2):
                                    for hc in range(2):
                                        MM(pK[:, (g * 2 + bch) * 64:(g * 2 + bch + 1) * 64], hid[:, g, hc, bch * 128:(bch + 1) * 128],
                                           w2[1][:, hc, :], hc == 0 and g == 0 and bch == 0, hc == 1, ["w2_1", "hid"], ["pK"], skip=True)
                            for bch in range(2):
                                CP("act", vcmpa[:, qd * 2 + bch, :, 0:64],
                                   pK[:, 0:256].rearrange("p (g b d) -> p g b d", g=2, b=2)[:, :, bch, :], ["pK"], ["vcmpa"], waw=False)

            if CFG['stop'] < 4:
                raise _Stop()
            s.next_phase()
            with ExitStack() as es:
                ov = T(es, "ov", [128, 8, 256], BF16)
                diag = T(es, "diag", [128, 8, 128], F32)
                winS = T(es, "winS", [128, 5, 128], F32)
                win0 = T(es, "win0", [128, 5, 128], F32)
                xiT = T(es, "xiT", [128, 512], F32)
                dmatT = T(es, "dmatT", [128, 512], F32)
                s.dma("pool", ov[:].rearrange("p m j -> p (m j)"), ovd, writes=["ov"])
                s.dma("sp", diag[:].rearrange("p m j -> p (m j)"), diagmask, writes=["diag"])
                s.dma("sp", winS[:].rearrange("p m j -> p (m j)"), winmaskS, writes=["winS"])
                s.dma("sp", win0[:].rearrange("p m j -> p (m j)"), winmask0, writes=["win0"])
                s.dma("sp", xiT[:], xiTd, writes=["xiT"])
                s.dma("sp", dmatT[:], dmatTd, writes=["dmatT"])
                qt = [T(es, "qt%d" % n, [64, 1024], BF16) for n in range(2)]
                gn = [T(es, "gn%d" % n, [128, 24], F32) for n in range(2)]
                kwt = [T(es, "kwt%d" % n, [64, 2, 640], BF16) for n in range(2)]
                vwa = [T(es, "vwa%d" % n, [128, 5, 2, 65], BF16) for n in range(2)]
                bimp = [T(es, "bimp%d" % n, [128, 256], F32) for n in range(2)]
                cmk = [T(es, "cmk%d" % n, [128, 2, 128], F32) for n in range(2)]
                rqT = [T(es, "rqT%d" % n, [128, 512], BF16) for n in range(2)]
                rkT = [T(es, "rkT%d" % n, [128, 512], BF16) for n in range(2)]
                rvt = [T(es, "rvt%d" % n, [128, 512], BF16) for n in range(2)]
                sgt = [T(es, "sgt%d" % n, [128, 512], F32) for n in range(2)]
                QA = [T(es, "QA%d" % n, [128, 4, 512], BF16) for n in range(2)]
                PT = [T(es, "PT%d" % n, [128, 512], BF16) for n in range(2)]
                oT = T(es, "oT", [65, 512], F32)
                den = T(es, "den", [128, 4], F32)
                coef = T(es, "coef", [128, 4], F32)
                nsa = T(es, "nsa", [128, 512], F32)
                nsab = T(es, "nsab", [128, 512], BF16)
                nsaTs = T(es, "nsaTs", [128, 512], BF16)
                iacc = T(es, "iacc", [128, 256], F32)
                itmp = T(es, "itmp", [128, 256], F32)
                m8a = T(es, "m8a", [128, 8], F32)
                m8b = T(es, "m8b", [128, 8], F32)
                thr = T(es, "thr", [128, 1], F32)
                nsp = T(es, "nsp", [128, 320], BF16)
                PTr = T(es, "PTr", [128, 512], BF16)
                rqx = T(es, "rqx", [128, 512], BF16)
                ss4 = T(es, "ss4", [128, 4], F32)
                rjunk = T(es, "rjunk", [128, 512], F32)
                retb = T(es, "retb", [128, 512], BF16)
                retTs = T(es, "retTs", [128, 512], BF16)
                pS = [P(es, "pS%d" % n, [128, 512], F32) for n in range(2)]
                pO = P(es, "pO", [128, 512], F32)
                pImp = P(es, "pImp", [128, 4, 256], F32)
                pTr = P(es, "pTr", [128, 512], F32)
                pRS = P(es, "pRS", [128, 512], F32)
                pRO = P(es, "pRO", [128, 512], F32)
                pTrb = pTr.bitcast(BF16) if hasattr(pTr, "bitcast") else None
                for n in range(2):
                    MEMSET("pool", vwa[n][:, :, :, 64:65], 1.0, ["vwa%d" % n], waw=False)
                MEMSET("pool", nsp[:], 0.0, ["nsp"])

                def p3_load(i):
                    pr = i % 2
                    s.dma("sp", qt[pr][:], QT[i], reads=["QT%d" % i], writes=["qt%d" % pr])
                    s.dma("sp", gn[pr][:], GN[i], reads=["GN%d" % i], writes=["gn%d" % pr])
                    s.dma("sp", kwt[pr][:].rearrange("p g t -> p (g t)"), KWT[i], reads=["KWT%d" % i], writes=["kwt%d" % pr])
                    s.dma("sp", vwa[pr][:, :, :, 0:64], VW[i].rearrange("p (j g d) -> p j g d", j=5, g=2),
                          reads=["VW%d" % i], writes=["vwa%d" % pr], waw=False)
                    s.dma("sp", bimp[pr][:], biasimp[i], writes=["bimp%d" % pr])
                    s.dma("sp", cmk[pr][:].rearrange("p a r -> p (a r)"), cmpmask[i], writes=["cmk%d" % pr])
                    s.dma("sp", rqT[pr][:], RQT[i], reads=["RQT%d" % i], writes=["rqT%d" % pr])
                    s.dma("sp", rkT[pr][:], RKT[i], reads=["RKT%d" % i], writes=["rkT%d" % pr])
                    s.dma("sp", rvt[pr][:], RV[i], reads=["RV%d" % i], writes=["rvt%d" % pr])
                    s.dma("sp", sgt[pr][:], SG[i], reads=["SG%d" % i], writes=["sgt%d" % pr])

                step = [0]

                def score_chunk(lhsT, rhs, R, mask, maskR, meng="dve"):
                    n = step[0] % 2
                    step[0] += 1
                    MM(pS[n][:], lhsT, rhs, True, True, R, ["pS%d" % n])
                    ACT(PT[n][:], pS[n][:], AF.Exp, ["pS%d" % n], ["PT%d" % n])
                    if mask is not None:
                        pv = PT[n][:].rearrange("p (h q) -> p h q", h=4)
                        TT(meng, pv, pv, bc(mask.unsqueeze(1), [128, 4, 128]), ALU.mult, ["PT%d" % n] + maskR, ["PT%d" % n])
                    return PT[n], "PT%d" % n

                def finish_branch(br, g, pr):
                    CP("act", oT[:], pO[0:65, :], ["pO"], ["oT"])
                    for h in range(4):
                        TR(pTr[:, h * 65:(h + 1) * 65], oT[0:65, h * 128:(h + 1) * 128], identf[0:65, 0:65], ["oT", "identf"], ["pTr"])
                    pv = pTr[:, 0:260].rearrange("p (h e) -> p h e", h=4)
                    TS("dve", den[:], pv[:, :, 64], 1e-30, None, ALU.max, None, ["pTr"], ["den"])
                    RECIP(den[:], den[:], ["den"], ["den"])
                    gv = gn[pr][:].rearrange("p (h b) -> p h b", h=8)
                    TT("dve", coef[:], den[:], gv[:, 4 * g:4 * g + 4, br], ALU.mult, ["den", "gn%d" % pr], ["coef"])
                    for h in range(4):
                        dst = nsa[:, (4 * g + h) * 64:(4 * g + h + 1) * 64]
                        if br == 0:
                            TS("dve", dst, pv[:, h, 0:64], coef[:, h:h + 1], None, ALU.mult, None, ["pTr", "coef"], ["nsa"], waw=False)
                        else:
                            STT("dve", dst, pv[:, h, 0:64], coef[:, h:h + 1], dst, ALU.mult, ALU.add, ["pTr", "coef", "nsa"], ["nsa"])

                p3_load(0)
                for i in range(CFG['p3_i']):
                    pr = i % 2
                    if i + 1 < CFG['p3_i']:
                        p3_load(i + 1)
                    nkc = 8 * i + 8
                    nv = (nkc - 1) // 32 + 1
                    ncm = i // 2 + 1
                    for g in range(2):
                        qa = QA[g]
                        kqa = "QA%d" % g
                        for v in range(nv):
                            CP("pool", qa[0:64, v, :], qt[pr][:, g * 512:(g + 1) * 512], ["qt%d" % pr], [kqa], waw=(v == 0))
                        pend = None
                        for m in range(ncm):
                            mask, maskR = None, []
                            if m >= ncm - 2:
                                mask, maskR = cmk[pr][:, m - (ncm - 2), :], ["cmk%d" % pr]
                            ptile, kpt = score_chunk(kcmpT[:, g, m * 128:(m + 1) * 128], qa[0:64, 0, :], ["kcmpT", kqa], mask, maskR)
                            if pend is not None:
                                pend()

                            def pend(m=m, ptile=ptile, kpt=kpt):
                                MM(pO[0:65, :], vcmpa[:, m, g, :], ptile[:], m == 0, m == ncm - 1, ["vcmpa", kpt], ["pO"])
                                for h in range(4):
                                    MM(pImp[:, h, :], ptile[:, h * 128:(h + 1) * 128], ov[:, m, :], m == 0 and h % 2 == 0, m == ncm - 1,
                                       [kpt, "ov"], ["pImp"], skip=True)
                        pend()
                        finish_branch(0, g, pr)
                        for h in range(4):
                            STT("dve", iacc[:], pImp[:, h, :], den[:, h:h + 1], bimp[pr][:] if h == 0 else iacc[:], ALU.mult, ALU.add,
                                ["pImp", "den", "bimp%d" % pr, "iacc"], ["iacc"])
                        s.op("dve", lambda e: e.max(out=m8a[:], in_=iacc[:]), ["iacc"], ["m8a"])
                        s.op("dve", lambda e: e.match_replace(out=itmp[:], in_to_replace=m8a[:], in_values=iacc[:], imm_value=-1e30),
                             ["iacc", "m8a"], ["itmp"])
                        s.op("dve", lambda e: e.max(out=m8b[:], in_=itmp[:]), ["itmp"], ["m8b"])
                        TS("dve", thr[:], m8b[:, 7:8], -64.0, None, ALU.max, None, ["m8b"], ["thr"])
                        TS("dve", nsp[:, 64:320], iacc[:], thr[:, 0:1], 1.0, ALU.is_ge, ALU.subtract, ["iacc", "thr"], ["nsp"])
                        wm = win0 if i == 0 else winS
                        pend = None
                        for j in range(5):
                            ptile, kpt = score_chunk(kwt[pr][:, g, j * 128:(j + 1) * 128], qa[0:64, 0, :], ["kwt%d" % pr, kqa],
                                                     wm[:, j, :], ["win0" if i == 0 else "winS"], meng="pool")
                            if pend is not None:
                                pend()

                            def pend(j=j, ptile=ptile, kpt=kpt):
                                MM(pO[0:65, :], vwa[pr][:, j, g, :], ptile[:], j == 0, j == 4, ["vwa%d" % pr, kpt], ["pO"])
                        pend()
                        for v in range(nv):
                            TR(pTrb[:, v * 128:(v + 1) * 128], nsp[:, 64 * v:64 * v + 128], identb[:], ["nsp", "identb"], ["pTr"])
                        for v in range(nv):
                            CP("act", qa[64:128, v, :].rearrange("p (h q) -> p h q", h=4),
                               bc(pTrb[64:128, v * 128:(v + 1) * 128].unsqueeze(1), [64, 4, 128]), ["pTr"], [kqa], waw=False)
                        finish_branch(2, g, pr)
                        pend = None
                        for kc in range(nkc):
                            mask, maskR = None, []
                            if kc >= nkc - 8:
                                mask, maskR = diag[:, kc - (nkc - 8), :], ["diag"]
                            ptile, kpt = score_chunk(KsTa[g][:, kc * 128:(kc + 1) * 128], qa[:, kc // 32, :], ["KsTa%d" % g, kqa], mask, maskR)
                            if pend is not None:
                                pend()

                            def pend(kc=kc, ptile=ptile, kpt=kpt):
                                MM(pO[0:65, :], Vsa[:, kc, g, :], ptile[:], kc == 0, kc == nkc - 1, ["Vsa", kpt], ["pO"])
                        pend()
                        finish_branch(1, g, pr)
                    CP("pool", nsab[:], nsa[:], ["nsa"], ["nsab"])
                    for k in range(4):
                        TR(pTrb[:, k * 128:(k + 1) * 128], nsab[:, k * 128:(k + 1) * 128], identb[:], ["nsab", "identb"], ["pTr"])
                    CP("act", nsaTs[:], pTrb[:, 0:512], ["pTr"], ["nsaTs"])
                    s.dma("sp", NSAT[i], nsaTs[:], reads=["nsaTs"], writes=["NSAT%d" % i])
                    for h in range(4):
                        MM(pRS[:, h * 128:(h + 1) * 128], rkT[pr][:, h * 128:(h + 1) * 128], rqT[pr][:, h * 128:(h + 1) * 128],
                           True, True, ["rkT%d" % pr, "rqT%d" % pr], ["pRS"])
                    TT("dve", PTr[:], pRS[:], dmatT[:], ALU.mult, ["pRS", "dmatT"], ["PTr"])
                    TT("pool", rqx[:], rqT[pr][:], xiT[:], ALU.mult, ["rqT%d" % pr, "xiT"], ["rqx"])
                    for h in range(4):
                        hs = slice(h * 128, (h + 1) * 128)
                        MM(pRO[:, hs], PTr[:, hs], rvt[pr][:, hs], True, False, ["PTr", "rvt%d" % pr], ["pRO"], skip=True)
                        MM(pRO[:, hs], rqx[:, hs], Sown[:, i, hs], False, True, ["rqx", "Sown"], ["pRO"], skip=True)
                    for h in range(4):
                        ACT(rjunk[:, h * 128:(h + 1) * 128], pRO[:, h * 128:(h + 1) * 128], AF.Square, ["pRO"], ["rjunk", "ss4"], waw=(h == 0),
                            accum_out=ss4[:, h:h + 1])
                    ACT(ss4[:], ss4[:], AF.Ln, ["ss4", "epsb"], ["ss4"], scale=1.0 / 128, bias=epsb[:])
                    ACT(ss4[:], ss4[:], AF.Exp, ["ss4"], ["ss4"], scale=-0.5)
                    for h in range(4):
                        hs = slice(h * 128, (h + 1) * 128)
                        STT("dve", retb[:, hs], pRO[:, hs], ss4[:, h:h + 1], sgt[pr][:, hs], ALU.mult, ALU.mult,
                            ["pRO", "ss4", "sgt%d" % pr], ["retb"], waw=False)
                    for k in range(4):
                        TR(pTrb[:, k * 128:(k + 1) * 128], retb[:, k * 128:(k + 1) * 128], identb[:], ["retb", "identb"], ["pTr"])
                    CP("act", retTs[:], pTrb[:, 0:512], ["pTr"], ["retTs"])
                    s.dma("sp", RETT[i], retTs[:], reads=["retTs"], writes=["RETT%d" % i])

        if CFG['stop'] < 5:
            raise _Stop()
        s.next_phase()
        with ExitStack() as es:
            Wga = T(es, "Wga", [128, 8, 2048], BF16)
            Wpa = T(es, "Wpa", [128, 4, 1024], BF16)
            Wpb = T(es, "Wpb", [128, 4, 1024], BF16)
            Wo = T(es, "Wo", [128, 8, 1024], BF16)
            grep = T(es, "grep4", [128, D], F32)
            s.dma("sp", grep[:], gmlp, writes=["grep4"])
            wload(Wga, w_in[:, C_GA:C_GA + 2048], "Wga")
            wload(Wpa, wpa, "Wpa")
            wload(Wpb, wpb, "Wpb")
            wload(Wo, wout, "Wo")
            hTg = [T(es, "hTg%d" % n, [128, 8, 512], BF16) for n in range(2)]
            nsT = [T(es, "nsT%d" % n, [128, 4, 512], BF16) for n in range(2)]
            reT = [T(es, "reT%d" % n, [128, 4, 512], BF16) for n in range(2)]
            sig = [T(es, "sig%d" % n, [128, 512], F32) for n in range(2)]
            mxa = T(es, "mxa", [128, 512], F32)
            mixT = T(es, "mixT", [128, 8, 512], BF16)
            xo_t = [T(es, "xo_t%d" % n, [128, D], F32) for n in range(2)]
            xm = [T(es, "xm%d" % n, [128, D], F32) for n in range(2)]
            junk = T(es, "junk4", [128, D], F32)
            ss = T(es, "ss4_", [128, 1], F32)
            rstd = T(es, "rstd4", [128, 1], F32)
            h2b = T(es, "h2b", [128, D], BF16)
            h2T = [T(es, "h2Ts%d" % n, [128, 1024], BF16) for n in range(2)]
            pG = [P(es, "pG%d" % n, [128, 512], F32) for n in range(2)]
            pPr = [P(es, "pPr%d" % n, [128, 512], F32) for n in range(2)]
            pX = P(es, "pX", [128, 2, 512], F32)
            pT = P(es, "pT4", [128, 1024], BF16)

            def p4_load(G):
                pr = G % 2
                for u in range(4):
                    i = 4 * G + u
                    s.dma("sp", hTg[pr][:, :, u * 128:(u + 1) * 128], HT[i].rearrange("p (k t) -> p k t", k=8),
                          reads=["HT%d" % i], writes=["hTg%d" % pr], waw=False)
                    s.dma("sp", nsT[pr][:, :, u * 128:(u + 1) * 128], NSAT[i].rearrange("p (k t) -> p k t", k=4),
                          reads=["NSAT%d" % i], writes=["nsT%d" % pr], waw=False)
                    s.dma("sp", reT[pr][:, :, u * 128:(u + 1) * 128], RETT[i].rearrange("p (k t) -> p k t", k=4),
                          reads=["RETT%d" % i], writes=["reT%d" % pr], waw=False)

            p4_load(0)
            for G in range(CFG.get('p4_g', 4)):
                pr = G % 2
                if G + 1 < CFG.get('p4_g', 4):
                    p4_load(G + 1)
                for oc in range(8):
                    for ab, (Wp, srcT, ksrc) in enumerate(((Wpa, nsT, "nsT"), (Wpb, reT, "reT"))):
                        c0 = ab * 1024 + oc * 128
                        for k in range(8):
                            MM(pG[ab][:], Wga[:, k, c0:c0 + 128], hTg[pr][:, k, :], k == 0, k == 7, ["Wga", "hTg%d" % pr], ["pG%d" % ab])
                        for k in range(4):
                            MM(pPr[ab][:], Wp[:, k, oc * 128:(oc + 1) * 128], srcT[pr][:, k, :], k == 0, k == 3,
                               ["Wpa", "Wpb", "%s%d" % (ksrc, pr)], ["pPr%d" % ab])
                        ACT(sig[ab][:], pG[ab][:], AF.Exp, ["pG%d" % ab], ["sig%d" % ab], scale=-1.0)
                        TS("dve", sig[ab][:], sig[ab][:], 1.0, None, ALU.add, None, ["sig%d" % ab], ["sig%d" % ab])
                        RECIP(sig[ab][:], sig[ab][:], ["sig%d" % ab], ["sig%d" % ab])
                    TT("dve", mxa[:], pPr[0][:], sig[0][:], ALU.mult, ["pPr0", "sig0"], ["mxa"])
                    TT("dve", sig[1][:], pPr[1][:], sig[1][:], ALU.mult, ["pPr1", "sig1"], ["sig1"])
                    TT("pool", mixT[:, oc, :], mxa[:], sig[1][:], ALU.add, ["mxa", "sig1"], ["mixT"], waw=False)
                for u in range(4):
                    i = 4 * G + u
                    p2_ = i % 2
                    s.dma("sp", xo_t[p2_][:], xo[(5 * i + 4) * 128:(5 * i + 5) * 128, :], writes=["xo_t%d" % p2_])
                    for hf in range(2):
                        for oc in range(8):
                            MM(pX[:, hf, :], mixT[:, oc, u * 128:(u + 1) * 128], Wo[:, oc, hf * 512:(hf + 1) * 512], oc == 0, oc == 7,
                               ["mixT", "Wo"], ["pX%d" % hf])
                    TT("dve", xm[p2_][:], pX[:].rearrange("p a c -> p (a c)"), xo_t[p2_][:], ALU.add, ["pX0", "pX1", "xo_t%d" % p2_], ["xm%d" % p2_])
                    s.dma("sp", XMID[i], xm[p2_][:], reads=["xm%d" % p2_], writes=["XMID%d" % i])
                    rmsnorm((junk, ss, rstd), xm[p2_][:], "xm%d" % p2_, grep[:], "grep4", h2b[:], "h2b", "4")
                    for k in range(8):
                        TR(pT[:, k * 128:(k + 1) * 128], h2b[:, k * 128:(k + 1) * 128], identb[:], ["h2b", "identb"], ["pT4"])
                    CP("act", h2T[p2_][:], pT[:], ["pT4"], ["h2Ts%d" % p2_])
                    s.dma("sp", H2T[i], h2T[p2_][:], reads=["h2Ts%d" % p2_], writes=["H2T%d" % i])

        if CFG['stop'] < 6:
            raise _Stop()
        s.next_phase()
        with ExitStack() as es:
            Wu = T(es, "Wu", [128, 8, 4096], BF16)
            Wd = T(es, "Wd", [128, 32, 1024], BF16)
            grep = T(es, "grep5", [128, D], F32)
            s.dma("sp", grep[:], gfin, writes=["grep5"])
            for q4 in range(4):
                for k in range(8):
                    s.dma("pool", Wu[:, k, q4 * 1024:(q4 + 1) * 1024], wup[k * 128:(k + 1) * 128, q4 * 1024:(q4 + 1) * 1024],
                          writes=["Wu%d" % q4], waw=False)
                for hc in range(q4 * 8, q4 * 8 + 8):
                    s.dma("pool", Wd[:, hc, :], wdown[hc * 128:(hc + 1) * 128, :], writes=["Wd%d" % q4], waw=False)
            h2g = [T(es, "h2g%d" % n, [128, 8, 256], BF16) for n in range(2)]
            ur = [T(es, "ur%d" % n, [128, 256], F32) for n in range(2)]
            uT = [T(es, "uT%d" % n, [128, 256], BF16) for n in range(2)]
            xmr = [T(es, "xmr%d" % n, [128, D], F32) for n in range(2)]
            xf = [T(es, "xf%d" % n, [128, D], F32) for n in range(2)]
            junk = T(es, "junk5", [128, D], F32)
            ss = T(es, "ss5", [128, 1], F32)
            rstd = T(es, "rstd5", [128, 1], F32)
            pU = [P(es, "pU%d" % n, [128, 512], F32) for n in range(2)]
            pD = P(es, "pD", [128, 4, 512], F32)

            def p5_load(G):
                pr = G % 2
                for u in range(2):
                    i = 2 * G + u
                    s.dma("sp", h2g[pr][:, :, u * 128:(u + 1) * 128], H2T[i].rearrange("p (k t) -> p k t", k=8),
                          reads=["H2T%d" % i], writes=["h2g%d" % pr], waw=False)

            p5_load(0)
            for G in range(CFG.get('p5_g', 8)):
                pr = G % 2
                if G + 1 < CFG.get('p5_g', 8):
                    p5_load(G + 1)
                for hc in range(32):
                    n = hc % 2
                    for k in range(8):
                        MM(pU[n][:, 0:256], Wu[:, k, hc * 128:(hc + 1) * 128], h2g[pr][:, k, :], k == 0, k == 7, ["Wu%d" % (hc // 8), "h2g%d" % pr], ["pU%d" % n])
                    ACT(ur[n][:], pU[n][:, 0:256], AF.Relu, ["pU%d" % n], ["ur%d" % n])
                    TT("dve", uT[n][:], ur[n][:], ur[n][:], ALU.mult, ["ur%d" % n], ["uT%d" % n])
                    for u in range(2):
                        for hf in range(2):
                            MM(pD[:, u * 2 + hf, :], uT[n][:, u * 128:(u + 1) * 128], Wd[:, hc, hf * 512:(hf + 1) * 512], hc == 0, hc == 31,
                               ["uT%d" % n, "Wd%d" % (hc // 8)], ["pD%d" % (u * 2 + hf)])
                for u in range(2):
                    i = 2 * G + u
                    s.dma("sp", xmr[u][:], XMID[i], reads=["XMID%d" % i], writes=["xmr%d" % u])
                    TT("dve", xf[u][:], pD[:, 2 * u:2 * u + 2, :].rearrange("p a c -> p (a c)"), xmr[u][:], ALU.add,
                       ["pD%d" % (2 * u), "pD%d" % (2 * u + 1), "xmr%d" % u], ["xf%d" % u])
                    rmsnorm((junk, ss, rstd), xf[u][:], "xf%d" % u, grep[:], "grep5", xmr[u][:], "xmr%d" % u, "5")
                    s.dma("sp", y[i], xmr[u][:], reads=["xmr%d" % u], writes=["Y"], waw=False)
            s.op("sp", None, reads=["Y"])

    except _Stop:
        pass
    s.op("sp", None, reads=["Y"])
    s.emit()
    try:
        top.close()
    except AssertionError:
        pass
    return nc


def _const_tables():
    n = np.arange(128, dtype=np.float64)
    zeta = np.stack([np.exp(LOGG[h] * (127.0 - n)) for h in range(4)], axis=1) * (128.0 ** -0.5)
    xi = np.stack([np.exp(LOGG[h] * (n + 1.0)) for h in range(4)], axis=0)
    xiT = np.broadcast_to(xi.reshape(1, 512), (128, 512))
    rel = n[None, :] - n[:, None]
    dmatT = np.stack([np.where(rel >= 0, np.exp(LOGG[h] * np.maximum(rel, 0.0)), 0.0) for h in range(4)], axis=1)
    dmatT = dmatT.reshape(128, 512) * (128.0 ** -0.5)
    c = np.arange(1024)
    j = np.arange(256)
    ovl = np.clip(np.minimum(c[:, None] * 16 + 32, j[None, :] * 64 + 64) - np.maximum(c[:, None] * 16, j[None, :] * 64), 0, None) / 32.0
    ov = ovl.reshape(8, 128, 256).transpose(1, 0, 2).reshape(128, 8 * 256)
    key = np.arange(4096)
    apat = ((key[None, :] // 64) % 64 == np.arange(64)[:, None]).astype(np.float64) * NEGBIG
    half_n = np.exp((-math.log(500000.0) * np.arange(8, dtype=np.float32) * 2.0 / 16).astype(np.float32))
    half_r = np.exp((-math.log(10000.0) * np.arange(64, dtype=np.float32) * 2.0 / 128).astype(np.float32))
    inv = np.broadcast_to(np.concatenate([half_n, half_r])[None, :], (128, 72))
    f = lambda a: np.ascontiguousarray(a, dtype=np.float32)
    kk = np.arange(128)[:, None]
    r = np.arange(128)[None, :]
    tri = (kk <= r).astype(np.float32)
    anti = (kk > r).astype(np.float32)
    ones = np.ones((128, 128), np.float32)
    winS = np.stack([anti, ones, ones, ones, tri], axis=1).reshape(128, 640)
    return dict(zeta=f(zeta), xiT=f(xiT), dmatT=f(dmatT), ov=f(ov), apat=f(apat), inv=f(inv),
                ident=f(np.eye(128)), winmaskS=f(winS)), tri, anti, ones


def _core_tables(c, tri, anti, ones):
    zeros = np.zeros((128, 128), np.float32)
    q = np.arange(128)
    jj = np.arange(256)
    biasimp = np.zeros((16, 128, 256), np.float32)
    cmpmask = np.zeros((16, 128, 256), np.float32)
    for i in range(16):
        B = 8 * i + c
        cur = 2 * B + (q >= 64).astype(np.int64)
        b = np.zeros((128, 256), np.float32)
        b += 128.0 * ((jj[None, :] >= cur[:, None] - 1) & (jj[None, :] <= cur[:, None]))
        b -= 256.0 * (jj[None, :] > cur[:, None])
        b[:, 0] += 128.0
        biasimp[i] = b
        ncm = i // 2 + 1
        for a in range(2):
            m = ncm - 2 + a
            if m < 0:
                continue
            ci = 128 * m + np.arange(128)
            valid = ((16 * ci[:, None] + 31) <= (128 * B + q[None, :])) & (ci[:, None] <= 1022)
            cmpmask[i, :, a * 128:(a + 1) * 128] = valid
    diag = np.stack([ones if m < c else (tri if m == c else zeros) for m in range(8)], axis=1).reshape(128, 1024)
    w0 = []
    for j in range(5):
        if c - 4 + j < 0:
            w0.append(zeros)
        else:
            w0.append(anti if j == 0 else (tri if j == 4 else ones))
    win0 = np.stack(w0, axis=1).reshape(128, 640)
    oh = np.zeros((128, 8), np.float32)
    oh[:, c] = 1.0
    return dict(biasimp=biasimp, cmpmask=cmpmask, diagmask=np.ascontiguousarray(diag, dtype=np.float32),
                winmask0=np.ascontiguousarray(win0, dtype=np.float32), oh=oh)


_NC_CACHE = {}


def _prep(x, positions, norm_mix, w_in, cmp_pos_k, cmp_pos_v, cmp_k_w1, cmp_k_w2, cmp_v_w1, cmp_v_w2,
          w_proj_a, w_proj_b, w_out, norm_mlp, w_up, w_down, norm_final, cores=range(NCORE)):
    f = lambda a: np.ascontiguousarray(np.asarray(a), dtype=np.float32)
    x2 = f(x).reshape(SEQ, D)
    pos = np.asarray(positions).reshape(SEQ).astype(np.int32)
    consts, tri, anti, ones = _const_tables()
    shared = dict(
        xg=x2, w_in=f(w_in)[0], ckw1=f(cmp_k_w1)[0], ckw2=f(cmp_k_w2)[0], cvw1=f(cmp_v_w1)[0], cvw2=f(cmp_v_w2)[0],
        pekT=np.ascontiguousarray(f(cmp_pos_k)[0].T), pevT=np.ascontiguousarray(f(cmp_pos_v)[0].T),
        wpa=f(w_proj_a)[0], wpb=f(w_proj_b)[0], wout=f(w_out)[0], wup=f(w_up)[0], wdown=f(w_down)[0],
        gmix=np.ascontiguousarray(np.broadcast_to(f(norm_mix)[0][None, :], (128, D))),
        gmlp=np.ascontiguousarray(np.broadcast_to(f(norm_mlp)[0][None, :], (128, D))),
        gfin=np.ascontiguousarray(np.broadcast_to(f(norm_final)[None, :], (128, D))),
        **consts)
    posg = pos.reshape(128, 128).T
    xpad = np.concatenate([np.zeros((512, D), np.float32), x2], axis=0)
    ppad = np.concatenate([np.zeros((512,), np.int32), pos], axis=0)
    in_maps = []
    for c in cores:
        rows = np.concatenate([np.arange(128 * (8 * i + c), 128 * (8 * i + c) + 640) for i in range(16)])
        xo = np.ascontiguousarray(xpad[rows])
        poso = ppad[rows].reshape(80, 128).T
        m = dict(shared)
        m["xo"] = xo
        m["posall"] = np.ascontiguousarray(np.concatenate([posg, poso], axis=1), dtype=np.int32)
        m.update(_core_tables(c, tri, anti, ones))
        in_maps.append(m)
    return in_maps


def kernel(**inputs):
    in_maps = _prep(**inputs)
    if "nc" not in _NC_CACHE:
        _NC_CACHE["nc"] = build_nc()
    res = run_bass_kernel_spmd(_NC_CACHE["nc"], in_maps, core_ids=list(range(NCORE)))
    out = np.zeros((SEQ, D), np.float32)
    for c in range(NCORE):
        yc = np.asarray(res.results[c]["y"]).reshape(16, 128, D)
        for i in range(16):
            B = 8 * i + c
            out[128 * B:128 * B + 128] = yc[i]
    return out.reshape(1, SEQ, D)
```

```python
import math
from contextlib import ExitStack

import numpy as np
import concourse.bass as bass
import concourse.mybir as mybir
from concourse.bass_utils import run_bass_kernel_spmd

F32 = mybir.dt.float32
BF16 = mybir.dt.bfloat16
I32 = mybir.dt.int32
AF = mybir.ActivationFunctionType
ALU = mybir.AluOpType

ENGS = ("pe", "dve", "act", "pool", "sp")
EPOCH = 24000
DMA_RING = {"sp": 8, "pool": 2, "act": 4}
NAMES = {}
CFG = dict(stop=99, p1_tiles=80, p2_tiles=128, p3_i=16, dbg=())

SEQ = 16384
D = 1024
NCORE = 8
NQB = 16
NTG = 128
TWO_PI = float(2 * np.pi)
MAGIC = 12582912.0
NEGBIG = 30000.0
GAM = [1.0 - 2.0 ** (-5.0 - h) for h in range(4)]
LOGG = [math.log(g) for g in GAM]


class _Stop(Exception):
    pass


class Sched:
    def __init__(self, nc):
        self.nc = nc
        self.ops = []
        self.phase = 0

    def next_phase(self):
        self.phase += 1

    def op(self, eng, fn, reads=(), writes=(), dma=False, waw=True):
        self.ops.append(dict(eng=eng, fn=fn, reads=tuple(reads), writes=tuple(writes),
                             dma=dma, waw=waw, deps=set(), signal=False, phase=self.phase))

    def dma(self, eng, out, in_, reads=(), writes=(), waw=True):
        self.op(eng, lambda e: e.dma_start(out=out, in_=in_), reads, writes, dma=True, waw=waw)

    def analyze(self):
        writers, readers = {}, {}
        ops = self.ops
        qcnt, last_on_sem = {}, {}
        for i, o in enumerate(ops):
            if not o["dma"]:
                continue
            n = qcnt.get(o["eng"], 0)
            qcnt[o["eng"]] = n + 1
            ring = DMA_RING[o["eng"]]
            key = ("dmaq", o["eng"], n % ring)
            o["sem"], o["val"] = key, 16 * (n // ring + 1)
            o["signal"] = True
            if key in last_on_sem:
                o["deps"].add(last_on_sem[key])
            last_on_sem[key] = i
        for o in ops:
            if o["fn"] is None:
                o["deps"].update(last_on_sem.values())
        last_eng, last_dma, barrier, cur = {}, {}, set(), 0
        for i, o in enumerate(ops):
            if o["phase"] != cur:
                cur = o["phase"]
                barrier = set(last_eng.values()) | set(last_dma.values())
            o["deps"].update(barrier)
            if o["dma"]:
                last_dma[o["sem"]] = i
            elif o["fn"] is not None:
                last_eng[o["eng"]] = i
        for i, o in enumerate(ops):
            deps = o["deps"]
            for k in o["reads"]:
                deps.update(writers.get(k, ()))
            for k in o["writes"]:
                deps.update(readers.get(k, ()))
                if o["waw"]:
                    deps.update(writers.get(k, ()))
            for k in o["reads"]:
                lst = readers.setdefault(k, [])
                if not o["dma"]:
                    lst[:] = [j for j in lst if ops[j]["dma"] or ops[j]["eng"] != o["eng"]]
                lst.append(i)
            for k in o["writes"]:
                if o["waw"]:
                    writers[k] = [i]
                    readers[k] = []
                else:
                    lst = writers.setdefault(k, [])
                    if not o["dma"]:
                        lst[:] = [j for j in lst if ops[j]["dma"] or ops[j]["eng"] != o["eng"]]
                    lst.append(i)
            deps.discard(i)
            if o["eng"] == "pe" and not o["dma"]:
                for j in [j for j in deps if ops[j]["eng"] == "pe" and not ops[j]["dma"]]:
                    deps.discard(j)
            for j in deps:
                ops[j]["signal"] = True
        cnt = {e: 0 for e in ENGS}
        dcnt = {}
        for o in ops:
            if not o["signal"]:
                continue
            if o["dma"]:
                continue
            else:
                e = o["eng"]
                o["sem"], o["val"] = ("eng", e, cnt[e] // EPOCH), cnt[e] % EPOCH + 1
                cnt[e] += 1
        self.sem_keys = sorted({o["sem"] for o in ops if o["signal"]}, key=str)
        print("sched: ops=%d signals=%s dma=%s nsem=%d" % (len(ops), cnt, qcnt, len(self.sem_keys)), flush=True)

    def emit(self):
        nc = self.nc
        self.analyze()
        ops = self.ops
        with ExitStack() as es:
            sems = {k: es.enter_context(nc.semaphore("s%d" % n)) for n, k in enumerate(self.sem_keys)}
            block = es.enter_context(nc.Block())

            def run_engine(engname, e):
                waited = {}
                for o in ops:
                    if o["eng"] != engname:
                        continue
                    need = {}
                    for j in o["deps"]:
                        k, v = ops[j]["sem"], ops[j]["val"]
                        if waited.get(k, 0) < v and need.get(k, 0) < v:
                            need[k] = v
                    for k, v in need.items():
                        e.wait_ge(sems[k], v)
                        waited[k] = v
                    if o["fn"] is None:
                        continue
                    inst = o["fn"](e)
                    if o["signal"]:
                        inst.then_inc(sems[o["sem"]], 16 if o["dma"] else 1)

            @block.sync
            def _(e):
                run_engine("sp", e)

            @block.scalar
            def _(e):
                run_engine("act", e)

            @block.vector
            def _(e):
                run_engine("dve", e)

            @block.gpsimd
            def _(e):
                run_engine("pool", e)

            @block.tensor
            def _(e):
                run_engine("pe", e)


C_Q, C_KC, C_VC, C_KS, C_VS, C_KW, C_VW, C_NG, C_RQ, C_RK, C_RV, C_RG, C_GA, C_GB = (
    0, 512, 640, 768, 896, 1024, 1152, 1280, 1304, 1816, 2328, 2840, 3352, 4376)


def build_nc():
    nc = bass.Bass("TRN2", target_bir_lowering=False)

    def din(name, shape, dt=F32):
        return nc.dram_tensor(name, list(shape), dt, kind="ExternalInput").ap()

    def dscr(name, shape, dt):
        kind = "ExternalOutput" if name in CFG["dbg"] else "Internal"
        return nc.dram_tensor(name, list(shape), dt, kind=kind).ap()

    xg = din("xg", [SEQ, D])
    xo = din("xo", [80 * 128, D])
    posall = din("posall", [128, 208], I32)
    inv = din("inv", [128, 72])
    w_in = din("w_in", [D, 5400])
    ckw1 = din("ckw1", [2048, 256])
    ckw2 = din("ckw2", [256, 64])
    cvw1 = din("cvw1", [2048, 256])
    cvw2 = din("cvw2", [256, 64])
    pekT = din("pekT", [64, 32])
    pevT = din("pevT", [64, 32])
    wpa = din("wpa", [512, D])
    wpb = din("wpb", [512, D])
    wout = din("wout", [D, D])
    wup = din("wup", [D, 4096])
    wdown = din("wdown", [4096, D])
    gmix = din("gmix", [128, D])
    gmlp = din("gmlp", [128, D])
    gfin = din("gfin", [128, D])
    biasimp = din("biasimp", [16, 128, 256])
    cmpmask = din("cmpmask", [16, 128, 256])
    diagmask = din("diagmask", [128, 8 * 128])
    winmask0 = din("winmask0", [128, 5 * 128])
    winmaskS = din("winmaskS", [128, 5 * 128])
    ohd = din("oh", [128, 8])
    ovd = din("ov", [128, 8 * 256])
    apat = din("apat", [64, 4096])
    identd = din("ident", [128, 128])
    zetad = din("zeta", [128, 4])
    xiTd = din("xiT", [128, 512])
    dmatTd = din("dmatT", [128, 512])
    y = nc.dram_tensor("y", [16, 128, D], F32, kind="ExternalOutput").ap()

    ROT = dscr("ROT", [128, 208, 144], F32)
    QT = dscr("QT", [16, 64, 1024], BF16)
    GN = dscr("GN", [16, 128, 24], F32)
    RQT = dscr("RQT", [16, 128, 512], BF16)
    RKT = dscr("RKT", [16, 128, 512], BF16)
    RV = dscr("RV", [16, 128, 512], BF16)
    SG = dscr("SG", [16, 128, 512], F32)
    KWT = dscr("KWT", [16, 64, 1280], BF16)
    VW = dscr("VW", [16, 128, 640], BF16)
    HT = dscr("HT", [16, 128, 1024], BF16)
    KCT = dscr("KCT", [128, 16512], BF16)
    VCT = dscr("VCT", [128, 16512], BF16)
    NSAT = dscr("NSAT", [16, 128, 512], BF16)
    RETT = dscr("RETT", [16, 128, 512], BF16)
    XMID = dscr("XMID", [16, 128, 1024], F32)
    H2T = dscr("H2T", [16, 128, 1024], BF16)

    s = Sched(nc)

    def ACT(out, in_, func, R, W, waw=True, **kw):
        s.op("act", lambda e: e.activation(out=out, in_=in_, func=func, **kw), R, W, waw=waw)

    def TT(eng, out, a, b, op, R, W, waw=True):
        s.op(eng, lambda e: e.tensor_tensor(out=out, in0=a, in1=b, op=op), R, W, waw=waw)

    def TS(eng, out, a, s1, s2, op0, op1, R, W, waw=True):
        if op1 is None:
            s.op(eng, lambda e: e.tensor_scalar(out=out, in0=a, scalar1=s1, scalar2=None, op0=op0), R, W, waw=waw)
        else:
            s.op(eng, lambda e: e.tensor_scalar(out=out, in0=a, scalar1=s1, scalar2=s2, op0=op0, op1=op1), R, W, waw=waw)

    def STT(eng, out, in0, scalar, in1, op0, op1, R, W, waw=True):
        s.op(eng, lambda e: e.scalar_tensor_tensor(out=out, in0=in0, scalar=scalar, in1=in1, op0=op0, op1=op1),
             R, W, waw=waw)

    def CP(eng, out, in_, R, W, waw=True):
        if eng == "act":
            ACT(out, in_, AF.Copy, R, W, waw=waw)
        else:
            s.op(eng, lambda e: e.tensor_copy(out=out, in_=in_), R, W, waw=waw)

    def RECIP(out, in_, R, W):
        s.op("dve", lambda e: e.reciprocal(out=out, in_=in_), R, W)

    def MM(out, lhsT, rhs, start, stop, R, W, skip=False):
        s.op("pe", lambda e: e.matmul(out, lhsT=lhsT, rhs=rhs, start=start, stop=stop, skip_group_check=skip),
             R, W, waw=False)

    def TR(out, in_, ident, R, W):
        s.op("pe", lambda e: e.transpose(out=out, in_=in_, identity=ident), R, W, waw=False)

    def MEMSET(eng, ap, val, W, waw=True):
        s.op(eng, lambda e: e.memset(ap, val), (), W, waw=waw)

    def wload(dst, src2d, key):
        K = dst.shape[1]
        for k in range(K):
            s.dma("pool", dst[:, k, :], src2d[k * 128:(k + 1) * 128, :], writes=[key], waw=False)

    def bc(ap, shape):
        return ap.to_broadcast(list(shape))

    top = ExitStack()
    try:
        uniq = [0]

        def T(es, name, shape, dt):
            uniq[0] += 1
            NAMES[name] = "sb%d_%s" % (uniq[0], name)
            return es.enter_context(nc.sbuf_tensor("sb%d_%s" % (uniq[0], name), list(shape), dt))

        def P(es, name, shape, dt):
            uniq[0] += 1
            return es.enter_context(nc.psum_tensor("ps%d_%s" % (uniq[0], name), list(shape), dt))

        identf = T(top, "identf", [128, 128], F32)
        identb = T(top, "identb", [128, 128], BF16)
        epsb = T(top, "epsb", [128, 1], F32)
        s.dma("sp", identf[:], identd, writes=["identf"])
        CP("dve", identb[:], identf[:], ["identf"], ["identb"])
        MEMSET("pool", epsb[:], 1e-6, ["epsb"])

        def rmsnorm(es_tmp, xt, kx, grep, kg, hb, khb, tag):
            junk, ss, rstd = es_tmp
            ACT(junk[:], xt, AF.Square, [kx], ["junk" + tag, "ss" + tag], accum_out=ss[:])
            ACT(rstd[:], ss[:], AF.Ln, ["ss" + tag, "epsb"], ["rstd" + tag], scale=1.0 / D, bias=epsb[:])
            ACT(rstd[:], rstd[:], AF.Exp, ["rstd" + tag], ["rstd" + tag], scale=-0.5)
            STT("dve", hb, xt, rstd[:], grep, ALU.mult, ALU.mult, [kx, "rstd" + tag, kg], [khb])

        def rotary(eng, x1, x2, cos, sin, o1, o2, tmp, R, W, tag):
            t1, t2, t3, t4 = tmp
            kt = ["rt%d%s" % (n, tag) for n in range(4)]
            TT(eng, t1, x1, cos, ALU.mult, R, [kt[0]])
            TT(eng, t2, x2, sin, ALU.mult, R, [kt[1]])
            TT(eng, t3, x1, sin, ALU.mult, R, [kt[2]])
            TT(eng, t4, x2, cos, ALU.mult, R, [kt[3]])
            TT(eng, o1, t1, t2, ALU.subtract, [kt[0], kt[1]], W, waw=False)
            TT(eng, o2, t4, t3, ALU.add, [kt[2], kt[3]], W, waw=False)

        p0 = ExitStack()
        if True:
            es = p0
            posi = T(es, "posi", [128, 208], I32)
            posf = T(es, "posf", [128, 208], F32)
            invs = T(es, "invs", [128, 72], F32)
            s.dma("sp", posi[:], posall, writes=["posi"])
            s.dma("sp", invs[:], inv, writes=["invs"])
            CP("dve", posf[:], posi[:], ["posi"], ["posf"])
            CH = 16
            ang = [T(es, "ang%d" % n, [128, CH, 72], F32) for n in range(2)]
            a2 = T(es, "a2", [128, CH, 72], F32)
            kf = T(es, "kf", [128, CH, 72], F32)
            rr = T(es, "rr", [128, CH, 72], F32)
            rotc = [T(es, "rotc%d" % n, [128, CH, 144], F32) for n in range(2)]

            def p0_chunk(ci):
                pr = ci % 2
                c0 = ci * CH
                TT("dve", ang[pr][:], bc(invs[:].unsqueeze(1), [128, CH, 72]),
                   bc(posf[:, c0:c0 + CH].unsqueeze(2), [128, CH, 72]), ALU.mult, ["invs", "posf"], ["ang%d" % pr])
                for which in range(2):
                    if which == 0:
                        src, ksrc = ang[pr], "ang%d" % pr
                    else:
                        TS("dve", a2[:], ang[pr][:], float(np.pi / 2), None, ALU.add, None, ["ang%d" % pr], ["a2"])
                        src, ksrc = a2, "a2"
                    TS("dve", kf[:], src[:], 1.0 / TWO_PI, MAGIC, ALU.mult, ALU.add, [ksrc], ["kf"])
                    TS("dve", kf[:], kf[:], -MAGIC, None, ALU.add, None, ["kf"], ["kf"])
                    STT("dve", rr[:], kf[:], -TWO_PI, src[:], ALU.mult, ALU.add, ["kf", ksrc], ["rr"])
                    TS("dve", rr[:], rr[:], 3.14159, -3.14159, ALU.min, ALU.max, ["rr"], ["rr"])
                    ACT(rotc[pr][:, :, which * 72:(which + 1) * 72], rr[:], AF.Sin, ["rr"], ["rotc%d" % pr], waw=False)
                s.dma("sp", ROT[:, c0:c0 + CH, :], rotc[pr][:], reads=["rotc%d" % pr], writes=["ROT%d" % ci], waw=False)

            for ci in range(8, 13):
                p0_chunk(ci)

        if CFG['stop'] < 1:
            raise _Stop()
        with ExitStack() as es:
            Wq = T(es, "Wq", [128, 8, 512], BF16)
            Wng = T(es, "Wng", [128, 8, 24], BF16)
            Wr = T(es, "Wr", [128, 8, 2048], BF16)
            Ww = T(es, "Ww", [128, 8, 256], BF16)
            grep = T(es, "grep1", [128, D], F32)
            s.dma("sp", grep[:], gmix, writes=["grep"])
            wload(Ww, w_in[:, C_KW:C_KW + 256], "Ww")
            wload(Wq, w_in[:, C_Q:C_Q + 512], "Wq")
            wload(Wng, w_in[:, C_NG:C_NG + 24], "Wng")
            wload(Wr, w_in[:, C_RQ:C_RQ + 2048], "Wr")
            xt = [T(es, "xt%d" % n, [128, D], F32) for n in range(2)]
            rot = [T(es, "rot%d" % n, [128, 144], F32) for n in range(2)]
            hb = [T(es, "hb%d" % n, [128, D], BF16) for n in range(2)]
            hT = [T(es, "hT%d" % n, [128, 8, 128], BF16) for n in range(2)]
            junk = T(es, "junk", [128, D], F32)
            ss = T(es, "ss", [128, 1], F32)
            rstd = T(es, "rstd", [128, 1], F32)
            kwf = T(es, "kwf", [128, 128], F32)
            kwb = T(es, "kwb", [128, 128], BF16)
            vwb = [T(es, "vwb%d" % n, [128, 128], BF16) for n in range(2)]
            kwT = [T(es, "kwT%d" % n, [64, 2, 128], BF16) for n in range(2)]
            rtmp_s = [T(es, "rts%d" % n, [128, 8, 8], F32) for n in range(4)]
            rtmp_b = [T(es, "rtb%d" % n, [128, 4, 64], F32) for n in range(4)]
            qf = T(es, "qf", [128, 512], F32)
            qb = T(es, "qb", [128, 512], BF16)
            qT = T(es, "qT", [64, 1024], BF16)
            ge = T(es, "ge", [128, 24], F32)
            rf = T(es, "rf", [128, 512], F32)
            rqb = T(es, "rqb", [128, 512], BF16)
            rkb = T(es, "rkb", [128, 512], BF16)
            rvb = T(es, "rvb", [128, 512], BF16)
            rTs = [T(es, "rTs%d" % n, [128, 512], BF16) for n in range(2)]
            sge = T(es, "sge", [128, 512], F32)
            sgo = T(es, "sgo", [128, 512], F32)
            pT = [P(es, "pT%d" % n, [128, 1024], BF16) for n in range(2)]
            pT2 = P(es, "pT2", [128, 1024], BF16)
            pW = P(es, "pW", [128, 512], F32)
            pQ = P(es, "pQ", [128, 512], F32)
            pR = P(es, "pR", [128, 2, 512], F32)

            def p1_load(t):
                pr = t % 2
                s.dma("sp", xt[pr][:], xo[t * 128:(t + 1) * 128, :], writes=["xt%d" % pr])
                s.dma("sp", rot[pr][:], ROT[:, 128 + t, :], reads=["ROT%d" % ((128 + t) // 16)], writes=["rot%d" % pr])

            p1_load(0)
            for t in range(CFG['p1_tiles']):
                i, j = divmod(t, 5)
                pr = t % 2
                if t + 1 < CFG['p1_tiles']:
                    p1_load(t + 1)
                if t % 8 == 2 and t // 8 < 8:
                    p0_chunk(t // 8)
                kx, kr, khb, khT, kpT = "xt%d" % pr, "rot%d" % pr, "hb%d" % pr, "hT%d" % pr, "pT%d" % pr
                rmsnorm((junk, ss, rstd), xt[pr][:], kx, grep[:], "grep", hb[pr][:], khb, "1")
                for k in range(8):
                    TR(pT[pr][:, k * 128:(k + 1) * 128], hb[pr][:, k * 128:(k + 1) * 128], identb[:], [khb, "identb"], [kpT])
                CP("act", hT[pr][:].rearrange("p k t -> p (k t)"), pT[pr][:], [kpT], [khT])
                for k in range(8):
                    MM(pW[:, 0:256], hT[pr][:, k, :], Ww[:, k, :], k == 0, k == 7, [khT, "Ww"], ["pW"])
                CP("act", vwb[pr][:], pW[:, 128:256], ["pW"], ["vwb%d" % pr])
                s.dma("sp", VW[i][:, j * 128:(j + 1) * 128], vwb[pr][:], reads=["vwb%d" % pr], writes=["VW%d" % i], waw=False)
                CP("act", kwf[:], pW[:, 0:128], ["pW"], ["kwf"])
                CP("pool", kwb[:], kwf[:], ["kwf"], ["kwb"])
                kv = kwf[:].rearrange("p (g d) -> p g d", g=2)
                ko = kwb[:].rearrange("p (g d) -> p g d", g=2)
                rotary("dve", kv[:, :, 0:8], kv[:, :, 8:16], bc(rot[pr][:, 72:80].unsqueeze(1), [128, 2, 8]),
                       bc(rot[pr][:, 0:8].unsqueeze(1), [128, 2, 8]), ko[:, :, 0:8], ko[:, :, 8:16],
                       [r[:, 0:2, :] for r in rtmp_s], ["kwf", kr, "kwb"], ["kwb"], "s")
                for g in range(2):
                    TR(pT2[0:64, g * 128:(g + 1) * 128], kwb[:, g * 64:(g + 1) * 64], identb[:], ["kwb", "identb"], ["pT2"])
                CP("dve", kwT[pr][:].rearrange("p g t -> p (g t)"), pT2[0:64, 0:256], ["pT2"], ["kwT%d" % pr])
                s.dma("sp", KWT[i].rearrange("p (g t) -> p g t", g=2)[:, :, j * 128:(j + 1) * 128], kwT[pr][:],
                      reads=["kwT%d" % pr], writes=["KWT%d" % i], waw=False)
                if j != 4:
                    continue
                s.dma("sp", HT[i], hT[pr][:].rearrange("p k t -> p (k t)"), reads=[khT], writes=["HT%d" % i])
                for k in range(8):
                    MM(pQ[:], hT[pr][:, k, :], Wq[:, k, :], k == 0, k == 7, [khT, "Wq"], ["pQ"])
                for k in range(8):
                    MM(pW[:, 256:280], hT[pr][:, k, :], Wng[:, k, :], k == 0, k == 7, [khT, "Wng"], ["pW"])
                ACT(qf[:], pQ[:], AF.Copy, ["pQ"], ["qf"], scale=0.125)
                CP("pool", qb[:], qf[:], ["qf"], ["qb"])
                qv = qf[:].rearrange("p (h d) -> p h d", h=8)
                qo = qb[:].rearrange("p (h d) -> p h d", h=8)
                rotary("dve", qv[:, :, 0:8], qv[:, :, 8:16], bc(rot[pr][:, 72:80].unsqueeze(1), [128, 8, 8]),
                       bc(rot[pr][:, 0:8].unsqueeze(1), [128, 8, 8]), qo[:, :, 0:8], qo[:, :, 8:16],
                       [r[:] for r in rtmp_s], ["qf", kr, "qb"], ["qb"], "s")
                for h in range(8):
                    TR(pT2[0:64, h * 128:(h + 1) * 128], qb[:, h * 64:(h + 1) * 64], identb[:], ["qb", "identb"], ["pT2"])
                CP("dve", qT[:], pT2[0:64, :], ["pT2"], ["qT"])
                s.dma("sp", QT[i], qT[:], reads=["qT"], writes=["QT%d" % i])
                ACT(ge[:], pW[:, 256:280], AF.Exp, ["pW"], ["ge"], scale=-1.0)
                TS("dve", ge[:], ge[:], 1.0, None, ALU.add, None, ["ge"], ["ge"])
                RECIP(ge[:], ge[:], ["ge"], ["ge"])
                s.dma("sp", GN[i], ge[:], reads=["ge"], writes=["GN%d" % i])
                for half in range(2):
                    for b in range(2):
                        c0 = (half * 2 + b) * 512
                        for k in range(8):
                            MM(pR[:, b, :], hT[pr][:, k, :], Wr[:, k, c0:c0 + 512], k == 0, k == 7, [khT, "Wr"], ["pR%d" % b])
                    if half == 0:
                        for b, (dst, dsc, kd) in enumerate(((rqb, RQT, "RQT"), (rkb, RKT, "RKT"))):
                            CP("act", rf[:], pR[:, b, :], ["pR%d" % b], ["rf"])
                            rv_ = rf[:].rearrange("p (h d) -> p h d", h=4)
                            ro = dst[:].rearrange("p (h d) -> p h d", h=4)
                            rotary("pool", rv_[:, :, 0:64], rv_[:, :, 64:128], bc(rot[pr][:, 80:144].unsqueeze(1), [128, 4, 64]),
                                   bc(rot[pr][:, 8:72].unsqueeze(1), [128, 4, 64]), ro[:, :, 0:64], ro[:, :, 64:128],
                                   [r[:] for r in rtmp_b], ["rf", kr, "rdst%d" % b], ["rdst%d" % b], "b")
                            for h in range(4):
                                TR(pT2[:, h * 128:(h + 1) * 128], dst[:, h * 128:(h + 1) * 128], identb[:], ["rdst%d" % b, "identb"], ["pT2"])
                            CP("dve", rTs[b][:], pT2[:, 0:512], ["pT2"], ["rTs%d" % b])
                            s.dma("sp", dsc[i], rTs[b][:], reads=["rTs%d" % b], writes=["%s%d" % (kd, i)])
                    else:
                        CP("act", rvb[:], pR[:, 0, :], ["pR0"], ["rvb"])
                        s.dma("sp", RV[i], rvb[:], reads=["rvb"], writes=["RV%d" % i])
                        ACT(sge[:], pR[:, 1, :], AF.Exp, ["pR1"], ["sge"], scale=-1.0)
                        TS("dve", sge[:], sge[:], 1.0, None, ALU.add, None, ["sge"], ["sge"])
                        RECIP(sge[:], sge[:], ["sge"], ["sge"])
                        TT("dve", sgo[:], pR[:, 1, :], sge[:], ALU.mult, ["pR1", "sge"], ["sgo"])
                        s.dma("sp", SG[i], sgo[:], reads=["sgo"], writes=["SG%d" % i])

        if CFG['stop'] < 2:
            raise _Stop()
        p0.close()
        s.next_phase()
        with ExitStack() as mid:
            KsTa = [T(mid, "KsTa%d" % g, [128, SEQ], BF16) for g in range(2)]
            Vsa = T(mid, "Vsa", [128, 128, 2, 65], BF16)
            Sown = T(mid, "Sown", [128, 16, 512], BF16)
            kcmpT = T(mid, "kcmpT", [64, 2, 1024], BF16)
            vcmpa = T(mid, "vcmpa", [128, 8, 2, 65], BF16)
            for g in range(2):
                for r8 in range(8):
                    s.dma("pool", KsTa[g][64:128, r8 * 2048:(r8 + 1) * 2048], apat[:, (r8 % 2) * 2048:(r8 % 2 + 1) * 2048],
                          writes=["KsTa%d" % g], waw=False)
            MEMSET("pool", Vsa[:, :, :, 64:65], 1.0, ["Vsa"], waw=False)
            MEMSET("pool", vcmpa[:, :, :, 64:65], 1.0, ["vcmpa"], waw=False)

            with ExitStack() as es:
                W1g = T(es, "W1g", [128, 8, 512], BF16)
                W2g = T(es, "W2g", [128, 8, 1024], BF16)
                grep = T(es, "grep2", [128, D], F32)
                zeta = T(es, "zeta", [128, 4], F32)
                oh = T(es, "oh", [128, 8], F32)
                s.dma("sp", grep[:], gmix, writes=["grep2"])
                s.dma("sp", zeta[:], zetad, writes=["zeta"])
                s.dma("sp", oh[:], ohd, writes=["oh"])
                wload(W1g, w_in[:, C_KC:C_KC + 512], "W1g")
                wload(W2g, w_in[:, C_RK:C_RK + 1024], "W2g")
                xt = [T(es, "gxt%d" % n, [128, D], F32) for n in range(2)]
                rot3 = [T(es, "grot%d" % n, [128, 144], F32) for n in range(3)]
                hb = [T(es, "ghb%d" % n, [128, D], BF16) for n in range(2)]
                hT = [T(es, "ghT%d" % n, [128, 8, 128], BF16) for n in range(2)]
                junk = T(es, "gjunk", [128, D], F32)
                ss = T(es, "gss", [128, 1], F32)
                rstd = T(es, "grstd", [128, 1], F32)
                abf = T(es, "abf", [128, 384], BF16)
                af = T(es, "af", [128, 384], F32)
                kvst = [T(es, "kvst%d" % n, [128, 256], BF16) for n in range(2)]
                rkf = T(es, "rkf", [128, 512], F32)
                rkb = T(es, "grkb", [128, 512], BF16)
                rvz = T(es, "rvz", [128, 512], BF16)
                rtmp_s = [T(es, "grts%d" % n, [128, 2, 2, 8], F32) for n in range(4)]
                rtmp_b = [T(es, "grtb%d" % n, [128, 4, 64], F32) for n in range(4)]
                Sst = T(es, "Sst", [128, 512], F32)
                acc = T(es, "acc", [128, 512], F32)
                zpad = T(es, "zpad", [128, 16], BF16)
                pT = [P(es, "gpT%d" % n, [128, 1024], BF16) for n in range(2)]
                pT2 = P(es, "gpT2", [128, 1024], BF16)
                pA2 = [P(es, "pA%d" % n, [128, 512], F32) for n in range(2)]
                pB = P(es, "pB", [128, 2, 512], F32)
                pL = P(es, "pL", [128, 512], F32)
                MEMSET("pool", Sst[:], 0.0, ["Sst"])
                MEMSET("pool", zpad[:], 0.0, ["zpad"])
                s.dma("sp", KCT[:, SEQ:SEQ + 16], zpad[:], reads=["zpad"], writes=["KCT"], waw=False)
                s.dma("sp", VCT[:, SEQ:SEQ + 16], zpad[:], reads=["zpad"], writes=["VCT"], waw=False)

                def p2_load(t):
                    pr = t % 2
                    s.dma("sp", xt[pr][:], xg[t * 128:(t + 1) * 128, :], writes=["gxt%d" % pr])
                    s.dma("sp", rot3[t % 3][:], ROT[:, t, :], reads=["ROT%d" % (t // 16)], writes=["grot%d" % (t % 3)])

                def p2_keys(t):
                    pr = t % 2
                    return pr, "gxt%d" % pr, "grot%d" % (t % 3), "ghb%d" % pr, "ghT%d" % pr, "gpT%d" % pr

                def p2_A1(t):
                    pr, kx, kr, khb, khT, kpT = p2_keys(t)
                    rmsnorm((junk, ss, rstd), xt[pr][:], kx, grep[:], "grep2", hb[pr][:], khb, "2")

                def p2_A2(t):
                    pr, kx, kr, khb, khT, kpT = p2_keys(t)
                    for k in range(8):
                        TR(pT[pr][:, k * 128:(k + 1) * 128], hb[pr][:, k * 128:(k + 1) * 128], identb[:], [khb, "identb"], [kpT])
                    CP("act", hT[pr][:].rearrange("p k t -> p (k t)"), pT[pr][:], [kpT], [khT])

                def p2_B1(t):
                    pr, kx, kr, khb, khT, kpT = p2_keys(t)
                    for k in range(8):
                        MM(pA2[pr][:], hT[pr][:, k, :], W1g[:, k, :], k == 0, k == 7, [khT, "W1g"], ["pA%d" % pr])
                    for b in range(2):
                        for k in range(8):
                            MM(pB[:, b, :], hT[pr][:, k, :], W2g[:, k, b * 512:(b + 1) * 512], k == 0, k == 7, [khT, "W2g"], ["pB%d" % b])

                def p2_B2a(t):
                    pr, kx, kr, khb, khT, kpT = p2_keys(t)
                    for h in range(4):
                        TS("dve", rvz[:, h * 128:(h + 1) * 128], pB[:, 1, h * 128:(h + 1) * 128], zeta[:, h:h + 1], None, ALU.mult, None,
                           ["pB1", "zeta"], ["rvz"], waw=(h == 0))
                    pA, kpA = pA2[pr], "pA%d" % pr
                    CP("act", rkf[:], pB[:, 0, :], ["pB0"], ["rkf"])
                    CP("act", abf[:], pA[:, 0:384], [kpA], ["abf"])
                    CP("act", Vsa[:, t, :, 0:64], pA[:, 384:512].rearrange("p (g d) -> p g d", g=2), [kpA], ["Vsa"], waw=False)
                    CP("act", af[:], pA[:, 0:384], [kpA], ["af"])
                    for a_ in (0, 2):
                        av = af[:, a_ * 128:(a_ + 1) * 128].rearrange("p (g d) -> p g d", g=2)
                        ao = abf[:, a_ * 128:(a_ + 1) * 128].rearrange("p (g d) -> p g d", g=2)
                        rotary("dve", av[:, :, 0:8], av[:, :, 8:16], bc(rot3[t % 3][:, 72:80].unsqueeze(1), [128, 2, 8]),
                               bc(rot3[t % 3][:, 0:8].unsqueeze(1), [128, 2, 8]), ao[:, :, 0:8], ao[:, :, 8:16],
                               [r[:, 0, :, :] for r in rtmp_s], ["af", kr, "abf"], ["abf"], "gs")
                    rv_ = rkf[:].rearrange("p (h d) -> p h d", h=4)
                    ro = rkb[:].rearrange("p (h d) -> p h d", h=4)
                    rotary("pool", rv_[:, :, 0:64], rv_[:, :, 64:128], bc(rot3[t % 3][:, 80:144].unsqueeze(1), [128, 4, 64]),
                           bc(rot3[t % 3][:, 8:72].unsqueeze(1), [128, 4, 64]), ro[:, :, 0:64], ro[:, :, 64:128],
                           [r[:] for r in rtmp_b], ["rkf", kr, "grkb"], ["grkb"], "gb")

                def p2_B2b(t):
                    pr, kx, kr, khb, khT, kpT = p2_keys(t)
                    TR(pT2[:, 0:128], abf[:, 0:128], identb[:], ["abf", "identb"], ["gpT2"])
                    TR(pT2[:, 128:256], abf[:, 128:256], identb[:], ["abf", "identb"], ["gpT2"])
                    TR(pT2[0:64, 256:384], abf[:, 256:320], identb[:], ["abf", "identb"], ["gpT2"])
                    TR(pT2[0:64, 384:512], abf[:, 320:384], identb[:], ["abf", "identb"], ["gpT2"])
                    CP("act", kvst[pr][:], pT2[:, 0:256], ["gpT2"], ["kvst%d" % pr])
                    s.dma("sp", KCT[:, t * 128:(t + 1) * 128], kvst[pr][:, 0:128], reads=["kvst%d" % pr], writes=["KCT"], waw=False)
                    s.dma("sp", VCT[:, t * 128:(t + 1) * 128], kvst[pr][:, 128:256], reads=["kvst%d" % pr], writes=["VCT"], waw=False)
                    CP("act", KsTa[0][0:64, t * 128:(t + 1) * 128], pT2[0:64, 256:384], ["gpT2"], ["KsTa0"], waw=False)
                    CP("act", KsTa[1][0:64, t * 128:(t + 1) * 128], pT2[0:64, 384:512], ["gpT2"], ["KsTa1"], waw=False)
                    m = t % 8
                    if m == 0:
                        TS("dve", acc[:], Sst[:], oh[:, 0:1], None, ALU.mult, None, ["Sst", "oh"], ["acc"])
                    else:
                        STT("dve", acc[:], Sst[:], oh[:, m:m + 1], acc[:], ALU.mult, ALU.add, ["Sst", "oh", "acc"], ["acc"])
                    if m == 7:
                        CP("act", Sown[:, t // 8, :], acc[:], ["acc"], ["Sown"], waw=False)
                    for h in range(4):
                        MM(pL[:, h * 128:(h + 1) * 128], rkb[:, h * 128:(h + 1) * 128], rvz[:, h * 128:(h + 1) * 128],
                           True, True, ["grkb", "rvz"], ["pL"])
                    for h in range(4):
                        STT("dve", Sst[:, h * 128:(h + 1) * 128], Sst[:, h * 128:(h + 1) * 128], float(GAM[h] ** 128),
                            pL[:, h * 128:(h + 1) * 128], ALU.mult, ALU.add, ["Sst", "pL"], ["Sst"])

                NT2 = CFG['p2_tiles']
                for t0 in range(min(2, NT2)):
                    p2_load(t0)
                if NT2 > 0:
                    p2_A1(0)
                    p2_A2(0)
                for t in range(NT2):
                    if t + 2 < NT2:
                        p2_load(t + 2)
                    if t + 1 < NT2:
                        p2_A1(t + 1)
                    p2_B1(t)
                    if t + 1 < NT2:
                        p2_A2(t + 1)
                    if t >= 1:
                        p2_B2b(t - 1)
                    p2_B2a(t)
                if NT2 > 0:
                    p2_B2b(NT2 - 1)
            if CFG['stop'] < 3:
                raise _Stop()
            s.next_phase()
            with ExitStack() as es:
                w1 = [T(es, "w1_%d" % n, [128, 32, 256], BF16) for n in range(2)]
                w2 = [T(es, "w2_%d" % n, [128, 2, 64], BF16) for n in range(2)]
                peT = [T(es, "peT%d" % n, [64, 32], BF16) for n in range(2)]
                bia = T(es, "bia", [128, 4], F32)
                nbia = T(es, "nbia", [128, 4], F32)
                src = [T(es, "csrc%d" % n, [128, 4112], BF16) for n in range(2)]
                hid = T(es, "hid", [128, 2, 2, 256], BF16)
                ex = T(es, "cex", [128, 256], F32)
                pC = [P(es, "pC%d" % n, [128, 512], F32) for n in range(2)]
                pK = P(es, "pK", [128, 512], F32)
                pBi = P(es, "pBi", [128, 512], F32)
                for kind, (wd1, wd2, ped) in enumerate(((ckw1, ckw2, pekT), (cvw1, cvw2, pevT))):
                    w1v = wd1.rearrange("(l d) h -> d l h", d=64)
                    for half in range(2):
                        for l4 in range(4):
                            s.dma("pool", w1[kind][half * 64:(half + 1) * 64, l4 * 8:(l4 + 1) * 8, :], w1v[:, l4 * 8:(l4 + 1) * 8, :],
                                  writes=["w1_%d" % kind], waw=False)
                    s.dma("pool", w2[kind][:], wd2.rearrange("(c p) d -> p c d", p=128), writes=["w2_%d" % kind])
                    s.dma("pool", peT[kind][:], ped, writes=["peT%d" % kind])
                    for hc in range(2):
                        for l in range(32):
                            MM(pBi[:, kind * 2 + hc:kind * 2 + hc + 1], w1[kind][0:64, l, hc * 128:(hc + 1) * 128], peT[kind][:, l:l + 1],
                               l == 0 and kind == 0 and hc == 0, l == 31, ["w1_%d" % kind, "peT%d" % kind], ["pBi"], skip=True)
                CP("dve", bia[:], pBi[:, 0:4], ["pBi"], ["bia"])
                TS("dve", nbia[:], bia[:], -1.0, None, ALU.mult, None, ["bia"], ["nbia"])
                cnt = 0
                for qd in range(CFG.get('p2b_q', 4)):
                    for kind, SRC in enumerate((KCT, VCT)):
                        s.dma("sp", src[kind][:], SRC[:, qd * 4096:qd * 4096 + 4112], reads=["KCT", "VCT"], writes=["csrc%d" % kind])
                        for g in range(2):
                            for hc in range(2):
                                pc = pC[cnt % 2]
                                kpc = "pC%d" % (cnt % 2)
                                cnt += 1
                                for l in range(32):
                                    MM(pc[:, 0:256], w1[kind][g * 64:(g + 1) * 64, l, hc * 128:(hc + 1) * 128],
                                       src[kind][g * 64:(g + 1) * 64, l:l + 4081:16], l == 0, l == 31,
                                       ["w1_%d" % kind, "csrc%d" % kind], [kpc])
                                col = kind * 2 + hc
                                ACT(ex[:], pc[:, 0:256], AF.Exp, [kpc, "nbia"], ["cex"], scale=-1.0, bias=nbia[:, col:col + 1])
                                TS("dve", ex[:], ex[:], 1.0, None, ALU.add, None, ["cex"], ["cex"])
                                RECIP(ex[:], ex[:], ["cex"], ["cex"])
                                STT("dve", hid[:, g, hc, :], pc[:, 0:256], bia[:, col:col + 1], ex[:], ALU.add, ALU.mult,
                                    [kpc, "bia", "cex"], ["hid"], waw=False)
                        if kind == 0:
                            for g in range(2):
                                for hc in range(2):
                                    MM(pK[0:64, g * 256:(g + 1) * 256], w2[0][:, hc, :], hid[:, g, hc, :], hc == 0 and g == 0, hc == 1,
                                       ["w2_0", "hid"], ["pK"], skip=True)
                            CP("act", kcmpT[:, :, qd * 256:(qd + 1) * 256], pK[0:64, :].rearrange("p (g b) -> p g b", g=2),
                               ["pK"], ["kcmpT"], waw=False)
                        else:
                            for g in range(2):
                                for bch in range(2):
                                    for hc in range(2):
                                        MM(pK[:, (g * 2 + bch) * 64:(g * 2 + bch + 1) * 64], hid[:, g, hc, bch * 128:(bch + 1) * 128],
                                           w2[1][:, hc, :], hc == 0 and g == 0 and bch == 0, hc == 1, ["w2_1", "hid"], ["pK"], skip=True)
                            for bch in range(2):
                                CP("act", vcmpa[:, qd * 2 + bch, :, 0:64],
                                   pK[:, 0:256].rearrange("p (g b d) -> p g b d", g=2, b=2)[:, :, bch, :], ["pK"], ["vcmpa"], waw=False)

            if CFG['stop'] < 4:
                raise _Stop()
            s.next_phase()
            with ExitStack() as es:
                ov = T(es, "ov", [128, 8, 256], BF16)
                diag = T(es, "diag", [128, 8, 128], F32)
                winS = T(es, "winS", [128, 5, 128], F32)
                win0 = T(es, "win0", [128, 5, 128], F32)
                xiT = T(es, "xiT", [128, 512], F32)
                dmatT = T(es, "dmatT", [128, 512], F32)
                s.dma("pool", ov[:].rearrange("p m j -> p (m j)"), ovd, writes=["ov"])
                s.dma("sp", diag[:].rearrange("p m j -> p (m j)"), diagmask, writes=["diag"])
                s.dma("sp", winS[:].rearrange("p m j -> p (m j)"), winmaskS, writes=["winS"])
                s.dma("sp", win0[:].rearrange("p m j -> p (m j)"), winmask0, writes=["win0"])
                s.dma("sp", xiT[:], xiTd, writes=["xiT"])
                s.dma("sp", dmatT[:], dmatTd, writes=["dmatT"])
                qt = [T(es, "qt%d" % n, [64, 1024], BF16) for n in range(2)]
                gn = [T(es, "gn%d" % n, [128, 24], F32) for n in range(2)]
                kwt = [T(es, "kwt%d" % n, [64, 2, 640], BF16) for n in range(2)]
                vwa = [T(es, "vwa%d" % n, [128, 5, 2, 65], BF16) for n in range(2)]
                bimp = [T(es, "bimp%d" % n, [128, 256], F32) for n in range(2)]
                cmk = [T(es, "cmk%d" % n, [128, 2, 128], F32) for n in range(2)]
                rqT = [T(es, "rqT%d" % n, [128, 512], BF16) for n in range(2)]
                rkT = [T(es, "rkT%d" % n, [128, 512], BF16) for n in range(2)]
                rvt = [T(es, "rvt%d" % n, [128, 512], BF16) for n in range(2)]
                sgt = [T(es, "sgt%d" % n, [128, 512], F32) for n in range(2)]
                QA = [T(es, "QA%d" % n, [128, 4, 512], BF16) for n in range(2)]
                PT = [T(es, "PT%d" % n, [128, 512], BF16) for n in range(2)]
                oT = T(es, "oT", [65, 512], F32)
                den = T(es, "den", [128, 4], F32)
                coef = T(es, "coef", [128, 4], F32)
                nsa = T(es, "nsa", [128, 512], F32)
                nsab = T(es, "nsab", [128, 512], BF16)
                nsaTs = T(es, "nsaTs", [128, 512], BF16)
                iacc = T(es, "iacc", [128, 256], F32)
                itmp = T(es, "itmp", [128, 256], F32)
                m8a = T(es, "m8a", [128, 8], F32)
                m8b = T(es, "m8b", [128, 8], F32)
                thr = T(es, "thr", [128, 1], F32)
                nsp = T(es, "nsp", [128, 320], BF16)
                PTr = T(es, "PTr", [128, 512], BF16)
                rqx = T(es, "rqx", [128, 512], BF16)
                ss4 = T(es, "ss4", [128, 4], F32)
                rjunk = T(es, "rjunk", [128, 512], F32)
                retb = T(es, "retb", [128, 512], BF16)
                retTs = T(es, "retTs", [128, 512], BF16)
                pS = [P(es, "pS%d" % n, [128, 512], F32) for n in range(2)]
                pO = P(es, "pO", [128, 512], F32)
                pImp = P(es, "pImp", [128, 4, 256], F32)
                pTr = P(es, "pTr", [128, 512], F32)
                pRS = P(es, "pRS", [128, 512], F32)
                pRO = P(es, "pRO", [128, 512], F32)
                pTrb = pTr.bitcast(BF16) if hasattr(pTr, "bitcast") else None
                for n in range(2):
                    MEMSET("pool", vwa[n][:, :, :, 64:65], 1.0, ["vwa%d" % n], waw=False)
                MEMSET("pool", nsp[:], 0.0, ["nsp"])

                def p3_load(i):
                    pr = i % 2
                    s.dma("sp", qt[pr][:], QT[i], reads=["QT%d" % i], writes=["qt%d" % pr])
                    s.dma("sp", gn[pr][:], GN[i], reads=["GN%d" % i], writes=["gn%d" % pr])
                    s.dma("sp", kwt[pr][:].rearrange("p g t -> p (g t)"), KWT[i], reads=["KWT%d" % i], writes=["kwt%d" % pr])
                    s.dma("sp", vwa[pr][:, :, :, 0:64], VW[i].rearrange("p (j g d) -> p j g d", j=5, g=2),
                          reads=["VW%d" % i], writes=["vwa%d" % pr], waw=False)
                    s.dma("sp", bimp[pr][:], biasimp[i], writes=["bimp%d" % pr])
                    s.dma("sp", cmk[pr][:].rearrange("p a r -> p (a r)"), cmpmask[i], writes=["cmk%d" % pr])
                    s.dma("sp", rqT[pr][:], RQT[i], reads=["RQT%d" % i], writes=["rqT%d" % pr])
                    s.dma("sp", rkT[pr][:], RKT[i], reads=["RKT%d" % i], writes=["rkT%d" % pr])
                    s.dma("sp", rvt[pr][:], RV[i], reads=["RV%d" % i], writes=["rvt%d" % pr])
                    s.dma("sp", sgt[pr][:], SG[i], reads=["SG%d" % i], writes=["sgt%d" % pr])

                step = [0]

                def score_chunk(lhsT, rhs, R, mask, maskR, meng="dve"):
                    n = step[0] % 2
                    step[0] += 1
                    MM(pS[n][:], lhsT, rhs, True, True, R, ["pS%d" % n])
                    ACT(PT[n][:], pS[n][:], AF.Exp, ["pS%d" % n], ["PT%d" % n])
                    if mask is not None:
                        pv = PT[n][:].rearrange("p (h q) -> p h q", h=4)
                        TT(meng, pv, pv, bc(mask.unsqueeze(1), [128, 4, 128]), ALU.mult, ["PT%d" % n] + maskR, ["PT%d" % n])
                    return PT[n], "PT%d" % n

                def finish_branch(br, g, pr):
                    CP("act", oT[:], pO[0:65, :], ["pO"], ["oT"])
                    for h in range(4):
                        TR(pTr[:, h * 65:(h + 1) * 65], oT[0:65, h * 128:(h + 1) * 128], identf[0:65, 0:65], ["oT", "identf"], ["pTr"])
                    pv = pTr[:, 0:260].rearrange("p (h e) -> p h e", h=4)
                    TS("dve", den[:], pv[:, :, 64], 1e-30, None, ALU.max, None, ["pTr"], ["den"])
                    RECIP(den[:], den[:], ["den"], ["den"])
                    gv = gn[pr][:].rearrange("p (h b) -> p h b", h=8)
                    TT("dve", coef[:], den[:], gv[:, 4 * g:4 * g + 4, br], ALU.mult, ["den", "gn%d" % pr], ["coef"])
                    for h in range(4):
                        dst = nsa[:, (4 * g + h) * 64:(4 * g + h + 1) * 64]
                        if br == 0:
                            TS("dve", dst, pv[:, h, 0:64], coef[:, h:h + 1], None, ALU.mult, None, ["pTr", "coef"], ["nsa"], waw=False)
                        else:
                            STT("dve", dst, pv[:, h, 0:64], coef[:, h:h + 1], dst, ALU.mult, ALU.add, ["pTr", "coef", "nsa"], ["nsa"])

                p3_load(0)
                for i in range(CFG['p3_i']):
                    pr = i % 2
                    if i + 1 < CFG['p3_i']:
                        p3_load(i + 1)
                    nkc = 8 * i + 8
                    nv = (nkc - 1) // 32 + 1
                    ncm = i // 2 + 1
                    for g in range(2):
                        qa = QA[g]
                        kqa = "QA%d" % g
                        for v in range(nv):
                            CP("pool", qa[0:64, v, :], qt[pr][:, g * 512:(g + 1) * 512], ["qt%d" % pr], [kqa], waw=(v == 0))
                        pend = None
                        for m in range(ncm):
                            mask, maskR = None, []
                            if m >= ncm - 2:
                                mask, maskR = cmk[pr][:, m - (ncm - 2), :], ["cmk%d" % pr]
                            ptile, kpt = score_chunk(kcmpT[:, g, m * 128:(m + 1) * 128], qa[0:64, 0, :], ["kcmpT", kqa], mask, maskR)
                            if pend is not None:
                                pend()

                            def pend(m=m, ptile=ptile, kpt=kpt):
                                MM(pO[0:65, :], vcmpa[:, m, g, :], ptile[:], m == 0, m == ncm - 1, ["vcmpa", kpt], ["pO"])
                                for h in range(4):
                                    MM(pImp[:, h, :], ptile[:, h * 128:(h + 1) * 128], ov[:, m, :], m == 0 and h % 2 == 0, m == ncm - 1,
                                       [kpt, "ov"], ["pImp"], skip=True)
                        pend()
                        finish_branch(0, g, pr)
                        for h in range(4):
                            STT("dve", iacc[:], pImp[:, h, :], den[:, h:h + 1], bimp[pr][:] if h == 0 else iacc[:], ALU.mult, ALU.add,
                                ["pImp", "den", "bimp%d" % pr, "iacc"], ["iacc"])
                        s.op("dve", lambda e: e.max(out=m8a[:], in_=iacc[:]), ["iacc"], ["m8a"])
                        s.op("dve", lambda e: e.match_replace(out=itmp[:], in_to_replace=m8a[:], in_values=iacc[:], imm_value=-1e30),
                             ["iacc", "m8a"], ["itmp"])
                        s.op("dve", lambda e: e.max(out=m8b[:], in_=itmp[:]), ["itmp"], ["m8b"])
                        TS("dve", thr[:], m8b[:, 7:8], -64.0, None, ALU.max, None, ["m8b"], ["thr"])
                        TS("dve", nsp[:, 64:320], iacc[:], thr[:, 0:1], 1.0, ALU.is_ge, ALU.subtract, ["iacc", "thr"], ["nsp"])
                        wm = win0 if i == 0 else winS
                        pend = None
                        for j in range(5):
                            ptile, kpt = score_chunk(kwt[pr][:, g, j * 128:(j + 1) * 128], qa[0:64, 0, :], ["kwt%d" % pr, kqa],
                                                     wm[:, j, :], ["win0" if i == 0 else "winS"], meng="pool")
                            if pend is not None:
                                pend()

                            def pend(j=j, ptile=ptile, kpt=kpt):
                                MM(pO[0:65, :], vwa[pr][:, j, g, :], ptile[:], j == 0, j == 4, ["vwa%d" % pr, kpt], ["pO"])
                        pend()
                        for v in range(nv):
                            TR(pTrb[:, v * 128:(v + 1) * 128], nsp[:, 64 * v:64 * v + 128], identb[:], ["nsp", "identb"], ["pTr"])
                        for v in range(nv):
                            CP("act", qa[64:128, v, :].rearrange("p (h q) -> p h q", h=4),
                               bc(pTrb[64:128, v * 128:(v + 1) * 128].unsqueeze(1), [64, 4, 128]), ["pTr"], [kqa], waw=False)
                        finish_branch(2, g, pr)
                        pend = None
                        for kc in range(nkc):
                            mask, maskR = None, []
                            if kc >= nkc - 8:
                                mask, maskR = diag[:, kc - (nkc - 8), :], ["diag"]
                            ptile, kpt = score_chunk(KsTa[g][:, kc * 128:(kc + 1) * 128], qa[:, kc // 32, :], ["KsTa%d" % g, kqa], mask, maskR)
                            if pend is not None:
                                pend()

                            def pend(kc=kc, ptile=ptile, kpt=kpt):
                                MM(pO[0:65, :], Vsa[:, kc, g, :], ptile[:], kc == 0, kc == nkc - 1, ["Vsa", kpt], ["pO"])
                        pend()
                        finish_branch(1, g, pr)
                    CP("pool", nsab[:], nsa[:], ["nsa"], ["nsab"])
                    for k in range(4):
                        TR(pTrb[:, k * 128:(k + 1) * 128], nsab[:, k * 128:(k + 1) * 128], identb[:], ["nsab", "identb"], ["pTr"])
                    CP("act", nsaTs[:], pTrb[:, 0:512], ["pTr"], ["nsaTs"])
                    s.dma("sp", NSAT[i], nsaTs[:], reads=["nsaTs"], writes=["NSAT%d" % i])
                    for h in range(4):
                        MM(pRS[:, h * 128:(h + 1) * 128], rkT[pr][:, h * 128:(h + 1) * 128], rqT[pr][:, h * 128:(h + 1) * 128],
                           True, True, ["rkT%d" % pr, "rqT%d" % pr], ["pRS"])
                    TT("dve", PTr[:], pRS[:], dmatT[:], ALU.mult, ["pRS", "dmatT"], ["PTr"])
                    TT("pool", rqx[:], rqT[pr][:], xiT[:], ALU.mult, ["rqT%d" % pr, "xiT"], ["rqx"])
                    for h in range(4):
                        hs = slice(h * 128, (h + 1) * 128)
                        MM(pRO[:, hs], PTr[:, hs], rvt[pr][:, hs], True, False, ["PTr", "rvt%d" % pr], ["pRO"], skip=True)
                        MM(pRO[:, hs], rqx[:, hs], Sown[:, i, hs], False, True, ["rqx", "Sown"], ["pRO"], skip=True)
                    for h in range(4):
                        ACT(rjunk[:, h * 128:(h + 1) * 128], pRO[:, h * 128:(h + 1) * 128], AF.Square, ["pRO"], ["rjunk", "ss4"], waw=(h == 0),
                            accum_out=ss4[:, h:h + 1])
                    ACT(ss4[:], ss4[:], AF.Ln, ["ss4", "epsb"], ["ss4"], scale=1.0 / 128, bias=epsb[:])
                    ACT(ss4[:], ss4[:], AF.Exp, ["ss4"], ["ss4"], scale=-0.5)
                    for h in range(4):
                        hs = slice(h * 128, (h + 1) * 128)
                        STT("dve", retb[:, hs], pRO[:, hs], ss4[:, h:h + 1], sgt[pr][:, hs], ALU.mult, ALU.mult,
                            ["pRO", "ss4", "sgt%d" % pr], ["retb"], waw=False)
                    for k in range(4):
                        TR(pTrb[:, k * 128:(k + 1) * 128], retb[:, k * 128:(k + 1) * 128], identb[:], ["retb", "identb"], ["pTr"])
                    CP("act", retTs[:], pTrb[:, 0:512], ["pTr"], ["retTs"])
                    s.dma("sp", RETT[i], retTs[:], reads=["retTs"], writes=["RETT%d" % i])

        if CFG['stop'] < 5:
            raise _Stop()
        s.next_phase()
        with ExitStack() as es:
            Wga = T(es, "Wga", [128, 8, 2048], BF16)
            Wpa = T(es, "Wpa", [128, 4, 1024], BF16)
            Wpb = T(es, "Wpb", [128, 4, 1024], BF16)
            Wo = T(es, "Wo", [128, 8, 1024], BF16)
            grep = T(es, "grep4", [128, D], F32)
            s.dma("sp", grep[:], gmlp, writes=["grep4"])
            wload(Wga, w_in[:, C_GA:C_GA + 2048], "Wga")
            wload(Wpa, wpa, "Wpa")
            wload(Wpb, wpb, "Wpb")
            wload(Wo, wout, "Wo")
            hTg = [T(es, "hTg%d" % n, [128, 8, 512], BF16) for n in range(2)]
            nsT = [T(es, "nsT%d" % n, [128, 4, 512], BF16) for n in range(2)]
            reT = [T(es, "reT%d" % n, [128, 4, 512], BF16) for n in range(2)]
            sig = [T(es, "sig%d" % n, [128, 512], F32) for n in range(2)]
            mxa = T(es, "mxa", [128, 512], F32)
            mixT = T(es, "mixT", [128, 8, 512], BF16)
            xo_t = [T(es, "xo_t%d" % n, [128, D], F32) for n in range(2)]
            xm = [T(es, "xm%d" % n, [128, D], F32) for n in range(2)]
            junk = T(es, "junk4", [128, D], F32)
            ss = T(es, "ss4_", [128, 1], F32)
            rstd = T(es, "rstd4", [128, 1], F32)
            h2b = T(es, "h2b", [128, D], BF16)
            h2T = [T(es, "h2Ts%d" % n, [128, 1024], BF16) for n in range(2)]
            pG = [P(es, "pG%d" % n, [128, 512], F32) for n in range(2)]
            pPr = [P(es, "pPr%d" % n, [128, 512], F32) for n in range(2)]
            pX = P(es, "pX", [128, 2, 512], F32)
            pT = P(es, "pT4", [128, 1024], BF16)

            def p4_load(G):
                pr = G % 2
                for u in range(4):
                    i = 4 * G + u
                    s.dma("sp", hTg[pr][:, :, u * 128:(u + 1) * 128], HT[i].rearrange("p (k t) -> p k t", k=8),
                          reads=["HT%d" % i], writes=["hTg%d" % pr], waw=False)
                    s.dma("sp", nsT[pr][:, :, u * 128:(u + 1) * 128], NSAT[i].rearrange("p (k t) -> p k t", k=4),
                          reads=["NSAT%d" % i], writes=["nsT%d" % pr], waw=False)
                    s.dma("sp", reT[pr][:, :, u * 128:(u + 1) * 128], RETT[i].rearrange("p (k t) -> p k t", k=4),
                          reads=["RETT%d" % i], writes=["reT%d" % pr], waw=False)

            p4_load(0)
            for G in range(CFG.get('p4_g', 4)):
                pr = G % 2
                if G + 1 < CFG.get('p4_g', 4):
                    p4_load(G + 1)
                for oc in range(8):
                    for ab, (Wp, srcT, ksrc) in enumerate(((Wpa, nsT, "nsT"), (Wpb, reT, "reT"))):
                        c0 = ab * 1024 + oc * 128
                        for k in range(8):
                            MM(pG[ab][:], Wga[:, k, c0:c0 + 128], hTg[pr][:, k, :], k == 0, k == 7, ["Wga", "hTg%d" % pr], ["pG%d" % ab])
                        for k in range(4):
                            MM(pPr[ab][:], Wp[:, k, oc * 128:(oc + 1) * 128], srcT[pr][:, k, :], k == 0, k == 3,
                               ["Wpa", "Wpb", "%s%d" % (ksrc, pr)], ["pPr%d" % ab])
                        ACT(sig[ab][:], pG[ab][:], AF.Exp, ["pG%d" % ab], ["sig%d" % ab], scale=-1.0)
                        TS("dve", sig[ab][:], sig[ab][:], 1.0, None, ALU.add, None, ["sig%d" % ab], ["sig%d" % ab])
                        RECIP(sig[ab][:], sig[ab][:], ["sig%d" % ab], ["sig%d" % ab])
                    TT("dve", mxa[:], pPr[0][:], sig[0][:], ALU.mult, ["pPr0", "sig0"], ["mxa"])
                    TT("dve", sig[1][:], pPr[1][:], sig[1][:], ALU.mult, ["pPr1", "sig1"], ["sig1"])
                    TT("pool", mixT[:, oc, :], mxa[:], sig[1][:], ALU.add, ["mxa", "sig1"], ["mixT"], waw=False)
                for u in range(4):
                    i = 4 * G + u
                    p2_ = i % 2
                    s.dma("sp", xo_t[p2_][:], xo[(5 * i + 4) * 128:(5 * i + 5) * 128, :], writes=["xo_t%d" % p2_])
                    for hf in range(2):
                        for oc in range(8):
                            MM(pX[:, hf, :], mixT[:, oc, u * 128:(u + 1) * 128], Wo[:, oc, hf * 512:(hf + 1) * 512], oc == 0, oc == 7,
                               ["mixT", "Wo"], ["pX%d" % hf])
                    TT("dve", xm[p2_][:], pX[:].rearrange("p a c -> p (a c)"), xo_t[p2_][:], ALU.add, ["pX0", "pX1", "xo_t%d" % p2_], ["xm%d" % p2_])
                    s.dma("sp", XMID[i], xm[p2_][:], reads=["xm%d" % p2_], writes=["XMID%d" % i])
                    rmsnorm((junk, ss, rstd), xm[p2_][:], "xm%d" % p2_, grep[:], "grep4", h2b[:], "h2b", "4")
                    for k in range(8):
                        TR(pT[:, k * 128:(k + 1) * 128], h2b[:, k * 128:(k + 1) * 128], identb[:], ["h2b", "identb"], ["pT4"])
                    CP("act", h2T[p2_][:], pT[:], ["pT4"], ["h2Ts%d" % p2_])
                    s.dma("sp", H2T[i], h2T[p2_][:], reads=["h2Ts%d" % p2_], writes=["H2T%d" % i])

        if CFG['stop'] < 6:
            raise _Stop()
        s.next_phase()
        with ExitStack() as es:
            Wu = T(es, "Wu", [128, 8, 4096], BF16)
            Wd = T(es, "Wd", [128, 32, 1024], BF16)
            grep = T(es, "grep5", [128, D], F32)
            s.dma("sp", grep[:], gfin, writes=["grep5"])
            for q4 in range(4):
                for k in range(8):
                    s.dma("pool", Wu[:, k, q4 * 1024:(q4 + 1) * 1024], wup[k * 128:(k + 1) * 128, q4 * 1024:(q4 + 1) * 1024],
                          writes=["Wu%d" % q4], waw=False)
                for hc in range(q4 * 8, q4 * 8 + 8):
                    s.dma("pool", Wd[:, hc, :], wdown[hc * 128:(hc + 1) * 128, :], writes=["Wd%d" % q4], waw=False)
            h2g = [T(es, "h2g%d" % n, [128, 8, 256], BF16) for n in range(2)]
            ur = [T(es, "ur%d" % n, [128, 256], F32) for n in range(2)]
            uT = [T(es, "uT%d" % n, [128, 256], BF16) for n in range(2)]
            xmr = [T(es, "xmr%d" % n, [128, D], F32) for n in range(2)]
            xf = [T(es, "xf%d" % n, [128, D], F32) for n in range(2)]
            junk = T(es, "junk5", [128, D], F32)
            ss = T(es, "ss5", [128, 1], F32)
            rstd = T(es, "rstd5", [128, 1], F32)
            pU = [P(es, "pU%d" % n, [128, 512], F32) for n in range(2)]
            pD = P(es, "pD", [128, 4, 512], F32)

            def p5_load(G):
                pr = G % 2
                for u in range(2):
                    i = 2 * G + u
                    s.dma("sp", h2g[pr][:, :, u * 128:(u + 1) * 128], H2T[i].rearrange("p (k t) -> p k t", k=8),
                          reads=["H2T%d" % i], writes=["h2g%d" % pr], waw=False)

            p5_load(0)
            for G in range(CFG.get('p5_g', 8)):
                pr = G % 2
                if G + 1 < CFG.get('p5_g', 8):
                    p5_load(G + 1)
                for hc in range(32):
                    n = hc % 2
                    for k in range(8):
                        MM(pU[n][:, 0:256], Wu[:, k, hc * 128:(hc + 1) * 128], h2g[pr][:, k, :], k == 0, k == 7, ["Wu%d" % (hc // 8), "h2g%d" % pr], ["pU%d" % n])
                    ACT(ur[n][:], pU[n][:, 0:256], AF.Relu, ["pU%d" % n], ["ur%d" % n])
                    TT("dve", uT[n][:], ur[n][:], ur[n][:], ALU.mult, ["ur%d" % n], ["uT%d" % n])
                    for u in range(2):
                        for hf in range(2):
                            MM(pD[:, u * 2 + hf, :], uT[n][:, u * 128:(u + 1) * 128], Wd[:, hc, hf * 512:(hf + 1) * 512], hc == 0, hc == 31,
                               ["uT%d" % n, "Wd%d" % (hc // 8)], ["pD%d" % (u * 2 + hf)])
                for u in range(2):
                    i = 2 * G + u
                    s.dma("sp", xmr[u][:], XMID[i], reads=["XMID%d" % i], writes=["xmr%d" % u])
                    TT("dve", xf[u][:], pD[:, 2 * u:2 * u + 2, :].rearrange("p a c -> p (a c)"), xmr[u][:], ALU.add,
                       ["pD%d" % (2 * u), "pD%d" % (2 * u + 1), "xmr%d" % u], ["xf%d" % u])
                    rmsnorm((junk, ss, rstd), xf[u][:], "xf%d" % u, grep[:], "grep5", xmr[u][:], "xmr%d" % u, "5")
                    s.dma("sp", y[i], xmr[u][:], reads=["xmr%d" % u], writes=["Y"], waw=False)
            s.op("sp", None, reads=["Y"])

    except _Stop:
        pass
    s.op("sp", None, reads=["Y"])
    s.emit()
    try:
        top.close()
    except AssertionError:
        pass
    return nc


def _const_tables():
    n = np.arange(128, dtype=np.float64)
    zeta = np.stack([np.exp(LOGG[h] * (127.0 - n)) for h in range(4)], axis=1) * (128.0 ** -0.5)
    xi = np.stack([np.exp(LOGG[h] * (n + 1.0)) for h in range(4)], axis=0)
    xiT = np.broadcast_to(xi.reshape(1, 512), (128, 512))
    rel = n[None, :] - n[:, None]
    dmatT = np.stack([np.where(rel >= 0, np.exp(LOGG[h] * np.maximum(rel, 0.0)), 0.0) for h in range(4)], axis=1)
    dmatT = dmatT.reshape(128, 512) * (128.0 ** -0.5)
    c = np.arange(1024)
    j = np.arange(256)
    ovl = np.clip(np.minimum(c[:, None] * 16 + 32, j[None, :] * 64 + 64) - np.maximum(c[:, None] * 16, j[None, :] * 64), 0, None) / 32.0
    ov = ovl.reshape(8, 128, 256).transpose(1, 0, 2).reshape(128, 8 * 256)
    key = np.arange(4096)
    apat = ((key[None, :] // 64) % 64 == np.arange(64)[:, None]).astype(np.float64) * NEGBIG
    half_n = np.exp((-math.log(500000.0) * np.arange(8, dtype=np.float32) * 2.0 / 16).astype(np.float32))
    half_r = np.exp((-math.log(10000.0) * np.arange(64, dtype=np.float32) * 2.0 / 128).astype(np.float32))
    inv = np.broadcast_to(np.concatenate([half_n, half_r])[None, :], (128, 72))
    f = lambda a: np.ascontiguousarray(a, dtype=np.float32)
    kk = np.arange(128)[:, None]
    r = np.arange(128)[None, :]
    tri = (kk <= r).astype(np.float32)
    anti = (kk > r).astype(np.float32)
    ones = np.ones((128, 128), np.float32)
    winS = np.stack([anti, ones, ones, ones, tri], axis=1).reshape(128, 640)
    return dict(zeta=f(zeta), xiT=f(xiT), dmatT=f(dmatT), ov=f(ov), apat=f(apat), inv=f(inv),
                ident=f(np.eye(128)), winmaskS=f(winS)), tri, anti, ones


def _core_tables(c, tri, anti, ones):
    zeros = np.zeros((128, 128), np.float32)
    q = np.arange(128)
    jj = np.arange(256)
    biasimp = np.zeros((16, 128, 256), np.float32)
    cmpmask = np.zeros((16, 128, 256), np.float32)
    for i in range(16):
        B = 8 * i + c
        cur = 2 * B + (q >= 64).astype(np.int64)
        b = np.zeros((128, 256), np.float32)
        b += 128.0 * ((jj[None, :] >= cur[:, None] - 1) & (jj[None, :] <= cur[:, None]))
        b -= 256.0 * (jj[None, :] > cur[:, None])
        b[:, 0] += 128.0
        biasimp[i] = b
        ncm = i // 2 + 1
        for a in range(2):
            m = ncm - 2 + a
            if m < 0:
                continue
            ci = 128 * m + np.arange(128)
            valid = ((16 * ci[:, None] + 31) <= (128 * B + q[None, :])) & (ci[:, None] <= 1022)
            cmpmask[i, :, a * 128:(a + 1) * 128] = valid
    diag = np.stack([ones if m < c else (tri if m == c else zeros) for m in range(8)], axis=1).reshape(128, 1024)
    w0 = []
    for j in range(5):
        if c - 4 + j < 0:
            w0.append(zeros)
        else:
            w0.append(anti if j == 0 else (tri if j == 4 else ones))
    win0 = np.stack(w0, axis=1).reshape(128, 640)
    oh = np.zeros((128, 8), np.float32)
    oh[:, c] = 1.0
    return dict(biasimp=biasimp, cmpmask=cmpmask, diagmask=np.ascontiguousarray(diag, dtype=np.float32),
                winmask0=np.ascontiguousarray(win0, dtype=np.float32), oh=oh)


_NC_CACHE = {}


def _prep(x, positions, norm_mix, w_in, cmp_pos_k, cmp_pos_v, cmp_k_w1, cmp_k_w2, cmp_v_w1, cmp_v_w2,
          w_proj_a, w_proj_b, w_out, norm_mlp, w_up, w_down, norm_final, cores=range(NCORE)):
    f = lambda a: np.ascontiguousarray(np.asarray(a), dtype=np.float32)
    x2 = f(x).reshape(SEQ, D)
    pos = np.asarray(positions).reshape(SEQ).astype(np.int32)
    consts, tri, anti, ones = _const_tables()
    shared = dict(
        xg=x2, w_in=f(w_in)[0], ckw1=f(cmp_k_w1)[0], ckw2=f(cmp_k_w2)[0], cvw1=f(cmp_v_w1)[0], cvw2=f(cmp_v_w2)[0],
        pekT=np.ascontiguousarray(f(cmp_pos_k)[0].T), pevT=np.ascontiguousarray(f(cmp_pos_v)[0].T),
        wpa=f(w_proj_a)[0], wpb=f(w_proj_b)[0], wout=f(w_out)[0], wup=f(w_up)[0], wdown=f(w_down)[0],
        gmix=np.ascontiguousarray(np.broadcast_to(f(norm_mix)[0][None, :], (128, D))),
        gmlp=np.ascontiguousarray(np.broadcast_to(f(norm_mlp)[0][None, :], (128, D))),
        gfin=np.ascontiguousarray(np.broadcast_to(f(norm_final)[None, :], (128, D))),
        **consts)
    posg = pos.reshape(128, 128).T
    xpad = np.concatenate([np.zeros((512, D), np.float32), x2], axis=0)
    ppad = np.concatenate([np.zeros((512,), np.int32), pos], axis=0)
    in_maps = []
    for c in cores:
        rows = np.concatenate([np.arange(128 * (8 * i + c), 128 * (8 * i + c) + 640) for i in range(16)])
        xo = np.ascontiguousarray(xpad[rows])
        poso = ppad[rows].reshape(80, 128).T
        m = dict(shared)
        m["xo"] = xo
        m["posall"] = np.ascontiguousarray(np.concatenate([posg, poso], axis=1), dtype=np.int32)
        m.update(_core_tables(c, tri, anti, ones))
        in_maps.append(m)
    return in_maps


def kernel(**inputs):
    in_maps = _prep(**inputs)
    if "nc" not in _NC_CACHE:
        _NC_CACHE["nc"] = build_nc()
    res = run_bass_kernel_spmd(_NC_CACHE["nc"], in_maps, core_ids=list(range(NCORE)))
    out = np.zeros((SEQ, D), np.float32)
    for c in range(NCORE):
        yc = np.asarray(res.results[c]["y"]).reshape(16, 128, D)
        for i in range(16):
            B = 8 * i + c
            out[128 * B:128 * B + 128] = yc[i]
    return out.reshape(1, SEQ, D)
```
